# Optimizing a Trainium2 kernel written in Bass

```python
import math
import jax, jax.numpy as jnp
from jax import lax
import numpy as np

D_MODEL = 2048
BATCH = 2
SEQ = 8192
DEPTH = 2

GRID_W = 64
CTX_LEN = 256
BLOCK_Q = 128
EPS = 1e-6
ROPE_THETA = 10000.0

A_HEADS = 8
A_KV_HEADS = 2
A_HEAD_DIM = 128
A_Q_W = A_HEADS * A_HEAD_DIM
A_KV_W = A_KV_HEADS * A_HEAD_DIM
A_QKV_W = A_Q_W + 2 * A_KV_W
HY_CH = D_MODEL // 2
HY_ORDER = 2
HY_FILTER_W = 64
HY_BANDS = 16
HY_EMB = 1 + 2 * HY_BANDS
HY_FAST = 0.3
HY_SLOW = 1.5
HY_TARGET = 1e-2
AB_IN_W = A_QKV_W + (HY_ORDER + 1) * HY_CH
AB_MIX_W = A_Q_W + HY_CH
C_HEADS = 16
C_HEAD_DIM = 64
C_Q_W = C_HEADS * 2 * C_HEAD_DIM
C_V_W = C_HEADS * 2 * C_HEAD_DIM
C_IN_W = 2 * C_Q_W + C_V_W
C_MIX_W = C_V_W
D_FF = 5632
N_EVEN = (DEPTH + 1) // 2
N_ODD = DEPTH // 2
F32 = jnp.float32

kernel_name = 'hybrid_dit_gqa_hyena_diffattn_prefix'


def rmsnorm(x, g):
    xf = x.astype(F32)
    y = xf * lax.rsqrt(jnp.mean(xf * xf, axis=-1, keepdims=True) + EPS)
    return (y * g.astype(F32)).astype(x.dtype)


def dwconv3(x, w, b):
    xp = jnp.pad(x, ((0, 0), (1, 1), (0, 0)))
    return xp[:, :-2] * w[0] + xp[:, 1:-1] * w[1] + xp[:, 2:] * w[2] + b


def axial_rope(n_tok, rot_dim):
    rows = n_tok // GRID_W
    row = jnp.broadcast_to(jnp.arange(rows, dtype=jnp.int32)[:, None], (rows, GRID_W)).reshape(n_tok)
    col = jnp.broadcast_to(jnp.arange(GRID_W, dtype=jnp.int32)[None, :], (rows, GRID_W)).reshape(n_tok)
    axis_dim = rot_dim // 2
    inv_freq = ROPE_THETA ** (-jnp.arange(0, axis_dim, 2, dtype=F32) / axis_dim)
    ang = jnp.concatenate([row.astype(F32)[:, None] * inv_freq,
                           col.astype(F32)[:, None] * inv_freq], axis=-1)
    return jnp.cos(ang), jnp.sin(ang)


def apply_rope(x, cos, sin):
    bshape = (1, cos.shape[0]) + (1,) * (x.ndim - 3) + (cos.shape[1],)
    co = cos.reshape(bshape).astype(x.dtype)
    si = sin.reshape(bshape).astype(x.dtype)
    x1 = x[..., 0::2]
    x2 = x[..., 1::2]
    return jnp.stack([x1 * co - x2 * si, x1 * si + x2 * co], axis=-1).reshape(x.shape)


def sweep_query_blocks(attend, q):
    b, n = q.shape[:2]
    nb = n // BLOCK_Q
    qb = jnp.moveaxis(q.reshape((b, nb, BLOCK_Q) + q.shape[2:]), 1, 0)
    ob = lax.map(attend, qb)
    return jnp.moveaxis(ob, 0, 1).reshape(b, n, ob.shape[-1])


def gqa_attend(q, k, v):
    b, tq = q.shape[:2]
    qg = q.reshape(b, tq, A_KV_HEADS, A_HEADS // A_KV_HEADS, A_HEAD_DIM)
    s = jnp.einsum('bqkgd,bskd->bkgqs', qg, k).astype(F32) * (A_HEAD_DIM ** -0.5)
    p = jax.nn.softmax(s, axis=-1).astype(v.dtype)
    o = jnp.einsum('bkgqs,bskd->bqkgd', p, v)
    return o.reshape(b, tq, A_Q_W)


def diff_attend(q, k, v, lam, subln_g, lambda_init):
    b, tq = q.shape[:2]
    s = jnp.einsum('bqhmd,bshmd->bhmqs', q, k).astype(F32) * (C_HEAD_DIM ** -0.5)
    p = jax.nn.softmax(s, axis=-1)
    p_diff = (p[:, :, 0] - lam * p[:, :, 1]).astype(v.dtype)
    o = jnp.einsum('bhqs,bshe->bqhe', p_diff, v)
    o = rmsnorm(o, subln_g) * (1.0 - lambda_init)
    return o.reshape(b, tq, C_MIX_W)


def hyena_filters(n_tok, w1, b1, w2, b2, w3, freq):
    t = jnp.arange(n_tok, dtype=F32)
    t_norm = t / max(n_tok - 1, 1)
    w = (2.0 * math.pi / n_tok) * t
    bands = jnp.linspace(1e-4, HY_BANDS - 1, HY_BANDS, dtype=F32)
    z = w[:, None] * bands
    feats = jnp.concatenate([t_norm[:, None], jnp.cos(z), -jnp.sin(z)], axis=-1)
    h = jnp.sin(freq[0] * (feats @ w1 + b1))
    h = jnp.sin(freq[1] * (h @ w2 + b2))
    h = (h @ w3).astype(F32).reshape(n_tok, HY_ORDER, 2, HY_CH)
    deltas = jnp.abs(jnp.linspace(math.log(HY_TARGET) / HY_FAST, math.log(HY_TARGET) / HY_SLOW, HY_CH, dtype=F32))
    decay = jnp.exp(-t_norm[:, None] * deltas)
    h = h * decay[:, None, None, :]
    filt = jnp.concatenate([h[:, :, 0], jnp.zeros((1, HY_ORDER, HY_CH), F32), h[:0:-1, :, 1]], axis=0)
    filt = filt / jnp.sum(jnp.abs(filt), axis=0, keepdims=True)
    return jnp.fft.rfft(filt, axis=0)


def long_conv(u, filt_f, skip):
    n = u.shape[1]
    uf = jnp.fft.rfft(u.astype(F32), n=2 * n, axis=1)
    y = jnp.fft.irfft(uf * filt_f, n=2 * n, axis=1)[:, :n]
    return (y + u.astype(F32) * skip.astype(F32)).astype(u.dtype)


def hyena(p, conv_w, conv_b, w1, b1, w2, b2, w3, freq, skip):
    n = p.shape[1]
    p = dwconv3(p, conv_w, conv_b)
    v, x1, x2 = jnp.split(p, 3, axis=-1)
    filt_f = hyena_filters(n, w1, b1, w2, b2, w3, freq)
    z = x1 * long_conv(v, filt_f[:, 0], skip[0])
    z = x2 * long_conv(z, filt_f[:, 1], skip[1])
    return z


def mixer_ab(h_lat, h_ctx, with_ctx, w_in, w_out, qk_g, hy_params):
    b, n, _ = h_lat.shape

    def heads(t, nh):
        return t.reshape(t.shape[:2] + (nh, A_HEAD_DIM))

    kv_c = h_ctx @ w_in[:, A_Q_W:A_QKV_W]
    k_c = rmsnorm(heads(kv_c[..., :A_KV_W], A_KV_HEADS), qk_g[1])
    v_c = heads(kv_c[..., A_KV_W:], A_KV_HEADS)
    p = h_lat @ w_in
    cos, sin = axial_rope(n, A_HEAD_DIM)
    q = apply_rope(rmsnorm(heads(p[..., :A_Q_W], A_HEADS), qk_g[0]), cos, sin)
    k = apply_rope(rmsnorm(heads(p[..., A_Q_W:A_Q_W + A_KV_W], A_KV_HEADS), qk_g[1]), cos, sin)
    v = heads(p[..., A_Q_W + A_KV_W:A_QKV_W], A_KV_HEADS)
    k_all = jnp.concatenate([k, k_c], axis=1)
    v_all = jnp.concatenate([v, v_c], axis=1)
    o_att = sweep_query_blocks(lambda qb: gqa_attend(qb, k_all, v_all), q)
    o_hy = hyena(p[..., A_QKV_W:], *hy_params)
    y_lat = jnp.concatenate([o_att, o_hy], axis=-1) @ w_out
    if not with_ctx:
        return y_lat, None
    q_c = rmsnorm(heads(h_ctx @ w_in[:, :A_Q_W], A_HEADS), qk_g[0])
    o_att_c = gqa_attend(q_c, k_c, v_c)
    o_hy_c = hyena(h_ctx @ w_in[:, A_QKV_W:], *hy_params)
    y_ctx = jnp.concatenate([o_att_c, o_hy_c], axis=-1) @ w_out
    return y_lat, y_ctx


def mixer_c(h_lat, h_ctx, with_ctx, lambda_init, w_in, w_out, lam_vecs, subln_g):
    b, n, _ = h_lat.shape

    def qk_heads(t):
        return t.reshape(t.shape[:2] + (C_HEADS, 2, C_HEAD_DIM))

    def v_heads(t):
        return t.reshape(t.shape[:2] + (C_HEADS, 2 * C_HEAD_DIM))

    lv = lam_vecs.astype(F32)
    lam = jnp.exp(jnp.sum(lv[0] * lv[1])) - jnp.exp(jnp.sum(lv[2] * lv[3])) + lambda_init
    kv_c = h_ctx @ w_in[:, C_Q_W:]
    k_c = qk_heads(kv_c[..., :C_Q_W])
    v_c = v_heads(kv_c[..., C_Q_W:])
    p = h_lat @ w_in
    cos, sin = axial_rope(n, C_HEAD_DIM)
    q = apply_rope(qk_heads(p[..., :C_Q_W]), cos, sin)
    k = apply_rope(qk_heads(p[..., C_Q_W:2 * C_Q_W]), cos, sin)
    v = v_heads(p[..., 2 * C_Q_W:])
    k_all = jnp.concatenate([k, k_c], axis=1)
    v_all = jnp.concatenate([v, v_c], axis=1)
    o = sweep_query_blocks(lambda qb: diff_attend(qb, k_all, v_all, lam, subln_g, lambda_init), q)
    y_lat = o @ w_out
    if not with_ctx:
        return y_lat, None
    q_c = qk_heads(h_ctx @ w_in[:, :C_Q_W])
    y_ctx = diff_attend(q_c, k_c, v_c, lam, subln_g, lambda_init) @ w_out
    return y_lat, y_ctx


def conv_ffn(h, w_up, conv_w, conv_b, w_down):
    u = dwconv3(h @ w_up, conv_w, conv_b)
    a, g = jnp.split(u, 2, axis=-1)
    return (a * jax.nn.silu(g)) @ w_down


def setup_inputs(seed: int = 0) -> dict:
    key = jax.random.key(seed)
    ks = jax.random.split(key, 27)

    def nrm(k, shape, scale=1.0):
        return jax.random.normal(k, shape, F32) * scale

    def gain(k, shape):
        return 1.0 + 0.1 * jax.random.normal(k, shape, F32)

    return {
        'x': nrm(ks[0], (BATCH, SEQ, D_MODEL)),
        'c': nrm(ks[1], (BATCH, D_MODEL)),
        'ctx': nrm(ks[2], (BATCH, CTX_LEN, D_MODEL)),
        'c_ctx': nrm(ks[3], (D_MODEL,)),
        'ada_w': nrm(ks[4], (DEPTH, D_MODEL, 6 * D_MODEL), D_MODEL ** -0.5),
        'ada_b': nrm(ks[5], (DEPTH, 6 * D_MODEL), 0.01),
        'norm_g': gain(ks[6], (DEPTH, 4, D_MODEL)),
        'ab_w_in': nrm(ks[7], (N_EVEN, D_MODEL, AB_IN_W), D_MODEL ** -0.5),
        'ab_w_out': nrm(ks[8], (N_EVEN, AB_MIX_W, D_MODEL), AB_MIX_W ** -0.5),
        'ab_qk_g': gain(ks[9], (N_EVEN, 2, A_HEAD_DIM)),
        'hy_conv_w': nrm(ks[10], (N_EVEN, 3, 3 * HY_CH), 3 ** -0.5),
        'hy_conv_b': nrm(ks[11], (N_EVEN, 3 * HY_CH), 0.01),
        'hy_w1': nrm(ks[12], (N_EVEN, HY_EMB, HY_FILTER_W), HY_EMB ** -0.5),
        'hy_b1': nrm(ks[13], (N_EVEN, HY_FILTER_W), 0.1),
        'hy_w2': nrm(ks[14], (N_EVEN, HY_FILTER_W, HY_FILTER_W), HY_FILTER_W ** -0.5),
        'hy_b2': nrm(ks[15], (N_EVEN, HY_FILTER_W), 0.1),
        'hy_w3': nrm(ks[16], (N_EVEN, HY_FILTER_W, HY_ORDER * 2 * HY_CH), HY_FILTER_W ** -0.5),
        'hy_freq': gain(ks[17], (N_EVEN, 2, HY_FILTER_W)),
        'hy_skip': nrm(ks[18], (N_EVEN, HY_ORDER, HY_CH)),
        'dc_w_in': nrm(ks[19], (N_ODD, D_MODEL, C_IN_W), D_MODEL ** -0.5),
        'dc_w_out': nrm(ks[20], (N_ODD, C_MIX_W, D_MODEL), C_MIX_W ** -0.5),
        'dc_lambda': nrm(ks[21], (N_ODD, 4, C_HEAD_DIM), 0.1),
        'dc_subln_g': gain(ks[22], (N_ODD, 2 * C_HEAD_DIM)),
        'ffn_w_up': nrm(ks[23], (DEPTH, D_MODEL, 2 * D_FF), D_MODEL ** -0.5),
        'ffn_conv_w': nrm(ks[24], (DEPTH, 3, 2 * D_FF), 3 ** -0.5),
        'ffn_conv_b': nrm(ks[25], (DEPTH, 2 * D_FF), 0.01),
        'ffn_w_down': nrm(ks[26], (DEPTH, D_FF, D_MODEL), D_FF ** -0.5),
    }


def reference(x, c, ctx, c_ctx, ada_w, ada_b, norm_g, ab_w_in, ab_w_out, ab_qk_g,
              hy_conv_w, hy_conv_b, hy_w1, hy_b1, hy_w2, hy_b2, hy_w3, hy_freq, hy_skip,
              dc_w_in, dc_w_out, dc_lambda, dc_subln_g,
              ffn_w_up, ffn_conv_w, ffn_conv_b, ffn_w_down):
    for i in range(DEPTH):
        j = i // 2
        with_ctx = i < DEPTH - 1
        mod = jax.nn.silu(c) @ ada_w[i] + ada_b[i]
        mod_c = jax.nn.silu(c_ctx) @ ada_w[i] + ada_b[i]
        sh1, sc1, g1, sh2, sc2, g2 = jnp.split(mod[:, None, :], 6, axis=-1)
        sh1c, sc1c, g1c, sh2c, sc2c, g2c = jnp.split(mod_c, 6, axis=-1)
        h = rmsnorm(x, norm_g[i, 0]) * (1.0 + sc1) + sh1
        hc = rmsnorm(ctx, norm_g[i, 0]) * (1.0 + sc1c) + sh1c
        if i % 2 == 0:
            hy_params = (hy_conv_w[j], hy_conv_b[j], hy_w1[j], hy_b1[j], hy_w2[j], hy_b2[j],
                         hy_w3[j], hy_freq[j], hy_skip[j])
            y, yc = mixer_ab(h, hc, with_ctx, ab_w_in[j], ab_w_out[j], ab_qk_g[j], hy_params)
        else:
            lambda_init = 0.8 - 0.6 * math.exp(-0.3 * i)
            y, yc = mixer_c(h, hc, with_ctx, lambda_init, dc_w_in[j], dc_w_out[j], dc_lambda[j], dc_subln_g[j])
        x = x + g1 * rmsnorm(y, norm_g[i, 1])
        h = rmsnorm(x, norm_g[i, 2]) * (1.0 + sc2) + sh2
        x = x + g2 * rmsnorm(conv_ffn(h, ffn_w_up[i], ffn_conv_w[i], ffn_conv_b[i], ffn_w_down[i]), norm_g[i, 3])
        if with_ctx:
            ctx = ctx + g1c * rmsnorm(yc, norm_g[i, 1])
            hc = rmsnorm(ctx, norm_g[i, 2]) * (1.0 + sc2c) + sh2c
            ctx = ctx + g2c * rmsnorm(conv_ffn(hc, ffn_w_up[i], ffn_conv_w[i], ffn_conv_b[i], ffn_w_down[i]), norm_g[i, 3])
    return x
```

```python
import contextlib
import numpy as np
import ml_dtypes
import concourse.bass as bass
import concourse.mybir as mybir
from concourse.bass_utils import run_bass_kernel_spmd

F32 = mybir.dt.float32
BF16 = mybir.dt.bfloat16
U8 = mybir.dt.uint8
AF = mybir.ActivationFunctionType
ALU = mybir.AluOpType
AX = mybir.AxisListType
NPBF = ml_dtypes.bfloat16

PE, ACT, DVE, POOL, SP = "pe", "act", "dve", "pool", "sp"
N_DMA_SEMS = 24
ARENA = 204800


def _esize(dt):
    return int(mybir.dt.size(dt))


def _box(ap):
    t = ap.tensor
    es = _esize(ap.dtype)
    C = 1
    for s in list(t.shape)[1:]:
        C *= int(s)
    off = int(ap.offset)
    r0 = off // C
    c0 = off % C
    rext = 0
    cext = 0
    for (step, cnt) in ap.ap:
        step = int(step)
        cnt = int(cnt)
        if cnt <= 1 or step == 0:
            continue
        if step % C == 0:
            rext += (step // C) * (cnt - 1)
        else:
            cext += step * (cnt - 1)
    if c0 + cext >= C:
        rext += (c0 + cext) // C
        return (t.name, r0, r0 + rext + 1, 0, C * es)
    return (t.name, r0, r0 + rext + 1, c0 * es, (c0 + cext + 1) * es)


def _ovl(a, b):
    return a[1] < b[2] and b[1] < a[2] and a[3] < b[4] and b[3] < a[4]


def _covers(a, b):
    return a[1] <= b[1] and a[2] >= b[2] and a[3] <= b[3] and a[4] >= b[4]


class Op:
    __slots__ = ("eng", "fn", "waits", "sig", "sem", "val", "is_dma", "idx")

    def __init__(self, eng, fn, is_dma=False):
        self.eng = eng
        self.fn = fn
        self.waits = {}
        self.sig = False
        self.sem = None
        self.val = None
        self.is_dma = is_dma


class Sched:
    def __init__(self, nc):
        self.nc = nc
        self.ops = {e: [] for e in (PE, ACT, DVE, POOL, SP)}
        self.recs = {}
        self.readonly = set()
        self.dma_rr = 0
        self.dma_rr_pool = 0
        self.dma_last = [None] * N_DMA_SEMS
        self.dma_cnt = [0] * N_DMA_SEMS
        self.n_ops = 0

    def _dep(self, op, src):
        if src is op:
            return
        if src.eng == PE and op.eng == PE and not src.is_dma and not op.is_dma:
            return
        src.sig = True
        key = id(src) if src.is_dma else src.eng
        cur = op.waits.get(key)
        if cur is None or cur.idx < src.idx:
            op.waits[key] = src

    def _track(self, op, reads, writes):
        for ap in reads:
            b = _box(ap)
            if b[0] in self.readonly:
                continue
            ps = (b[0] == "psum")
            if ps:
                b = (b[0], 0, 128, b[3] // 2048 * 2048, (b[4] + 2047) // 2048 * 2048)
            lst = self.recs.setdefault(b[0], [])
            merged = False
            for rec in lst:
                if rec[1] or (ps and rec[2].eng != op.eng):
                    if _ovl(rec[0], b):
                        self._dep(op, rec[2])
                if (not rec[1]) and (not op.is_dma) and (not merged) and rec[2].eng == op.eng \
                        and (not rec[2].is_dma) and rec[0] == b:
                    rec[2] = op
                    merged = True
            if not merged:
                lst.append([b, False, op])
        for ap in writes:
            b = _box(ap)
            if b[0] == "psum":
                b = (b[0], 0, 128, b[3] // 2048 * 2048, (b[4] + 2047) // 2048 * 2048)
            lst = self.recs.setdefault(b[0], [])
            keep = []
            for rec in lst:
                if _ovl(rec[0], b):
                    if rec[2] is not op:
                        self._dep(op, rec[2])
                        if _covers(b, rec[0]):
                            continue
                keep.append(rec)
            keep.append([b, True, op])
            self.recs[b[0]] = keep

    def add(self, eng, fn, reads=(), writes=()):
        op = Op(eng, fn)
        op.idx = self.n_ops
        self.n_ops += 1
        self._track(op, reads, writes)
        self.ops[eng].append(op)
        return op

    def dma(self, out, in_, eng=SP, **kw):
        op = Op(eng, None, is_dma=True)
        op.idx = self.n_ops
        self.n_ops += 1
        half = N_DMA_SEMS // 2
        if eng == POOL:
            slot = half + self.dma_rr_pool
            self.dma_rr_pool = (self.dma_rr_pool + 1) % half
        else:
            slot = self.dma_rr
            self.dma_rr = (self.dma_rr + 1) % half
        prev = self.dma_last[slot]
        if prev is not None:
            op.waits[id(prev)] = prev
        self.dma_cnt[slot] += 16
        op.sem = slot
        op.val = self.dma_cnt[slot]
        op.sig = True
        self.dma_last[slot] = op
        op.fn = (out, in_, kw)
        self._track(op, [in_], [out])
        self.ops[eng].append(op)
        return op

    def emit(self):
        nc = self.nc
        with contextlib.ExitStack() as st:
            esem = {e: st.enter_context(nc.semaphore("s_" + e)) for e in (PE, ACT, DVE, POOL)}
            dsem = [st.enter_context(nc.semaphore("d%d" % i)) for i in range(N_DMA_SEMS)]
            for e in (PE, ACT, DVE, POOL, SP):
                n = 0
                for op in self.ops[e]:
                    if op.is_dma:
                        continue
                    if op.sig:
                        n += 1
                        op.sem = e
                        op.val = n
            block = st.enter_context(nc.Block())

            def run(e, eng):
                waited = {}
                for op in self.ops[e]:
                    for src in op.waits.values():
                        if src.is_dma:
                            sem = dsem[src.sem]
                            k = ("d", src.sem)
                        else:
                            sem = esem[src.sem]
                            k = src.sem
                        if waited.get(k, 0) >= src.val:
                            continue
                        waited[k] = src.val
                        eng.wait_ge(sem, src.val)
                    if op.is_dma:
                        out, in_, kw = op.fn
                        eng.dma_start(out=out, in_=in_, **kw).then_inc(dsem[op.sem], 16)
                    else:
                        ins = op.fn(eng)
                        if op.sig:
                            ins.then_inc(esem[e], 1)
                for slot in range(N_DMA_SEMS):
                    last = self.dma_last[slot]
                    if last is not None and last.eng == e:
                        eng.wait_ge(dsem[slot], last.val)

            @block.tensor
            def _(eng):
                run(PE, eng)

            @block.scalar
            def _(eng):
                run(ACT, eng)

            @block.vector
            def _(eng):
                run(DVE, eng)

            @block.gpsimd
            def _(eng):
                run(POOL, eng)

            @block.sync
            def _(eng):
                run(SP, eng)


class Prog:
    def __init__(self):
        self.nc = bass.Bass("TRN2", target_bir_lowering=False)
        self.st = contextlib.ExitStack()
        self.arena = self.st.enter_context(self.nc.sbuf_tensor("arena", [128, ARENA], U8))
        self.psum = self.st.enter_context(self.nc.psum_tensor("psum", [128, 4096], F32))
        self.S = Sched(self.nc)
        self.top = 0
        self.inputs = {}
        self.outputs = []

    def din(self, name, shape, dtype=F32):
        self.S.readonly.add(name)
        return self.nc.dram_tensor(name, list(shape), dtype, kind="ExternalInput").ap()

    def dout(self, name, shape, dtype=F32):
        self.outputs.append(name)
        return self.nc.dram_tensor(name, list(shape), dtype, kind="ExternalOutput").ap()

    def dscr(self, name, shape, dtype=F32):
        return self.nc.dram_tensor(name, list(shape), dtype, kind="Internal").ap()

    def alloc(self, shape, dtype):
        n = 1
        for s in shape[1:]:
            n *= int(s)
        nbytes = n * _esize(dtype)
        nbytes = (nbytes + 63) // 64 * 64
        assert self.top + nbytes <= ARENA, ("arena overflow", self.top, nbytes, shape)
        v = self.arena[:, self.top:self.top + nbytes].bitcast(dtype)[:, 0:n]
        self.top += nbytes
        if len(shape) > 2:
            names = " ".join("d%d" % i for i in range(1, len(shape)))
            kw = {"d%d" % i: int(shape[i]) for i in range(1, len(shape))}
            v = v.rearrange("p (%s) -> p %s" % (names, names), **kw)
        if int(shape[0]) < 128:
            v = v[0:int(shape[0])]
        return v

    def mark(self):
        return self.top

    def release(self, m):
        self.top = m

    def bank(self, i, n=512, dtype=F32):
        if dtype == F32:
            return self.psum[:, i * 512:i * 512 + n]
        v = self.psum[:, i * 512:(i + 1) * 512].bitcast(dtype)
        return v[:, 0:n]

    def dma(self, out, in_, eng=SP, **kw):
        return self.S.dma(out, in_, eng=eng, **kw)

    def mm(self, out, lhsT, rhs, start=True, stop=True, skip=False):
        if skip:
            return self.S.add(PE, lambda e: e.matmul(out, lhsT, rhs, start=start, stop=stop,
                                                     skip_group_check=True),
                              reads=[lhsT, rhs], writes=[out])
        return self.S.add(PE, lambda e: e.matmul(out, lhsT, rhs, start=start, stop=stop),
                          reads=[lhsT, rhs], writes=[out])

    def transpose(self, out, in_, ident):
        return self.S.add(PE, lambda e: e.transpose(out, in_, ident),
                          reads=[in_, ident], writes=[out])

    def act(self, out, in_, func, bias=None, scale=None, accum_out=None, eng=ACT):
        kw = {}
        reads = [in_]
        writes = [out]
        if bias is not None:
            kw["bias"] = bias
            if not isinstance(bias, (int, float)):
                reads.append(bias)
        if scale is not None:
            kw["scale"] = scale
            if not isinstance(scale, (int, float)):
                reads.append(scale)
        if accum_out is not None:
            kw["accum_out"] = accum_out
            writes.append(accum_out)
        return self.S.add(eng, lambda e: e.activation(out=out, in_=in_, func=func, **kw),
                          reads=reads, writes=writes)

    def tt(self, out, in0, in1, op, eng=DVE):
        return self.S.add(eng, lambda e: e.tensor_tensor(out=out, in0=in0, in1=in1, op=op),
                          reads=[in0, in1], writes=[out])

    def ts(self, out, in0, s1, s2, op0, op1=None, eng=DVE, accum_out=None):
        reads = [in0]
        writes = [out]
        for s in (s1, s2):
            if s is not None and not isinstance(s, (int, float)):
                reads.append(s)
        kw = {}
        if op1 is not None:
            kw["op1"] = op1
        if accum_out is not None:
            kw["accum_out"] = accum_out
            writes.append(accum_out)
        return self.S.add(eng, lambda e: e.tensor_scalar(out=out, in0=in0, scalar1=s1, scalar2=s2,
                                                         op0=op0, **kw),
                          reads=reads, writes=writes)

    def stt(self, out, in0, scalar, in1, op0, op1, eng=DVE):
        reads = [in0, in1]
        if not isinstance(scalar, (int, float)):
            reads.append(scalar)
        return self.S.add(eng, lambda e: e.scalar_tensor_tensor(out=out, in0=in0, scalar=scalar, in1=in1,
                                                                op0=op0, op1=op1),
                          reads=reads, writes=[out])

    def copy(self, out, in_, eng=DVE):
        if eng == ACT:
            return self.act(out, in_, AF.Copy)
        return self.S.add(eng, lambda e: e.tensor_copy(out=out, in_=in_), reads=[in_], writes=[out])

    def memset(self, out, val, eng=DVE):
        return self.S.add(eng, lambda e: e.memset(out, val), writes=[out])

    def recip(self, out, in_):
        return self.S.add(DVE, lambda e: e.reciprocal(out=out, in_=in_), reads=[in_], writes=[out])

    def finish(self):
        self.S.emit()
        return self.nc
D = 2048
KC = 16
EPS = 1e-6


def bcast_mid(ap2, n):
    a = ap2.ap
    return bass.AP(ap2.tensor, ap2.offset, [list(a[0]), [0, n]] + [list(x) for x in a[1:]])


def bcast_last(ap2, n):
    a = ap2.ap
    return bass.AP(ap2.tensor, ap2.offset, [list(x) for x in a] + [[0, n]])


def load_consts(P, ones_d, rot_d=None, ident_d=None):
    c = {}
    c["ones"] = P.alloc([128, 128], BF16)
    P.dma(c["ones"], ones_d)
    if rot_d is not None:
        c["rot"] = P.alloc([128, 128], BF16)
        P.dma(c["rot"], rot_d)
    if ident_d is not None:
        c["ident"] = P.alloc([128, 128], BF16)
        P.dma(c["ident"], ident_d)
    c["eps"] = P.alloc([128, 1], F32)
    P.memset(c["eps"], EPS)
    return c


def sumsq_rstd(P, consts, src_chunks, w, nfeat, bank_i, rstd_out, sq_tmp):
    n = len(src_chunks)
    bk = P.bank(bank_i)[:, 0:w]
    for i, s in enumerate(src_chunks):
        P.act(sq_tmp[:, i, 0:w], s, AF.Square)
    for i in range(n):
        P.mm(bk, consts["ones"], sq_tmp[:, i, 0:w], start=(i == 0), stop=(i == n - 1))
    P.act(rstd_out, bk, AF.Sqrt, bias=consts["eps"], scale=1.0 / nfeat)
    P.recip(rstd_out, rstd_out)


def modnorm(P, consts, segs, gam, modsb, sc_idx, sh_idx, hT):
    m0 = P.mark()
    ab = {}
    for (_, _, _, j) in segs:
        if j in ab:
            continue
        a = P.alloc([128, KC], F32)
        b = P.alloc([128, KC], F32)
        P.stt(a, modsb[:, sc_idx * KC:(sc_idx + 1) * KC, j], 1.0, gam, ALU.add, ALU.mult)
        P.copy(b, modsb[:, sh_idx * KC:(sh_idx + 1) * KC, j])
        ab[j] = (a, b)
    xts = [P.alloc([128, KC, 512], F32) for _ in range(2)]
    sq = P.alloc([128, KC, 512], BF16)
    rstd = P.alloc([128, 512], F32)
    ti = 0
    for (xd, col0, n, j) in segs:
        xv = xd.rearrange("(c p) t -> p c t", p=128)
        a, b = ab[j]
        for t0 in range(0, n, 512):
            w = min(512, n - t0)
            xt = xts[ti % 2]
            ti += 1
            P.dma(xt[:, :, 0:w], xv[:, :, t0:t0 + w])
            sumsq_rstd(P, consts, [xt[:, c, 0:w] for c in range(KC)], w, D, 7, rstd[:, 0:w], sq)
            P.tt(xt[:, :, 0:w], xt[:, :, 0:w], bcast_mid(rstd[:, 0:w], KC), ALU.mult)
            for c in range(KC):
                P.act(hT[:, c, col0 + t0:col0 + t0 + w], xt[:, c, 0:w], AF.Identity,
                      bias=b[:, c:c + 1], scale=a[:, c:c + 1])
    P.release(m0)


def gemm(P, hT, kc_n, Wd, chunks, tiles, epilogue, nbanks=3, bank0=0, wslots=None):
    Wv = Wd.rearrange("(kc p) n -> p kc n", p=128)
    own = wslots is None
    if own:
        wslots = [P.alloc([128, kc_n, 128], BF16) for _ in range(3)]
    bi = 0
    for ci, n in enumerate(chunks):
        wt = wslots[ci % len(wslots)]
        P.dma(wt, Wv[:, :, n * 128:(n + 1) * 128], eng=POOL)
        for ti, (a, b) in enumerate(tiles):
            bk = P.bank(bank0 + bi % nbanks)[:, 0:b - a]
            bi += 1
            for k in range(kc_n):
                P.mm(bk, wt[:, k, :], hT[:, k, a:b], start=(k == 0), stop=(k == kc_n - 1))
            epilogue(n, ti, a, b, bk)


def qk_epilogue(P, consts, bk, w, g_col, cos, sin, out_bf, tmp, bank_ss, bank_pq):
    sq, qg, t1, t2, rstd = tmp
    if g_col is not None:
        P.act(sq[:, 0:w], bk, AF.Square)
        P.ts(qg[:, 0:w], bk, g_col, None, ALU.mult)
        ss = P.bank(bank_ss)[:, 0:w]
        P.mm(ss, consts["ones"], sq[:, 0:w])
        P.act(rstd[:, 0:w], ss, AF.Sqrt, bias=consts["eps"], scale=1.0 / 128.0)
        P.recip(rstd[:, 0:w], rstd[:, 0:w])
    else:
        P.copy(qg[:, 0:w], bk)
    if cos is not None:
        pq = P.bank(bank_pq)[:, 0:w]
        P.mm(pq, consts["rot"], qg[:, 0:w])
        P.tt(t1[:, 0:w], qg[:, 0:w], cos, ALU.mult)
        P.tt(t2[:, 0:w], pq, sin, ALU.mult)
        if g_col is not None:
            P.tt(t1[:, 0:w], t1[:, 0:w], t2[:, 0:w], ALU.add)
            P.tt(out_bf, t1[:, 0:w], rstd[:, 0:w], ALU.mult)
        else:
            P.tt(out_bf, t1[:, 0:w], t2[:, 0:w], ALU.add)
    else:
        if g_col is not None:
            P.tt(out_bf, qg[:, 0:w], rstd[:, 0:w], ALU.mult)
        else:
            P.copy(out_bf, qg[:, 0:w])


def resid(P, consts, segs, gam, modsb, g_idx):
    m0 = P.mark()
    gg = {}
    for seg in segs:
        j = seg[4]
        if j not in gg:
            g = P.alloc([128, KC], F32)
            P.tt(g, modsb[:, g_idx * KC:(g_idx + 1) * KC, j], gam, ALU.mult)
            gg[j] = g
    xts = [P.alloc([128, KC, 512], F32) for _ in range(2)]
    yts = [P.alloc([128, KC, 512], F32) for _ in range(2)]
    sq = P.alloc([128, KC, 512], BF16)
    rstd = P.alloc([128, 512], F32)
    ti = 0
    for (xin, yd, xout, n, j) in segs:
        xv = xin.rearrange("(c p) t -> p c t", p=128)
        yv = yd.rearrange("(c p) t -> p c t", p=128)
        ov = xout.rearrange("(c p) t -> p c t", p=128)
        g = gg[j]
        for t0 in range(0, n, 512):
            w = min(512, n - t0)
            xt = xts[ti % 2]
            yt = yts[ti % 2]
            ti += 1
            P.dma(xt[:, :, 0:w], xv[:, :, t0:t0 + w])
            P.dma(yt[:, :, 0:w], yv[:, :, t0:t0 + w])
            sumsq_rstd(P, consts, [yt[:, c, 0:w] for c in range(KC)], w, D, 7, rstd[:, 0:w], sq)
            P.tt(yt[:, :, 0:w], yt[:, :, 0:w], bcast_mid(rstd[:, 0:w], KC), ALU.mult)
            for c in range(KC):
                P.stt(xt[:, c, 0:w], yt[:, c, 0:w], g[:, c:c + 1], xt[:, c, 0:w], ALU.mult, ALU.add)
            P.dma(ov[:, :, t0:t0 + w], xt[:, :, 0:w])
    P.release(m0)
TOWN = 2048
NCTX = 256
TA = TOWN + 2
TALL = TA + NCTX


def load_vec(P, d, shape, dtype=F32):
    t = P.alloc(shape, dtype)
    P.dma(t, d)
    return t


def mod_phase(P, consts, cT_d, ada_w_d, adab_d, modT_out):
    m0 = P.mark()
    cs = P.alloc([128, KC, 2], F32)
    P.dma(cs, cT_d)
    sc = P.alloc([128, KC, 2], BF16)
    P.act(sc, cs, AF.Silu)
    wslots = [P.alloc([128, KC, 128], BF16) for _ in range(4)]
    for i in range(2):
        Wv = ada_w_d[i].rearrange("(kc p) n -> p kc n", p=128)
        adab = P.alloc([128, 96], F32)
        P.dma(adab, adab_d[i])
        bk = P.bank(6)
        for k in range(96):
            wt = wslots[k % 4]
            P.dma(wt, Wv[:, :, k * 128:(k + 1) * 128], eng=POOL)
            for c in range(KC):
                P.mm(bk[:, 2 * k:2 * k + 2], wt[:, c, :], sc[:, c, :], start=(c == 0), stop=(c == KC - 1))
        msb = P.alloc([128, 96, 2], F32)
        bv = bk[:, 0:192].rearrange("p (k j) -> p k j", j=2)
        for j in range(2):
            P.tt(msb[:, :, j], bv[:, :, j], adab, ALU.add)
        P.dma(modT_out[i], msb.rearrange("p k j -> p (k j)"))
    P.release(m0)


def build_A(stage=9, qchunks=None, hchunks=None):
    P = Prog()
    xT = P.din("xT", [D, TA])
    ctxT = P.din("ctxT", [D, NCTX])
    cT = P.din("cT", [128, KC, 2])
    ada_w = P.din("ada_w", [2, D, 6 * D])
    adab = P.din("adab", [2, 128, 96])
    ng = P.din("ng0", [128, KC])
    w_in = P.din("ab_w_in", [D, 4608])
    qkg = P.din("qkg", [128, 2])
    cosd = P.din("cos128", [128, TOWN])
    sind = P.din("sin128", [128, TOWN])
    ones_d = P.din("ones", [128, 128], BF16)
    rot_d = P.din("rot", [128, 128], BF16)
    cw_d = P.din("hy_cw", [128, 24, 3])
    cb_d = P.din("hy_cb", [128, 24])
    em_d = P.din("edge", [128, 2])
    modT = P.dout("modT", [2, 128, 192])
    qT = P.dout("qT", [1024, TOWN], BF16)
    kT = P.dout("kT", [256, TOWN], BF16)
    vT = P.dout("vT", [256, TOWN], BF16)
    hyT = P.dout("hyT", [3072, TOWN], BF16)
    qcT = P.dout("qcT", [1024, NCTX], BF16)
    kcT = P.dout("kcT", [256, NCTX], BF16)
    vcT = P.dout("vcT", [256, NCTX], BF16)
    hycT = P.dout("hycT", [3072, NCTX], BF16)

    consts = load_consts(P, ones_d, rot_d)
    mod_phase(P, consts, cT, ada_w, adab, modT)
    if stage <= 1:
        return P
    modsb = P.alloc([128, 96, 2], F32)
    P.dma(modsb.rearrange("p k j -> p (k j)"), modT[0])
    gam = load_vec(P, ng, [128, KC])
    hT = P.alloc([128, KC, TALL], BF16)
    modnorm(P, consts, [(xT, 0, TA, 0), (ctxT, TA, NCTX, 1)], gam, modsb, 1, 0, hT)
    if stage <= 2:
        return P

    g2 = load_vec(P, qkg, [128, 2])
    cos = load_vec(P, cosd, [128, TOWN])
    sin = load_vec(P, sind, [128, TOWN])
    cw = load_vec(P, cw_d, [128, 24, 3])
    cb = load_vec(P, cb_d, [128, 24])
    em = load_vec(P, em_d, [128, 2])
    tmp = (P.alloc([128, 512], BF16), P.alloc([128, 512], BF16), P.alloc([128, 512], F32),
           P.alloc([128, 512], F32), P.alloc([128, 512], F32))
    osts = [P.alloc([128, 512], BF16) for _ in range(3)]
    oi = [0]
    lat_tiles = [(1 + 512 * i, 1 + 512 * (i + 1)) for i in range(4)]
    ctx_tile = (TA, TALL)

    def ep_qkv(n, ti, a, b, bk):
        w = b - a
        ost = osts[oi[0] % 3]
        oi[0] += 1
        is_ctx = (ti == 4)
        if n < 8:
            dst = (qcT if is_ctx else qT)[n * 128:(n + 1) * 128]
            gc = g2[:, 0:1]
        elif n < 10:
            dst = (kcT if is_ctx else kT)[(n - 8) * 128:(n - 7) * 128]
            gc = g2[:, 1:2]
        else:
            dst = (vcT if is_ctx else vT)[(n - 10) * 128:(n - 9) * 128]
            gc = None
        if gc is None:
            P.act(ost[:, 0:w], bk, AF.Copy)
        elif is_ctx:
            qk_epilogue(P, consts, bk, w, gc, None, None, ost[:, 0:w], tmp, 3, 5)
        else:
            qk_epilogue(P, consts, bk, w, gc, cos[:, a - 1:b - 1], sin[:, a - 1:b - 1], ost[:, 0:w], tmp,
                        3 + ti % 2, 5 + ti % 2)
        if is_ctx:
            P.dma(dst[:, 0:w], ost[:, 0:w])
        else:
            P.dma(dst[:, a - 1:b - 1], ost[:, 0:w])

    gemm(P, hT, KC, w_in, list(range(12)) if qchunks is None else qchunks, lat_tiles + [ctx_tile], ep_qkv)
    if stage <= 3:
        return P

    U = [P.alloc([128, TALL + 2], F32) for _ in range(2)]
    V = P.alloc([128, TALL], F32)
    Vb = [P.alloc([128, TALL], BF16) for _ in range(2)]
    for u in U:
        P.memset(u[:, TA:TA + 1], 0.0)
        P.memset(u[:, TALL + 1:TALL + 2], 0.0)
    hy_tiles = [(512 * i, 512 * (i + 1)) for i in range(4)] + [(2048, TA), (TA, TALL)]

    def ep_hy(n, ti, a, b, bk):
        hc = n - 12
        u = U[hc % 2]
        if ti < 5:
            P.act(u[:, a:b], bk, AF.Copy)
        else:
            P.act(u[:, TA + 1:TALL + 1], bk, AF.Copy)
            vb = Vb[hc % 2]
            P.ts(u[:, 0:1], u[:, 0:1], em[:, 0:1], None, ALU.mult)
            P.ts(u[:, TA - 1:TA], u[:, TA - 1:TA], em[:, 1:2], None, ALU.mult)
            P.act(V, u[:, 1:TALL + 1], AF.Identity, bias=cb[:, hc:hc + 1], scale=cw[:, hc, 1:2])
            P.stt(V, u[:, 0:TALL], cw[:, hc, 0:1], V, ALU.mult, ALU.add)
            P.stt(vb, u[:, 2:TALL + 2], cw[:, hc, 2:3], V, ALU.mult, ALU.add)
            P.dma(hyT[hc * 128:(hc + 1) * 128, :], vb[:, 0:TOWN])
            P.dma(hycT[hc * 128:(hc + 1) * 128, :], vb[:, TA:TA + NCTX])

    gemm(P, hT, KC, w_in, list(range(12, 36)) if hchunks is None else hchunks, hy_tiles, ep_hy)
    return P
def attention(P, consts, qT_d, kT_d, v_d, oT_d, nheads, kv_of, nmaps, Tq, S, scale, finish, vw=128):
    m0 = P.mark()
    QW = min(512 // nmaps, Tq)
    dk = 128 // nmaps
    nkt = S // 128
    nqs = QW // 128
    KTs = [P.alloc([128, S], BF16) for _ in range(2)]
    VAs = [P.alloc([128, nkt, 129], BF16) for _ in range(2)]
    for va in VAs:
        P.memset(va[:, :, 128:129], 1.0)
    QTs = [P.alloc([128, QW], BF16) for _ in range(2)]
    PTs = [P.alloc([128, 512], BF16) for _ in range(3)]
    obf = [P.alloc([128, 128], BF16) for _ in range(2)]
    oTs = [P.alloc([128, QW], BF16) for _ in range(2)]
    vv = v_d.rearrange("(kt p) e -> p kt e", p=128)
    cur_kv = None
    kvi = 0
    qi = 0
    pi = 0
    oi = 0
    for h in range(nheads):
        kv = kv_of(h)
        if kv != cur_kv:
            KT = KTs[kvi % 2]
            VA = VAs[kvi % 2]
            kvi += 1
            cur_kv = kv
            P.dma(KT, kT_d[kv * 128:(kv + 1) * 128, :])
            P.dma(VA[:, :, 0:128], vv[:, :, kv * vw:kv * vw + 128])
        for qt in range(Tq // QW):
            QT = QTs[qi % 2]
            obase = 4 + 2 * (qi % 2)
            qi += 1
            P.dma(QT, qT_d[h * 128:(h + 1) * 128, qt * QW:(qt + 1) * QW])
            def Oacc(m, qs):
                idx = m * nqs + qs
                return P.bank(obase + idx // 2)[:, (idx % 2) * 256:(idx % 2) * 256 + 129]
            for kt in range(nkt):
                PT = PTs[pi % 3]
                pi += 1
                for m in range(nmaps):
                    sb = P.bank(kt % 3) if nmaps == 1 else P.bank(2 * m + kt % 2)
                    P.mm(sb[:, 0:QW], KT[m * dk:(m + 1) * dk, kt * 128:(kt + 1) * 128],
                         QT[m * dk:(m + 1) * dk, :])
                    P.act(PT[:, m * QW:(m + 1) * QW], sb[:, 0:QW], AF.Exp, scale=scale)
                for m in range(nmaps):
                    for qs in range(nqs):
                        P.mm(Oacc(m, qs), PT[:, m * QW + qs * 128:m * QW + (qs + 1) * 128], VA[:, kt, :],
                             start=(kt == 0 and (m * nqs + qs) % 2 == 0), stop=(kt == nkt - 1), skip=True)
            oT = oTs[oi % 2]
            oi += 1
            tb = P.bank(3, 1024, BF16)
            for qs in range(nqs):
                ob = obf[qs % 2]
                finish([Oacc(m, qs) for m in range(nmaps)], ob)
                P.transpose(tb[:, qs * 128:(qs + 1) * 128], ob, consts["ident"])
            P.copy(oT, tb[:, 0:QW])
            P.dma(oT_d[h * 128:(h + 1) * 128, qt * QW:(qt + 1) * QW], oT)
    P.release(m0)


def make_finish_gqa(P):
    r = P.alloc([128, 1], F32)

    def finish(Os, ob):
        O = Os[0]
        P.recip(r, O[:, 128:129])
        P.ts(ob, O[:, 0:128], r, None, ALU.mult)
    return finish


def make_finish_diff(P, consts, lam_bc_d, sg_bc_d, lambda_init):
    lv = P.alloc([128, 4, 64], F32)
    P.dma(lv, lam_bc_d)
    pr = P.alloc([128, 2, 64], F32)
    lvv = lv.rearrange("p (a b) d -> p a b d", b=2)
    P.tt(pr, lvv[:, :, 0, :], lvv[:, :, 1, :], ALU.mult)
    s2 = P.alloc([128, 2], F32)
    P.S.add(DVE, lambda e: e.reduce_sum(out=s2, in_=pr, axis=AX.X), reads=[pr], writes=[s2])
    e2 = P.alloc([128, 2], F32)
    P.act(e2, s2, AF.Exp)
    lam = P.alloc([128, 1], F32)
    P.tt(lam, e2[:, 0:1], e2[:, 1:2], ALU.subtract)
    P.ts(lam, lam, float(lambda_init), None, ALU.add)
    sg = P.alloc([128, 128], F32)
    P.dma(sg, sg_bc_d)
    P.ts(sg, sg, float(1.0 - lambda_init), None, ALU.mult)
    r0 = P.alloc([128, 1], F32)
    r1 = P.alloc([128, 1], F32)
    ss = P.alloc([128, 1], F32)
    t = P.alloc([128, 128], F32)
    o = P.alloc([128, 128], F32)
    junk = P.alloc([128, 128], F32)

    def finish(Os, ob):
        O0, O1 = Os
        P.recip(r0, O0[:, 128:129])
        P.recip(r1, O1[:, 128:129])
        P.tt(r1, r1, lam, ALU.mult)
        P.ts(t, O1[:, 0:128], r1, None, ALU.mult)
        P.stt(o, O0[:, 0:128], r0, t, ALU.mult, ALU.subtract)
        P.act(junk, o, AF.Square, accum_out=ss)
        P.act(ss, ss, AF.Sqrt, bias=consts["eps"], scale=1.0 / 128.0)
        P.recip(ss, ss)
        P.stt(ob, o, ss, sg, ALU.mult, ALU.mult)
    return finish
HC = 32
MAGIC = 12582912.0


def hy_host_tables(S, SD):
    N = 128 * S
    n2 = np.arange(S)[:, None]
    k2 = np.arange(S)[None, :]
    a = 2 * np.pi * (n2 * k2 % S) / S
    FA = np.concatenate([np.cos(a), -np.sin(a)], axis=1)
    n1 = np.arange(128)[:, None, None]
    kk2 = np.arange(S)[None, :, None]
    k1 = np.arange(128)[None, None, :]
    ph = 2 * np.pi * ((n1 * (S * k1 + kk2)) % N) / N
    GT = np.stack([np.cos(ph), -np.sin(ph), np.sin(ph)], axis=2)
    kk1 = np.arange(128)[:, None]
    nn1 = np.arange(128)[None, :]
    th = 2 * np.pi * ((kk1 * nn1) % 128) / 128
    CI = np.stack([np.concatenate([np.cos(th), np.sin(th)], 1),
                   np.concatenate([-np.sin(th), np.cos(th)], 1)], axis=1)
    ek2 = np.arange(S)[:, None, None]
    en1 = np.arange(128)[None, :, None]
    en2 = np.arange(SD)[None, None, :]
    ps = 2 * np.pi * ((ek2 * (en1 + 128 * en2)) % N) / N
    ET = np.stack([np.cos(ps), -np.sin(ps)], axis=2)
    return {"FA": FA.astype(NPBF), "GT": GT.astype(NPBF), "CI": CI.astype(NPBF), "ET": ET.astype(NPBF)}


def hy_host_filter_consts(n_tok, ch0, nch):
    HY_BANDS, HY_CH = 16, 1024
    f32 = np.float32
    t = np.arange(n_tok, dtype=f32)
    t_norm = (t / f32(max(n_tok - 1, 1))).astype(f32)
    w = (f32(2.0 * np.pi / n_tok) * t).astype(f32)
    bands = np.linspace(1e-4, HY_BANDS - 1, HY_BANDS, dtype=f32)
    z = (w[:, None] * bands).astype(f32)
    feats = np.concatenate([t_norm[:, None], np.cos(z), -np.sin(z)], axis=-1).astype(f32)
    deltas = np.abs(np.linspace(np.log(1e-2) / 0.3, np.log(1e-2) / 1.5, HY_CH, dtype=f32)).astype(f32)
    decay = np.exp(-t_norm[:, None] * deltas[None, ch0:ch0 + nch]).astype(f32)
    N = 2 * n_tok
    idx = np.zeros(N, np.int64)
    idx[:n_tok] = np.arange(n_tok)
    idx[n_tok + 1:] = N - np.arange(n_tok + 1, N)
    featsT = np.ascontiguousarray(feats[idx].T)
    decF = np.zeros((N, nch), f32)
    decB = np.zeros((N, nch), f32)
    decF[:n_tok] = decay
    decB[n_tok + 1:] = decay[idx[n_tok + 1:]]
    S = N // 128
    ng = nch // HC

    def lay(d):
        return np.ascontiguousarray(d.reshape(S, 128, ng, HC).transpose(2, 0, 1, 3))
    return featsT.astype(NPBF), lay(decF), lay(decB)


def hy_load_tables(P, td, S, SD):
    T = {"S": S, "SD": SD}
    T["FA"] = P.alloc([S, 2 * S], BF16)
    P.dma(T["FA"], td["FA"])
    T["CI"] = P.alloc([128, 2, 256], BF16)
    P.dma(T["CI"], td["CI"])
    T["ET"] = P.alloc([S, 128, 2, SD], BF16)
    P.dma(T["ET"], td["ET"])
    T["GTd"] = td["GT"]
    return T


def hy_mlp(P, consts, featsT_d, N, w1_d, b1_d, w2_d, b2_d, fr_d, hid2T):
    m0 = P.mark()
    w1 = P.alloc([33, 64], BF16)
    P.dma(w1, w1_d, eng=POOL)
    w2 = P.alloc([64, 64], BF16)
    P.dma(w2, w2_d, eng=POOL)
    bb = P.alloc([64, 2], F32)
    P.dma(bb[:, 0:1], b1_d)
    P.dma(bb[:, 1:2], b2_d)
    fr = P.alloc([64, 2], F32)
    P.dma(fr, fr_d)
    sc = P.alloc([64, 2], F32)
    of = P.alloc([64, 2], F32)
    P.ts(sc, fr, float(1.0 / (2 * np.pi)), None, ALU.mult)
    P.tt(of, bb, sc, ALU.mult)
    ft = P.alloc([33, N], BF16)
    P.dma(ft, featsT_d)
    h1 = P.alloc([64, N], BF16)
    y = P.alloc([64, 512], F32)
    mm_ = P.alloc([64, 512], F32)
    for layer in range(2):
        src, wt, dst = (ft, w1, h1) if layer == 0 else (h1, w2, hid2T)
        for t0 in range(0, N, 512):
            w = min(512, N - t0)
            bk = P.bank(6 + (t0 // 512) % 2)[0:64, 0:w]
            P.mm(bk, wt, src[:, t0:t0 + w])
            P.ts(y[:, 0:w], bk, sc[:, layer:layer + 1], of[:, layer:layer + 1], ALU.mult, ALU.add)
            P.ts(mm_[:, 0:w], y[:, 0:w], MAGIC, None, ALU.add)
            P.ts(mm_[:, 0:w], mm_[:, 0:w], MAGIC, None, ALU.subtract)
            P.tt(y[:, 0:w], y[:, 0:w], mm_[:, 0:w], ALU.subtract)
            P.act(dst[:, t0:t0 + w], y[:, 0:w], AF.Sin, scale=float(2 * np.pi))
    P.release(m0)


def hy_fwd(P, T, u_sb, nblk, A_sb, gti, epilogue):
    S = T["S"]
    C = HC
    for c in range(C):
        bk = P.bank(c % 2)[:, 0:2 * S]
        P.mm(bk, u_sb[0:nblk, c, :], T["FA"][0:nblk, :])
        P.copy(A_sb[:, :, :, c], bk.rearrange("p (j k) -> p k j", j=2), eng=(DVE if c % 2 else ACT))
    KB = min(256 // C, S)
    GCH = min(16, S)
    gts = T["gts"]
    for k0 in range(0, S, KB):
        if k0 % GCH == 0:
            gt = gts[gti[0] % 2]
            gti[0] += 1
            P.dma(gt[:, 0:GCH], T["GTd"][:, k0:k0 + GCH])
        bk = P.bank(2 + (k0 // KB) % 2)[:, 0:KB * 2 * C].rearrange("p (k j c) -> p k j c", k=KB, j=2)
        for kb in range(KB):
            k2 = k0 + kb
            g = gt[:, k2 % GCH]
            P.mm(bk[:, kb, 0, :], g[:, 0, :], A_sb[:, k2, 0, :], start=True, stop=False)
            P.mm(bk[:, kb, 0, :], g[:, 2, :], A_sb[:, k2, 1, :], start=False, stop=True)
            P.mm(bk[:, kb, 1, :], g[:, 1, :], A_sb[:, k2, 0, :], start=True, stop=False)
            P.mm(bk[:, kb, 1, :], g[:, 0, :], A_sb[:, k2, 1, :], start=False, stop=True)
        epilogue(k0, KB, bk)


def hy_conv(P, T, u_sb, Hd, gate_sb, skip_bc, z_sb, bufs, gti):
    S, SD = T["S"], T["SD"]
    C = HC
    AB, Y_sb, us, hch, tmps, tz = bufs
    A_sb = AB[:, 0:S * 2 * C].rearrange("p (k j c) -> p k j c", k=S, j=2)
    KB = min(256 // C, S)

    def ep(k0, KB_, bk):
        h = hch[(k0 // KB_) % 2]
        P.dma(h[:, 0:KB_], Hd[:, k0:k0 + KB_])
        t1, t2, t3, t4 = tmps
        P.tt(t1[:, 0:KB_], bk[:, :, 0, :], h[:, 0:KB_, 0, :], ALU.mult)
        P.tt(t2[:, 0:KB_], bk[:, :, 1, :], h[:, 0:KB_, 1, :], ALU.mult)
        P.tt(t3[:, 0:KB_], bk[:, :, 0, :], h[:, 0:KB_, 1, :], ALU.mult)
        P.tt(t4[:, 0:KB_], bk[:, :, 1, :], h[:, 0:KB_, 0, :], ALU.mult)
        P.tt(Y_sb[:, 0, :, k0:k0 + KB_].rearrange("p c k -> p k c"), t1[:, 0:KB_], t2[:, 0:KB_], ALU.subtract,
             eng=POOL)
        P.tt(Y_sb[:, 1, :, k0:k0 + KB_].rearrange("p c k -> p k c"), t3[:, 0:KB_], t4[:, 0:KB_], ALU.add,
             eng=POOL)

    hy_fwd(P, T, u_sb, SD, A_sb, gti, ep)
    P.tt(us, u_sb, bcast_last(skip_bc[0:SD, :], 128), ALU.mult, eng=POOL)
    P_sb = AB[0:S, :].rearrange("p (n j c) -> p n j c", n=128, j=2)
    for c in range(C):
        bk = P.bank(c % 2)[0:S, 0:256]
        P.mm(bk, Y_sb[:, 0, c, :], T["CI"][:, 0, :], start=True, stop=False)
        P.mm(bk, Y_sb[:, 1, c, :], T["CI"][:, 1, :], start=False, stop=True)
        P.copy(P_sb[:, :, :, c], bk.rearrange("p (j n) -> p n j", j=2), eng=(DVE if c % 2 else ACT))
    NB = 512 // C
    for n0 in range(0, 128, NB):
        bk = P.bank(4 + (n0 // NB) % 2)[0:SD, 0:NB * C].rearrange("p (n c) -> p n c", n=NB)
        for nb in range(NB):
            n1 = n0 + nb
            P.mm(bk[:, nb, :], T["ET"][:, n1, 0, :], P_sb[:, n1, 0, :], start=True, stop=False)
            P.mm(bk[:, nb, :], T["ET"][:, n1, 1, :], P_sb[:, n1, 1, :], start=False, stop=True)
        P.tt(tz[0:SD], bk.rearrange("p n c -> p c n"), us[:, :, n0:n0 + NB], ALU.add)
        P.tt(z_sb[:, :, n0:n0 + NB], tz[0:SD], gate_sb[:, :, n0:n0 + NB], ALU.mult)


def hy_filter(P, consts, T, hid2T, w3sb, o, decF_d, decB_d, Hd, bufs, gti, N):
    S = T["S"]
    C = HC
    AB, filt, tmpf, dch, stage, red, rl, ab_full = bufs
    A_sb = AB[:, 0:S * 2 * C].rearrange("p (k j c) -> p k j c", k=S, j=2)
    NB = 512 // C
    hv = hid2T.rearrange("p (n2 n1) -> p n1 n2", n1=128)
    for n0 in range(0, 128, NB):
        bF = P.bank(6)[0:S, 0:NB * C].rearrange("p (n c) -> p n c", n=NB)
        bB = P.bank(7)[0:S, 0:NB * C].rearrange("p (n c) -> p n c", n=NB)
        dF, dB = dch[(n0 // NB) % 2]
        P.dma(dF, decF_d[:, n0:n0 + NB, :])
        P.dma(dB, decB_d[:, n0:n0 + NB, :])
        for nb in range(NB):
            P.mm(bF[:, nb, :], hv[:, n0 + nb, :], w3sb[:, 0, :])
        for nb in range(NB):
            P.mm(bB[:, nb, :], hv[:, n0 + nb, :], w3sb[:, 1, :])
        t1 = tmpf[0][0:S]
        t2 = tmpf[1][0:S]
        P.tt(t1, bF, dF, ALU.mult)
        P.tt(t2, bB, dB, ALU.mult)
        P.tt(filt[:, :, n0:n0 + NB].rearrange("p c n -> p n c"), t1, t2, ALU.add, eng=POOL)
    ab = ab_full[0:S]
    P.act(ab, filt, AF.Abs)
    P.S.add(DVE, lambda e: e.reduce_sum(out=red[0:S, 0:C], in_=ab, axis=AX.X), reads=[ab], writes=[red[0:S, 0:C]])
    rh = red[0:S, C:2 * C].bitcast(BF16)[:, 0:C]
    rlo = red[0:S, 2 * C:3 * C].bitcast(BF16)[:, 0:C]
    rt = red[0:S, 3 * C:4 * C]
    P.copy(rh, red[0:S, 0:C])
    P.tt(rt, red[0:S, 0:C], rh, ALU.subtract)
    P.copy(rlo, rt)
    bk = P.bank(5)[:, 0:C]
    P.mm(bk, consts["ones"][0:S, :], rh, start=True, stop=False)
    P.mm(bk, consts["ones"][0:S, :], rlo, start=False, stop=True)
    P.ts(rl, bk, float(N), None, ALU.mult)
    P.recip(rl, rl)

    def ep(k0, KB_, bk2):
        st = stage[:, (k0 // KB_) % 2 * KB_:(k0 // KB_) % 2 * KB_ + KB_]
        P.tt(st.rearrange("p k j c -> p (k j) c"), bk2.rearrange("p k j c -> p (k j) c"),
             bcast_mid(rl, KB_ * 2), ALU.mult)
        P.dma(Hd[:, k0:k0 + KB_], st)

    hy_fwd(P, T, filt, S, A_sb, gti, ep)


def hyena_phase(P, consts, td, S, SD, hyv_d, featsT_d, decF_d, decB_d, wd, skipbc_d, w3_d, out_d, NCH, tag):
    N = 128 * S
    C = HC
    NG = NCH // C
    KB = min(256 // C, S)
    NB = 512 // C
    mT = P.mark()
    T = hy_load_tables(P, td, S, SD)
    T["gts"] = [P.alloc([128, min(16, S), 3, 128], BF16) for _ in range(2)]
    gti = [0]
    AB = P.alloc([128, 128 * 2 * C], BF16)
    us = P.alloc([max(S, SD), C, 128], F32)
    w3 = P.alloc([64, 2, 2, NCH], BF16)
    P.dma(w3, w3_d, eng=POOL)
    skb = P.alloc([128, 2, NCH], F32)
    P.dma(skb, skipbc_d)
    Hd = P.dscr("Hd_" + tag, [2, NG, 128, S, 2, C])
    m1 = P.mark()
    hid2T = P.alloc([64, N], BF16)
    hy_mlp(P, consts, featsT_d, N, wd["w1"], wd["b1"], wd["w2"], wd["b2"], wd["fr"], hid2T)
    filt = P.alloc([S, C, 128], BF16)
    tmpf = [P.alloc([S, NB, C], F32) for _ in range(2)]
    dch = [(P.alloc([S, NB, C], F32), P.alloc([S, NB, C], F32)) for _ in range(2)]
    stage = P.alloc([128, 2 * KB, 2, C], F32)
    red = P.alloc([128, 4 * C], F32)
    rl = P.alloc([128, C], F32)
    for o in range(2):
        for g in range(NG):
            w3g = w3[:, o, :, g * C:(g + 1) * C]
            hy_filter(P, consts, T, hid2T, w3g, o, decF_d[g], decB_d[g], Hd[o, g],
                      (AB, filt, tmpf, dch, stage, red, rl, us), gti, N)
    P.release(m1)
    Y_sb = P.alloc([128, 2, C, S], BF16)
    hch = [P.alloc([128, KB, 2, C], F32) for _ in range(2)]
    tmps = [P.alloc([128, KB, C], F32) for _ in range(4)]
    tz = P.alloc([max(SD, 1), C, NB], F32)
    xs = [[P.alloc([SD, C, 128], BF16) for _ in range(3)] for _ in range(2)]
    z1 = P.alloc([SD, C, 128], BF16)
    oo = [P.alloc([SD, C, 128], BF16) for _ in range(2)]
    bufs = (AB, Y_sb, us[0:SD], hch, tmps, tz)
    for g in range(NG):
        xv = xs[g % 2]
        for s3 in range(3):
            P.dma(xv[s3], hyv_d[s3, g * C:(g + 1) * C, :].rearrange("c (a b) -> a c b", b=128))
        hy_conv(P, T, xv[0], Hd[0, g], xv[1], skb[:, 0, g * C:(g + 1) * C], z1, bufs, gti)
        ob = oo[g % 2]
        hy_conv(P, T, z1, Hd[1, g], xv[2], skb[:, 1, g * C:(g + 1) * C], ob, bufs, gti)
        P.dma(out_d[g * C:(g + 1) * C, :].rearrange("c (a b) -> a c b", b=128), ob)
    P.release(mT)
SEQ = 8192
SKV = SEQ + NCTX
HYCH = 256


def build_B():
    P = Prog()
    qT = P.din("qT", [1024, TOWN], BF16)
    kT = P.din("kT", [256, SKV], BF16)
    v = P.din("v", [SKV, 256], BF16)
    qcT = P.din("qcT", [1024, NCTX], BF16)
    kcT = P.din("kcT", [256, NCTX], BF16)
    vc = P.din("vc", [NCTX, 256], BF16)
    ones_d = P.din("ones", [128, 128], BF16)
    ident_d = P.din("ident", [128, 128], BF16)
    oT = P.dout("oT", [1024, TOWN], BF16)
    ocT = P.dout("ocT", [1024, NCTX], BF16)
    consts = load_consts(P, ones_d, None, ident_d)
    fin = make_finish_gqa(P)
    attention(P, consts, qT, kT, v, oT, 8, lambda h: h // 4, 1, TOWN, SKV, 128 ** -0.5, fin)
    attention(P, consts, qcT, kcT, vc, ocT, 8, lambda h: h // 4, 1, NCTX, NCTX, 128 ** -0.5, fin)
    wd = {"w1": P.din("hy_w1", [33, 64]), "b1": P.din("hy_b1", [64, 1]), "w2": P.din("hy_w2", [64, 64]),
          "b2": P.din("hy_b2", [64, 1]), "fr": P.din("hy_fr", [64, 2])}
    sk_d = P.din("hy_skipbc", [128, 2, HYCH])
    w3_d = P.din("hy_w3", [64, 2, 2, HYCH])
    for tag, n_tok in (("l", SEQ), ("c", NCTX)):
        S = 2 * n_tok // 128
        SD = n_tok // 128
        td = {"FA": P.din("FA" + tag, [S, 2 * S], BF16), "GT": P.din("GT" + tag, [128, S, 3, 128], BF16),
              "CI": P.din("CI" + tag, [128, 2, 256], BF16), "ET": P.din("ET" + tag, [S, 128, 2, SD], BF16)}
        hyv = P.din("hyv" + tag, [3, HYCH, n_tok], BF16)
        ft = P.din("featsT" + tag, [33, 2 * n_tok], BF16)
        dF = P.din("decF" + tag, [HYCH // HC, S, 128, HC])
        dB = P.din("decB" + tag, [HYCH // HC, S, 128, HC])
        out = P.dout("ohy" + tag, [HYCH, n_tok], BF16)
        hyena_phase(P, consts, td, S, SD, hyv, ft, dF, dB, wd, sk_d, w3_d, out, HYCH, tag)
    return P


def build_D():
    P = Prog()
    qT = P.din("qT", [D, TOWN], BF16)
    kT = P.din("kT", [D, SKV], BF16)
    v = P.din("v", [SKV, D], BF16)
    lam_d = P.din("lam", [128, 4, 64])
    sg_d = P.din("sg", [128, 128])
    ones_d = P.din("ones", [128, 128], BF16)
    ident_d = P.din("ident", [128, 128], BF16)
    oT = P.dout("oT", [D, TOWN], BF16)
    consts = load_consts(P, ones_d, None, ident_d)
    lambda_init = 0.8 - 0.6 * float(np.exp(-0.3 * 1))
    fin = make_finish_diff(P, consts, lam_d, sg_d, lambda_init)
    attention(P, consts, qT, kT, v, oT, 16, lambda h: h, 2, TOWN, SKV, 64 ** -0.5, fin)
    return P
DFF = 5632
FC = DFF // 128


def ffn_up(P, consts, h2T, Tl, with_ctx, w_up_d, cw, cb, em, actT_d):
    m0 = P.mark()
    tall = Tl + (NCTX if with_ctx else 0)
    UW = tall + 2
    Ua = [P.alloc([128, UW], F32) for _ in range(2)]
    Ug = [P.alloc([128, UW], F32) for _ in range(2)]
    Va = P.alloc([128, tall], F32)
    Vg = P.alloc([128, tall], F32)
    ab = [P.alloc([128, tall], BF16) for _ in range(2)]
    sg = P.alloc([128, tall], F32)
    for u in Ua + Ug:
        P.memset(u[:, Tl:Tl + 1], 0.0)
        P.memset(u[:, UW - 1:UW], 0.0)
    tiles = [(512 * i, 512 * (i + 1)) for i in range((Tl - 2) // 512)] + [(Tl - 2, Tl)]
    if with_ctx:
        tiles.append((Tl, tall))
    nt = len(tiles)
    order = []
    for j in range(FC):
        order += [j, j + FC]

    def conv(u, hc, V):
        P.ts(u[:, 0:1], u[:, 0:1], em[:, 0:1], None, ALU.mult)
        P.ts(u[:, Tl - 1:Tl], u[:, Tl - 1:Tl], em[:, 1:2], None, ALU.mult)
        P.act(V, u[:, 1:tall + 1], AF.Identity, bias=cb[:, hc:hc + 1], scale=cw[:, hc, 1:2])
        P.stt(V, u[:, 0:tall], cw[:, hc, 0:1], V, ALU.mult, ALU.add)
        P.stt(V, u[:, 2:tall + 2], cw[:, hc, 2:3], V, ALU.mult, ALU.add)

    def ep(n, ti, a, b, bk):
        j = n % FC
        u = (Ua if n < FC else Ug)[j % 2]
        if with_ctx and ti == nt - 1:
            P.act(u[:, Tl + 1:tall + 1], bk, AF.Copy)
        else:
            P.act(u[:, a:b], bk, AF.Copy)
        if ti == nt - 1:
            if n < FC:
                conv(u, n, Va)
            else:
                conv(u, n, Vg)
                P.act(sg, Vg, AF.Silu)
                o = ab[j % 2]
                P.tt(o, Va, sg, ALU.mult)
                P.dma(actT_d[j * 128:(j + 1) * 128, 0:Tl - 2], o[:, 0:Tl - 2])
                if with_ctx:
                    P.dma(actT_d[j * 128:(j + 1) * 128, Tl - 2:Tl - 2 + NCTX], o[:, Tl:Tl + NCTX])

    gemm(P, h2T, KC, w_up_d, order, tiles, ep)
    P.release(m0)


def ffn_down(P, consts, actT_d, ntok, w_down_d, yT_d):
    m0 = P.mark()
    BLK = 768
    av = actT_d.rearrange("(kc p) t -> p kc t", p=128)
    act = P.alloc([128, FC, BLK], BF16)
    wsl = [P.alloc([128, FC, 128], BF16) for _ in range(3)]
    st = [P.alloc([128, 512], F32) for _ in range(3)]
    si = [0]
    for b0 in range(0, ntok, BLK):
        bw = min(BLK, ntok - b0)
        P.dma(act[:, :, 0:bw], av[:, :, b0:b0 + bw])
        tiles = [(a, min(a + 512, bw)) for a in range(0, bw, 512)]

        def ep(n, ti, a, b, bk):
            s = st[si[0] % 3]
            si[0] += 1
            P.copy(s[:, 0:b - a], bk, eng=(ACT if si[0] % 2 else DVE))
            P.dma(yT_d[n * 128:(n + 1) * 128, b0 + a:b0 + b], s[:, 0:b - a])

        gemm(P, act, FC, w_down_d, list(range(KC)), tiles, ep, wslots=wsl)
    P.release(m0)


def build_post(layer):
    with_ctx = (layer == 0)
    P = Prog()
    Tl = TA
    tall = Tl + (NCTX if with_ctx else 0)
    ntok = TOWN + (NCTX if with_ctx else 0)
    oT = P.din("oT", [D, Tl], BF16)
    xT = P.din("xT", [D, Tl])
    modT = P.din("modT", [128, 192])
    ngs = P.din("ngs", [3, 128, KC])
    w_out = P.din("w_out", [D, D])
    w_up = P.din("w_up", [D, 2 * DFF])
    w_down = P.din("w_down", [DFF, D])
    fcw = P.din("fcw", [128, 2 * FC, 3])
    fcb = P.din("fcb", [128, 2 * FC])
    em_d = P.din("edge", [128, 2])
    ones_d = P.din("ones", [128, 128], BF16)
    rot_d = P.din("rot", [128, 128], BF16)
    if with_ctx:
        ocT = P.din("ocT", [D, NCTX], BF16)
        ctxT = P.din("ctxT", [D, NCTX])
    yT = P.dscr("yT", [D, tall])
    xmT = P.dscr("xmT", [D, tall])
    actT = P.dscr("actT", [DFF, ntok], BF16)
    y2T = P.dscr("y2T", [D, ntok])
    xoT = P.dout("xoT", [D, TOWN])
    if with_ctx:
        cxoT = P.dout("cxoT", [D, NCTX])

    consts = load_consts(P, ones_d, rot_d)
    modsb = P.alloc([128, 96, 2], F32)
    P.dma(modsb.rearrange("p k j -> p (k j)"), modT)
    gam = P.alloc([128, 3, KC], F32)
    P.dma(gam, ngs.rearrange("a p c -> p a c"))
    em = load_vec(P, em_d, [128, 2])
    cw = load_vec(P, fcw, [128, 2 * FC, 3])
    cb = load_vec(P, fcb, [128, 2 * FC])

    m1 = P.mark()
    osb = P.alloc([128, KC, tall], BF16)
    P.dma(osb[:, :, 0:Tl], oT.rearrange("(c p) t -> p c t", p=128))
    if with_ctx:
        P.dma(osb[:, :, Tl:tall], ocT.rearrange("(c p) t -> p c t", p=128))
    st = [P.alloc([128, 512], F32) for _ in range(3)]
    si = [0]
    tiles = [(512 * i, 512 * (i + 1)) for i in range(4)] + [(2048, Tl)] + ([(Tl, tall)] if with_ctx else [])

    def ep_out(n, ti, a, b, bk):
        s = st[si[0] % 3]
        si[0] += 1
        P.copy(s[:, 0:b - a], bk, eng=(ACT if si[0] % 2 else DVE))
        P.dma(yT[n * 128:(n + 1) * 128, a:b], s[:, 0:b - a])

    gemm(P, osb, KC, w_out, list(range(KC)), tiles, ep_out)
    P.release(m1)

    segs = [(xT, yT[:, 0:Tl], xmT[:, 0:Tl], Tl, 0)]
    if with_ctx:
        segs.append((ctxT, yT[:, Tl:tall], xmT[:, Tl:tall], NCTX, 1))
    resid(P, consts, segs, gam[:, 0, :], modsb, 2)

    m3 = P.mark()
    h2T = P.alloc([128, KC, tall], BF16)
    msegs = [(xmT[:, 0:Tl], 0, Tl, 0)]
    if with_ctx:
        msegs.append((xmT[:, Tl:tall], Tl, NCTX, 1))
    modnorm(P, consts, msegs, gam[:, 1, :], modsb, 4, 3, h2T)
    ffn_up(P, consts, h2T, Tl, with_ctx, w_up, cw, cb, em, actT)
    P.release(m3)
    ffn_down(P, consts, actT, ntok, w_down, y2T)
    segs = [(xmT[:, 1:1 + TOWN], y2T[:, 0:TOWN], xoT, TOWN, 0)]
    if with_ctx:
        segs.append((xmT[:, Tl:tall], y2T[:, TOWN:ntok], cxoT, NCTX, 1))
    resid(P, consts, segs, gam[:, 2, :], modsb, 5)
    if layer == 1:
        return P

    mod1 = P.din("modT1", [128, 192])
    ng10 = P.din("ng10", [128, KC])
    w_in1 = P.din("dc_w_in", [D, 6144])
    cosd = P.din("cos64", [128, TOWN])
    sind = P.din("sin64", [128, TOWN])
    q1T = P.dout("q1T", [D, TOWN], BF16)
    k1T = P.dout("k1T", [D, TOWN], BF16)
    v1T = P.dout("v1T", [D, TOWN], BF16)
    kc1T = P.dout("kc1T", [D, NCTX], BF16)
    vc1T = P.dout("vc1T", [D, NCTX], BF16)
    modsb1 = P.alloc([128, 96, 2], F32)
    P.dma(modsb1.rearrange("p k j -> p (k j)"), mod1)
    gam1 = load_vec(P, ng10, [128, KC])
    cos = load_vec(P, cosd, [128, TOWN])
    sin = load_vec(P, sind, [128, TOWN])
    t1 = TOWN + NCTX
    hT = P.alloc([128, KC, t1], BF16)
    modnorm(P, consts, [(xoT, 0, TOWN, 0), (cxoT, TOWN, NCTX, 1)], gam1, modsb1, 1, 0, hT)
    tmp = (P.alloc([128, 512], BF16), P.alloc([128, 512], BF16), P.alloc([128, 512], F32),
           P.alloc([128, 512], F32), P.alloc([128, 512], F32))
    osts = [P.alloc([128, 512], BF16) for _ in range(3)]
    oi = [0]
    lat_tiles = [(512 * i, 512 * (i + 1)) for i in range(4)]

    def ep1(n, ti, a, b, bk):
        w = b - a
        ost = osts[oi[0] % 3]
        oi[0] += 1
        is_ctx = (ti == 4)
        if n < 16:
            dst = q1T[n * 128:(n + 1) * 128]
        elif n < 32:
            dst = (kc1T if is_ctx else k1T)[(n - 16) * 128:(n - 15) * 128]
        else:
            dst = (vc1T if is_ctx else v1T)[(n - 32) * 128:(n - 31) * 128]
        if n >= 32 or is_ctx:
            P.act(ost[:, 0:w], bk, AF.Copy)
        else:
            qk_epilogue(P, consts, bk, w, None, cos[:, a:b], sin[:, a:b], ost[:, 0:w], tmp, 3 + ti % 2, 5 + ti % 2)
        if is_ctx:
            P.dma(dst[:, 0:w], ost[:, 0:w])
        else:
            P.dma(dst[:, a:b], ost[:, 0:w])

    gemm(P, hT, KC, w_in1, list(range(16)), lat_tiles, ep1)
    gemm(P, hT, KC, w_in1, list(range(16, 48)), lat_tiles + [(TOWN, t1)], ep1)
    return P
GRID_W = 64
ROPE_THETA = 10000.0


def pc(v, k=None):
    v = np.asarray(v, np.float32).reshape(-1, 128)
    return np.ascontiguousarray(v.T)


def rope_tables(hd, t0, n):
    t = np.arange(t0, t0 + n)
    row = (t // GRID_W).astype(np.float32)
    col = (t % GRID_W).astype(np.float32)
    axis_dim = hd // 2
    inv_freq = (ROPE_THETA ** (-np.arange(0, axis_dim, 2, dtype=np.float32) / np.float32(axis_dim))).astype(np.float32)
    ang = np.concatenate([row[:, None] * inv_freq[None, :], col[:, None] * inv_freq[None, :]], axis=-1).astype(np.float32)
    pair = (np.arange(128) % hd) // 2
    a = ang[:, pair].T
    return np.ascontiguousarray(np.cos(a).astype(np.float32)), np.ascontiguousarray(np.sin(a).astype(np.float32))


def rot_matrix():
    R = np.zeros((128, 128), np.float32)
    for i in range(64):
        R[2 * i + 1, 2 * i] = -1.0
        R[2 * i, 2 * i + 1] = 1.0
    return R.astype(NPBF)


def prep_A(inp, core):
    b, qtr = core // 4, core % 4
    t0 = qtr * TOWN
    x = inp["x"][b]
    xe = np.zeros((TA, D), np.float32)
    lo, hi = max(t0 - 1, 0), min(t0 + TOWN + 1, x.shape[0])
    xe[lo - (t0 - 1):hi - (t0 - 1)] = x[lo:hi]
    cT = np.stack([pc(inp["c"][b]), pc(inp["c_ctx"])], axis=-1)
    cos, sin = rope_tables(128, t0, TOWN)
    m = {
        "xT": np.ascontiguousarray(xe.T),
        "ctxT": np.ascontiguousarray(inp["ctx"][b].T),
        "cT": np.ascontiguousarray(cT),
        "ada_w": inp["ada_w"],
        "adab": np.stack([pc(inp["ada_b"][i]) for i in range(2)]),
        "ng0": pc(inp["norm_g"][0, 0]),
        "ab_w_in": inp["ab_w_in"][0],
        "qkg": np.ascontiguousarray(inp["ab_qk_g"][0].T),
        "cos128": cos, "sin128": sin,
        "ones": np.ones((128, 128), NPBF),
        "rot": rot_matrix(),
        "hy_cw": np.ascontiguousarray(inp["hy_conv_w"][0].reshape(3, 24, 128).transpose(2, 1, 0)),
        "hy_cb": pc(inp["hy_conv_b"][0]),
        "edge": np.tile(np.array([[1.0 if t0 > 0 else 0.0, 1.0 if t0 + TOWN < x.shape[0] else 0.0]], np.float32), (128, 1)),
    }
    return m
NCORES = 8
_PROGS = {}


def _prog(name, builder):
    if name not in _PROGS:
        _PROGS[name] = builder().finish()
    return _PROGS[name]


def _run(name, builder, maps):
    nc = _prog(name, builder)
    res = run_bass_kernel_spmd(nc, maps, core_ids=list(range(NCORES)))
    return [{k: np.asarray(v) for k, v in r.items()} for r in res.results]


def _pad_cols(a):
    z = np.zeros((a.shape[0], 1), a.dtype)
    return np.concatenate([z, a, z], axis=1)


def prep_B(inp, A, core, cache):
    b, qtr = core // 4, core % 4
    g = [A[4 * b + q] for q in range(4)]
    kT = np.concatenate([x["kT"] for x in g] + [g[0]["kcT"]], axis=1)
    vT = np.concatenate([x["vT"] for x in g] + [g[0]["vcT"]], axis=1)
    ch = slice(qtr * HYCH, (qtr + 1) * HYCH)
    hyl = np.stack([np.concatenate([x["hyT"][s * 1024:(s + 1) * 1024][ch] for x in g], axis=1) for s in range(3)])
    hyc = np.stack([g[0]["hycT"][s * 1024:(s + 1) * 1024][ch] for s in range(3)])
    m = {
        "qT": A[core]["qT"], "kT": np.ascontiguousarray(kT), "v": np.ascontiguousarray(vT.T),
        "qcT": g[0]["qcT"], "kcT": g[0]["kcT"], "vc": np.ascontiguousarray(g[0]["vcT"].T),
        "ones": np.ones((128, 128), NPBF), "ident": np.eye(128, dtype=np.float32).astype(NPBF),
        "hy_w1": inp["hy_w1"][0], "hy_b1": inp["hy_b1"][0][:, None], "hy_w2": inp["hy_w2"][0],
        "hy_b2": inp["hy_b2"][0][:, None], "hy_fr": np.ascontiguousarray(inp["hy_freq"][0].T),
        "hy_skipbc": np.ascontiguousarray(np.tile(inp["hy_skip"][0][None, :, ch], (128, 1, 1))),
        "hy_w3": np.ascontiguousarray(inp["hy_w3"][0].reshape(64, 2, 2, 1024)[:, :, :, ch]),
        "hyvl": np.ascontiguousarray(hyl), "hyvc": np.ascontiguousarray(hyc),
    }
    for tag, n_tok in (("l", SEQ), ("c", NCTX)):
        key = ("tab", tag)
        if key not in cache:
            cache[key] = hy_host_tables(2 * n_tok // 128, n_tok // 128)
        for k, v in cache[key].items():
            m[k + tag] = v
        key = ("filt", tag, qtr)
        if key not in cache:
            cache[key] = hy_host_filter_consts(n_tok, qtr * HYCH, HYCH)
        m["featsT" + tag], m["decF" + tag], m["decB" + tag] = cache[key]
    return m


def _post_common(inp, layer, core, o_full, x_full, modT):
    b, qtr = core // 4, core % 4
    t0 = qtr * TOWN
    return {
        "oT": np.ascontiguousarray(_pad_cols(o_full)[:, t0:t0 + TA]),
        "xT": np.ascontiguousarray(_pad_cols(x_full)[:, t0:t0 + TA]),
        "modT": modT,
        "ngs": np.stack([pc(inp["norm_g"][layer, i]) for i in (1, 2, 3)]),
        "w_out": (inp["ab_w_out"] if layer == 0 else inp["dc_w_out"])[0],
        "w_up": inp["ffn_w_up"][layer], "w_down": inp["ffn_w_down"][layer],
        "fcw": np.ascontiguousarray(inp["ffn_conv_w"][layer].reshape(3, 2 * FC, 128).transpose(2, 1, 0)),
        "fcb": pc(inp["ffn_conv_b"][layer]),
        "edge": np.tile(np.array([[1.0 if t0 > 0 else 0.0, 1.0 if t0 + TOWN < SEQ else 0.0]], np.float32), (128, 1)),
        "ones": np.ones((128, 128), NPBF), "rot": rot_matrix(),
    }


def kernel(**inp):
    inp = {k: np.asarray(v) for k, v in inp.items()}
    cache = {}
    A = _run("A", build_A, [prep_A(inp, c) for c in range(NCORES)])
    B = _run("B", build_B, [prep_B(inp, A, c, cache) for c in range(NCORES)])
    del cache
    mapsC = []
    for b in range(2):
        g = [B[4 * b + q] for q in range(4)]
        o_full = np.concatenate([np.concatenate([x["oT"] for x in g], axis=1),
                                 np.concatenate([x["ohyl"] for x in g], axis=0)], axis=0)
        oc = np.concatenate([g[0]["ocT"], np.concatenate([x["ohyc"] for x in g], axis=0)], axis=0)
        x_full = np.ascontiguousarray(inp["x"][b].T)
        for q in range(4):
            c = 4 * b + q
            m = _post_common(inp, 0, c, o_full, x_full, A[c]["modT"][0])
            cos, sin = rope_tables(64, q * TOWN, TOWN)
            m.update({"ocT": np.ascontiguousarray(oc), "ctxT": np.ascontiguousarray(inp["ctx"][b].T),
                      "modT1": A[c]["modT"][1], "ng10": pc(inp["norm_g"][1, 0]), "dc_w_in": inp["dc_w_in"][0],
                      "cos64": cos, "sin64": sin})
            mapsC.append(m)
    C = _run("C", lambda: build_post(0), mapsC)
    del mapsC, B
    mapsD = []
    for b in range(2):
        g = [C[4 * b + q] for q in range(4)]
        kT = np.ascontiguousarray(np.concatenate([x["k1T"] for x in g] + [g[0]["kc1T"]], axis=1))
        v = np.ascontiguousarray(np.concatenate([x["v1T"] for x in g] + [g[0]["vc1T"]], axis=1).T)
        for q in range(4):
            mapsD.append({"qT": g[q]["q1T"], "kT": kT, "v": v,
                          "lam": np.ascontiguousarray(np.tile(inp["dc_lambda"][0][None], (128, 1, 1))),
                          "sg": np.ascontiguousarray(np.tile(inp["dc_subln_g"][0][None], (128, 1))),
                          "ones": np.ones((128, 128), NPBF), "ident": np.eye(128, dtype=np.float32).astype(NPBF)})
    Dm = _run("D", build_D, mapsD)
    del mapsD
    mapsE = []
    for b in range(2):
        o_full = np.concatenate([Dm[4 * b + q]["oT"] for q in range(4)], axis=1)
        x_full = np.concatenate([C[4 * b + q]["xoT"] for q in range(4)], axis=1)
        for q in range(4):
            c = 4 * b + q
            mapsE.append(_post_common(inp, 1, c, o_full, x_full, A[c]["modT"][1]))
    E = _run("E", lambda: build_post(1), mapsE)
    out = np.empty((2, SEQ, D), np.float32)
    for c in range(NCORES):
        b, q = c // 4, c % 4
        out[b, q * TOWN:(q + 1) * TOWN, :] = E[c]["xoT"].T
    return out
```

```python
import contextlib
import numpy as np
import ml_dtypes
import concourse.bass as bass
import concourse.mybir as mybir
from concourse.bass_utils import run_bass_kernel_spmd

F32 = mybir.dt.float32
BF16 = mybir.dt.bfloat16
U8 = mybir.dt.uint8
AF = mybir.ActivationFunctionType
ALU = mybir.AluOpType
AX = mybir.AxisListType
NPBF = ml_dtypes.bfloat16

PE, ACT, DVE, POOL, SP = "pe", "act", "dve", "pool", "sp"
N_DMA_SEMS = 24
ARENA = 204800


def _esize(dt):
    return int(mybir.dt.size(dt))


def _box(ap):
    t = ap.tensor
    es = _esize(ap.dtype)
    C = 1
    for s in list(t.shape)[1:]:
        C *= int(s)
    off = int(ap.offset)
    r0 = off // C
    c0 = off % C
    rext = 0
    cext = 0
    for (step, cnt) in ap.ap:
        step = int(step)
        cnt = int(cnt)
        if cnt <= 1 or step == 0:
            continue
        if step % C == 0:
            rext += (step // C) * (cnt - 1)
        else:
            cext += step * (cnt - 1)
    if c0 + cext >= C:
        rext += (c0 + cext) // C
        return (t.name, r0, r0 + rext + 1, 0, C * es)
    return (t.name, r0, r0 + rext + 1, c0 * es, (c0 + cext + 1) * es)


def _ovl(a, b):
    return a[1] < b[2] and b[1] < a[2] and a[3] < b[4] and b[3] < a[4]


def _covers(a, b):
    return a[1] <= b[1] and a[2] >= b[2] and a[3] <= b[3] and a[4] >= b[4]


class Op:
    __slots__ = ("eng", "fn", "waits", "sig", "sem", "val", "is_dma", "idx")

    def __init__(self, eng, fn, is_dma=False):
        self.eng = eng
        self.fn = fn
        self.waits = {}
        self.sig = False
        self.sem = None
        self.val = None
        self.is_dma = is_dma


class Sched:
    def __init__(self, nc):
        self.nc = nc
        self.ops = {e: [] for e in (PE, ACT, DVE, POOL, SP)}
        self.recs = {}
        self.readonly = set()
        self.dma_rr = 0
        self.dma_rr_pool = 0
        self.dma_last = [None] * N_DMA_SEMS
        self.dma_cnt = [0] * N_DMA_SEMS
        self.n_ops = 0

    def _dep(self, op, src):
        if src is op:
            return
        if src.eng == PE and op.eng == PE and not src.is_dma and not op.is_dma:
            return
        src.sig = True
        key = id(src) if src.is_dma else src.eng
        cur = op.waits.get(key)
        if cur is None or cur.idx < src.idx:
            op.waits[key] = src

    def _track(self, op, reads, writes):
        for ap in reads:
            b = _box(ap)
            if b[0] in self.readonly:
                continue
            ps = (b[0] == "psum")
            if ps:
                b = (b[0], 0, 128, b[3] // 2048 * 2048, (b[4] + 2047) // 2048 * 2048)
            lst = self.recs.setdefault(b[0], [])
            merged = False
            for rec in lst:
                if rec[1] or (ps and rec[2].eng != op.eng):
                    if _ovl(rec[0], b):
                        self._dep(op, rec[2])
                if (not rec[1]) and (not op.is_dma) and (not merged) and rec[2].eng == op.eng \
                        and (not rec[2].is_dma) and rec[0] == b:
                    rec[2] = op
                    merged = True
            if not merged:
                lst.append([b, False, op])
        for ap in writes:
            b = _box(ap)
            if b[0] == "psum":
                b = (b[0], 0, 128, b[3] // 2048 * 2048, (b[4] + 2047) // 2048 * 2048)
            lst = self.recs.setdefault(b[0], [])
            keep = []
            for rec in lst:
                if _ovl(rec[0], b):
                    if rec[2] is not op:
                        self._dep(op, rec[2])
                        if _covers(b, rec[0]):
                            continue
                keep.append(rec)
            keep.append([b, True, op])
            self.recs[b[0]] = keep

    def add(self, eng, fn, reads=(), writes=()):
        op = Op(eng, fn)
        op.idx = self.n_ops
        self.n_ops += 1
        self._track(op, reads, writes)
        self.ops[eng].append(op)
        return op

    def dma(self, out, in_, eng=SP, **kw):
        op = Op(eng, None, is_dma=True)
        op.idx = self.n_ops
        self.n_ops += 1
        half = N_DMA_SEMS // 2
        if eng == POOL:
            slot = half + self.dma_rr_pool
            self.dma_rr_pool = (self.dma_rr_pool + 1) % half
        else:
            slot = self.dma_rr
            self.dma_rr = (self.dma_rr + 1) % half
        prev = self.dma_last[slot]
        if prev is not None:
            op.waits[id(prev)] = prev
        self.dma_cnt[slot] += 16
        op.sem = slot
        op.val = self.dma_cnt[slot]
        op.sig = True
        self.dma_last[slot] = op
        op.fn = (out, in_, kw)
        self._track(op, [in_], [out])
        self.ops[eng].append(op)
        return op

    def collective(self, kind, groups, in_ap, out_ap):
        op = self.dma(out_ap, in_ap, eng=POOL)
        op.fn = (out_ap, in_ap, {"_cc": (kind, groups)})
        return op

    def emit(self):
        nc = self.nc
        with contextlib.ExitStack() as st:
            esem = {e: st.enter_context(nc.semaphore("s_" + e)) for e in (PE, ACT, DVE, POOL)}
            dsem = [st.enter_context(nc.semaphore("d%d" % i)) for i in range(N_DMA_SEMS)]
            for e in (PE, ACT, DVE, POOL, SP):
                n = 0
                for op in self.ops[e]:
                    if op.is_dma:
                        continue
                    if op.sig:
                        n += 1
                        op.sem = e
                        op.val = n
            block = st.enter_context(nc.Block())

            def run(e, eng):
                waited = {}
                for op in self.ops[e]:
                    for src in op.waits.values():
                        if src.is_dma:
                            sem = dsem[src.sem]
                            k = ("d", src.sem)
                        else:
                            sem = esem[src.sem]
                            k = src.sem
                        if waited.get(k, 0) >= src.val:
                            continue
                        waited[k] = src.val
                        eng.wait_ge(sem, src.val)
                    if op.is_dma:
                        out, in_, kw = op.fn
                        if "_cc" in kw:
                            kind, groups = kw["_cc"]
                            eng.collective_compute(kind, mybir.AluOpType.bypass, replica_groups=groups,
                                                   ins=[in_], outs=[out]).then_inc(dsem[op.sem], 16)
                        else:
                            eng.dma_start(out=out, in_=in_, **kw).then_inc(dsem[op.sem], 16)
                    else:
                        ins = op.fn(eng)
                        if op.sig:
                            ins.then_inc(esem[e], 1)
                for slot in range(N_DMA_SEMS):
                    last = self.dma_last[slot]
                    if last is not None and last.eng == e:
                        eng.wait_ge(dsem[slot], last.val)

            @block.tensor
            def _(eng):
                run(PE, eng)

            @block.scalar
            def _(eng):
                run(ACT, eng)

            @block.vector
            def _(eng):
                run(DVE, eng)

            @block.gpsimd
            def _(eng):
                run(POOL, eng)

            @block.sync
            def _(eng):
                run(SP, eng)


class Prog:
    def __init__(self):
        self.nc = bass.Bass("TRN2", target_bir_lowering=False)
        self.st = contextlib.ExitStack()
        self.arena = self.st.enter_context(self.nc.sbuf_tensor("arena", [128, ARENA], U8))
        self.psum = self.st.enter_context(self.nc.psum_tensor("psum", [128, 4096], F32))
        self.S = Sched(self.nc)
        self.top = 0
        self.inputs = {}
        self.outputs = []

    def din(self, name, shape, dtype=F32):
        self.S.readonly.add(name)
        return self.nc.dram_tensor(name, list(shape), dtype, kind="ExternalInput").ap()

    def dout(self, name, shape, dtype=F32):
        self.outputs.append(name)
        return self.nc.dram_tensor(name, list(shape), dtype, kind="ExternalOutput").ap()

    def dscr(self, name, shape, dtype=F32):
        return self.nc.dram_tensor(name, list(shape), dtype, kind="Internal").ap()

    def alloc(self, shape, dtype):
        n = 1
        for s in shape[1:]:
            n *= int(s)
        nbytes = n * _esize(dtype)
        nbytes = (nbytes + 63) // 64 * 64
        assert self.top + nbytes <= ARENA, ("arena overflow", self.top, nbytes, shape)
        v = self.arena[:, self.top:self.top + nbytes].bitcast(dtype)[:, 0:n]
        self.top += nbytes
        if len(shape) > 2:
            names = " ".join("d%d" % i for i in range(1, len(shape)))
            kw = {"d%d" % i: int(shape[i]) for i in range(1, len(shape))}
            v = v.rearrange("p (%s) -> p %s" % (names, names), **kw)
        if int(shape[0]) < 128:
            v = v[0:int(shape[0])]
        return v

    def mark(self):
        return self.top

    def release(self, m):
        self.top = m

    def bank(self, i, n=512, dtype=F32):
        if dtype == F32:
            return self.psum[:, i * 512:i * 512 + n]
        v = self.psum[:, i * 512:(i + 1) * 512].bitcast(dtype)
        return v[:, 0:n]

    def dma(self, out, in_, eng=SP, **kw):
        return self.S.dma(out, in_, eng=eng, **kw)

    def mm(self, out, lhsT, rhs, start=True, stop=True, skip=False):
        if skip:
            return self.S.add(PE, lambda e: e.matmul(out, lhsT, rhs, start=start, stop=stop,
                                                     skip_group_check=True),
                              reads=[lhsT, rhs], writes=[out])
        return self.S.add(PE, lambda e: e.matmul(out, lhsT, rhs, start=start, stop=stop),
                          reads=[lhsT, rhs], writes=[out])

    def transpose(self, out, in_, ident):
        return self.S.add(PE, lambda e: e.transpose(out, in_, ident),
                          reads=[in_, ident], writes=[out])

    def act(self, out, in_, func, bias=None, scale=None, accum_out=None, eng=ACT):
        kw = {}
        reads = [in_]
        writes = [out]
        if bias is not None:
            kw["bias"] = bias
            if not isinstance(bias, (int, float)):
                reads.append(bias)
        if scale is not None:
            kw["scale"] = scale
            if not isinstance(scale, (int, float)):
                reads.append(scale)
        if accum_out is not None:
            kw["accum_out"] = accum_out
            writes.append(accum_out)
        return self.S.add(eng, lambda e: e.activation(out=out, in_=in_, func=func, **kw),
                          reads=reads, writes=writes)

    def tt(self, out, in0, in1, op, eng=DVE):
        return self.S.add(eng, lambda e: e.tensor_tensor(out=out, in0=in0, in1=in1, op=op),
                          reads=[in0, in1], writes=[out])

    def ts(self, out, in0, s1, s2, op0, op1=None, eng=DVE, accum_out=None):
        reads = [in0]
        writes = [out]
        for s in (s1, s2):
            if s is not None and not isinstance(s, (int, float)):
                reads.append(s)
        kw = {}
        if op1 is not None:
            kw["op1"] = op1
        if accum_out is not None:
            kw["accum_out"] = accum_out
            writes.append(accum_out)
        return self.S.add(eng, lambda e: e.tensor_scalar(out=out, in0=in0, scalar1=s1, scalar2=s2,
                                                         op0=op0, **kw),
                          reads=reads, writes=writes)

    def stt(self, out, in0, scalar, in1, op0, op1, eng=DVE):
        reads = [in0, in1]
        if not isinstance(scalar, (int, float)):
            reads.append(scalar)
        return self.S.add(eng, lambda e: e.scalar_tensor_tensor(out=out, in0=in0, scalar=scalar, in1=in1,
                                                                op0=op0, op1=op1),
                          reads=reads, writes=[out])

    def copy(self, out, in_, eng=DVE):
        if eng == ACT:
            return self.act(out, in_, AF.Copy)
        return self.S.add(eng, lambda e: e.tensor_copy(out=out, in_=in_), reads=[in_], writes=[out])

    def memset(self, out, val, eng=DVE):
        return self.S.add(eng, lambda e: e.memset(out, val), writes=[out])

    def recip(self, out, in_):
        return self.S.add(DVE, lambda e: e.reciprocal(out=out, in_=in_), reads=[in_], writes=[out])

    def finish(self):
        self.S.emit()
        return self.nc
D = 2048
KC = 16
EPS = 1e-6


def bcast_mid(ap2, n):
    a = ap2.ap
    return bass.AP(ap2.tensor, ap2.offset, [list(a[0]), [0, n]] + [list(x) for x in a[1:]])


def bcast_last(ap2, n):
    a = ap2.ap
    return bass.AP(ap2.tensor, ap2.offset, [list(x) for x in a] + [[0, n]])


def load_consts(P, ones_d, rot_d=None, ident_d=None):
    c = {}
    c["ones"] = P.alloc([128, 128], BF16)
    P.dma(c["ones"], ones_d)
    if rot_d is not None:
        c["rot"] = P.alloc([128, 128], BF16)
        P.dma(c["rot"], rot_d)
    if ident_d is not None:
        c["ident"] = P.alloc([128, 128], BF16)
        P.dma(c["ident"], ident_d)
    c["eps"] = P.alloc([128, 1], F32)
    P.memset(c["eps"], EPS)
    return c


def sumsq_rstd(P, consts, src_chunks, w, nfeat, bank_i, rstd_out, sq_tmp):
    n = len(src_chunks)
    bk = P.bank(bank_i)[:, 0:w]
    for i, s in enumerate(src_chunks):
        P.act(sq_tmp[:, i, 0:w], s, AF.Square)
    for i in range(n):
        P.mm(bk, consts["ones"], sq_tmp[:, i, 0:w], start=(i == 0), stop=(i == n - 1))
    P.act(rstd_out, bk, AF.Sqrt, bias=consts["eps"], scale=1.0 / nfeat)
    P.recip(rstd_out, rstd_out)


def modnorm(P, consts, segs, gam, modsb, sc_idx, sh_idx, hT):
    m0 = P.mark()
    ab = {}
    for (_, _, _, j) in segs:
        if j in ab:
            continue
        a = P.alloc([128, KC], F32)
        b = P.alloc([128, KC], F32)
        P.stt(a, modsb[:, sc_idx * KC:(sc_idx + 1) * KC, j], 1.0, gam, ALU.add, ALU.mult)
        P.copy(b, modsb[:, sh_idx * KC:(sh_idx + 1) * KC, j])
        ab[j] = (a, b)
    xts = [P.alloc([128, KC, 512], F32) for _ in range(2)]
    sq = P.alloc([128, KC, 512], BF16)
    rstd = P.alloc([128, 512], F32)
    ti = 0
    for (xd, col0, n, j) in segs:
        xv = xd.rearrange("(c p) t -> p c t", p=128)
        a, b = ab[j]
        for t0 in range(0, n, 512):
            w = min(512, n - t0)
            xt = xts[ti % 2]
            ti += 1
            P.dma(xt[:, :, 0:w], xv[:, :, t0:t0 + w])
            sumsq_rstd(P, consts, [xt[:, c, 0:w] for c in range(KC)], w, D, 7, rstd[:, 0:w], sq)
            P.tt(xt[:, :, 0:w], xt[:, :, 0:w], bcast_mid(rstd[:, 0:w], KC), ALU.mult)
            for c in range(KC):
                P.act(hT[:, c, col0 + t0:col0 + t0 + w], xt[:, c, 0:w], AF.Identity,
                      bias=b[:, c:c + 1], scale=a[:, c:c + 1])
    P.release(m0)


def gemm(P, hT, kc_n, Wd, chunks, tiles, epilogue, nbanks=3, bank0=0, wslots=None):
    Wv = Wd.rearrange("(kc p) n -> p kc n", p=128)
    own = wslots is None
    if own:
        wslots = [P.alloc([128, kc_n, 128], BF16) for _ in range(3)]
    bi = 0
    for ci, n in enumerate(chunks):
        wt = wslots[ci % len(wslots)]
        P.dma(wt, Wv[:, :, n * 128:(n + 1) * 128], eng=POOL)
        for ti, (a, b) in enumerate(tiles):
            bk = P.bank(bank0 + bi % nbanks)[:, 0:b - a]
            bi += 1
            for k in range(kc_n):
                P.mm(bk, wt[:, k, :], hT[:, k, a:b], start=(k == 0), stop=(k == kc_n - 1))
            epilogue(n, ti, a, b, bk)


def qk_epilogue(P, consts, bk, w, g_col, cos, sin, out_bf, tmp, bank_ss, bank_pq):
    sq, qg, t1, t2, rstd = tmp
    if g_col is not None:
        P.act(sq[:, 0:w], bk, AF.Square)
        P.ts(qg[:, 0:w], bk, g_col, None, ALU.mult)
        ss = P.bank(bank_ss)[:, 0:w]
        P.mm(ss, consts["ones"], sq[:, 0:w])
        P.act(rstd[:, 0:w], ss, AF.Sqrt, bias=consts["eps"], scale=1.0 / 128.0)
        P.recip(rstd[:, 0:w], rstd[:, 0:w])
    else:
        P.copy(qg[:, 0:w], bk)
    if cos is not None:
        pq = P.bank(bank_pq)[:, 0:w]
        P.mm(pq, consts["rot"], qg[:, 0:w])
        P.tt(t1[:, 0:w], qg[:, 0:w], cos, ALU.mult)
        P.tt(t2[:, 0:w], pq, sin, ALU.mult)
        if g_col is not None:
            P.tt(t1[:, 0:w], t1[:, 0:w], t2[:, 0:w], ALU.add)
            P.tt(out_bf, t1[:, 0:w], rstd[:, 0:w], ALU.mult)
        else:
            P.tt(out_bf, t1[:, 0:w], t2[:, 0:w], ALU.add)
    else:
        if g_col is not None:
            P.tt(out_bf, qg[:, 0:w], rstd[:, 0:w], ALU.mult)
        else:
            P.copy(out_bf, qg[:, 0:w])


def resid(P, consts, segs, gam, modsb, g_idx):
    m0 = P.mark()
    gg = {}
    for seg in segs:
        j = seg[4]
        if j not in gg:
            g = P.alloc([128, KC], F32)
            P.tt(g, modsb[:, g_idx * KC:(g_idx + 1) * KC, j], gam, ALU.mult)
            gg[j] = g
    xts = [P.alloc([128, KC, 512], F32) for _ in range(2)]
    yts = [P.alloc([128, KC, 512], F32) for _ in range(2)]
    sq = P.alloc([128, KC, 512], BF16)
    rstd = P.alloc([128, 512], F32)
    ti = 0
    for (xin, yd, xout, n, j) in segs:
        xv = xin.rearrange("(c p) t -> p c t", p=128)
        yv = yd.rearrange("(c p) t -> p c t", p=128)
        ov = xout.rearrange("(c p) t -> p c t", p=128)
        g = gg[j]
        for t0 in range(0, n, 512):
            w = min(512, n - t0)
            xt = xts[ti % 2]
            yt = yts[ti % 2]
            ti += 1
            P.dma(xt[:, :, 0:w], xv[:, :, t0:t0 + w])
            P.dma(yt[:, :, 0:w], yv[:, :, t0:t0 + w])
            sumsq_rstd(P, consts, [yt[:, c, 0:w] for c in range(KC)], w, D, 7, rstd[:, 0:w], sq)
            P.tt(yt[:, :, 0:w], yt[:, :, 0:w], bcast_mid(rstd[:, 0:w], KC), ALU.mult)
            for c in range(KC):
                P.stt(xt[:, c, 0:w], yt[:, c, 0:w], g[:, c:c + 1], xt[:, c, 0:w], ALU.mult, ALU.add)
            P.dma(ov[:, :, t0:t0 + w], xt[:, :, 0:w])
    P.release(m0)
TOWN = 2048
NCTX = 256
TA = TOWN + 2
TALL = TA + NCTX


def load_vec(P, d, shape, dtype=F32):
    t = P.alloc(shape, dtype)
    P.dma(t, d)
    return t


def mod_phase(P, consts, cT_d, ada_w_d, adab_d, modT_out):
    m0 = P.mark()
    cs = P.alloc([128, KC, 2], F32)
    P.dma(cs, cT_d)
    sc = P.alloc([128, KC, 2], BF16)
    P.act(sc, cs, AF.Silu)
    wslots = [P.alloc([128, KC, 128], BF16) for _ in range(4)]
    for i in range(2):
        Wv = ada_w_d[i].rearrange("(kc p) n -> p kc n", p=128)
        adab = P.alloc([128, 96], F32)
        P.dma(adab, adab_d[i])
        bk = P.bank(6)
        for k in range(96):
            wt = wslots[k % 4]
            P.dma(wt, Wv[:, :, k * 128:(k + 1) * 128], eng=POOL)
            for c in range(KC):
                P.mm(bk[:, 2 * k:2 * k + 2], wt[:, c, :], sc[:, c, :], start=(c == 0), stop=(c == KC - 1))
        msb = P.alloc([128, 96, 2], F32)
        bv = bk[:, 0:192].rearrange("p (k j) -> p k j", j=2)
        for j in range(2):
            P.tt(msb[:, :, j], bv[:, :, j], adab, ALU.add)
        P.dma(modT_out[i], msb.rearrange("p k j -> p (k j)"))
    P.release(m0)


def build_A(stage=9, qchunks=None, hchunks=None):
    P = Prog()
    xT = P.din("xT", [D, TA])
    ctxT = P.din("ctxT", [D, NCTX])
    cT = P.din("cT", [128, KC, 2])
    ada_w = P.din("ada_w", [2, D, 6 * D])
    adab = P.din("adab", [2, 128, 96])
    ng = P.din("ng0", [128, KC])
    w_in = P.din("ab_w_in", [D, 4608])
    qkg = P.din("qkg", [128, 2])
    cosd = P.din("cos128", [128, TOWN])
    sind = P.din("sin128", [128, TOWN])
    ones_d = P.din("ones", [128, 128], BF16)
    rot_d = P.din("rot", [128, 128], BF16)
    cw_d = P.din("hy_cw", [128, 24, 3])
    cb_d = P.din("hy_cb", [128, 24])
    em_d = P.din("edge", [128, 2])
    modT = P.dout("modT", [2, 128, 192])
    qT = P.dout("qT", [1024, TOWN], BF16)
    kT = P.dout("kT", [256, TOWN], BF16)
    vT = P.dout("vT", [256, TOWN], BF16)
    hyT = P.dout("hyT", [3072, TOWN], BF16)
    qcT = P.dout("qcT", [1024, NCTX], BF16)
    kcT = P.dout("kcT", [256, NCTX], BF16)
    vcT = P.dout("vcT", [256, NCTX], BF16)
    hycT = P.dout("hycT", [3072, NCTX], BF16)

    consts = load_consts(P, ones_d, rot_d)
    mod_phase(P, consts, cT, ada_w, adab, modT)
    if stage <= 1:
        return P
    modsb = P.alloc([128, 96, 2], F32)
    P.dma(modsb.rearrange("p k j -> p (k j)"), modT[0])
    gam = load_vec(P, ng, [128, KC])
    hT = P.alloc([128, KC, TALL], BF16)
    modnorm(P, consts, [(xT, 0, TA, 0), (ctxT, TA, NCTX, 1)], gam, modsb, 1, 0, hT)
    if stage <= 2:
        return P

    g2 = load_vec(P, qkg, [128, 2])
    cos = load_vec(P, cosd, [128, TOWN])
    sin = load_vec(P, sind, [128, TOWN])
    cw = load_vec(P, cw_d, [128, 24, 3])
    cb = load_vec(P, cb_d, [128, 24])
    em = load_vec(P, em_d, [128, 2])
    tmp = (P.alloc([128, 512], BF16), P.alloc([128, 512], BF16), P.alloc([128, 512], F32),
           P.alloc([128, 512], F32), P.alloc([128, 512], F32))
    osts = [P.alloc([128, 512], BF16) for _ in range(3)]
    oi = [0]
    lat_tiles = [(1 + 512 * i, 1 + 512 * (i + 1)) for i in range(4)]
    ctx_tile = (TA, TALL)

    def ep_qkv(n, ti, a, b, bk):
        w = b - a
        ost = osts[oi[0] % 3]
        oi[0] += 1
        is_ctx = (ti == 4)
        if n < 8:
            dst = (qcT if is_ctx else qT)[n * 128:(n + 1) * 128]
            gc = g2[:, 0:1]
        elif n < 10:
            dst = (kcT if is_ctx else kT)[(n - 8) * 128:(n - 7) * 128]
            gc = g2[:, 1:2]
        else:
            dst = (vcT if is_ctx else vT)[(n - 10) * 128:(n - 9) * 128]
            gc = None
        if gc is None:
            P.act(ost[:, 0:w], bk, AF.Copy)
        elif is_ctx:
            qk_epilogue(P, consts, bk, w, gc, None, None, ost[:, 0:w], tmp, 3, 5)
        else:
            qk_epilogue(P, consts, bk, w, gc, cos[:, a - 1:b - 1], sin[:, a - 1:b - 1], ost[:, 0:w], tmp,
                        3 + ti % 2, 5 + ti % 2)
        if is_ctx:
            P.dma(dst[:, 0:w], ost[:, 0:w])
        else:
            P.dma(dst[:, a - 1:b - 1], ost[:, 0:w])

    gemm(P, hT, KC, w_in, list(range(12)) if qchunks is None else qchunks, lat_tiles + [ctx_tile], ep_qkv)
    if stage <= 3:
        return P

    U = [P.alloc([128, TALL + 2], F32) for _ in range(2)]
    V = P.alloc([128, TALL], F32)
    Vb = [P.alloc([128, TALL], BF16) for _ in range(2)]
    for u in U:
        P.memset(u[:, TA:TA + 1], 0.0)
        P.memset(u[:, TALL + 1:TALL + 2], 0.0)
    hy_tiles = [(512 * i, 512 * (i + 1)) for i in range(4)] + [(2048, TA), (TA, TALL)]

    def ep_hy(n, ti, a, b, bk):
        hc = n - 12
        u = U[hc % 2]
        if ti < 5:
            P.act(u[:, a:b], bk, AF.Copy)
        else:
            P.act(u[:, TA + 1:TALL + 1], bk, AF.Copy)
            vb = Vb[hc % 2]
            P.ts(u[:, 0:1], u[:, 0:1], em[:, 0:1], None, ALU.mult)
            P.ts(u[:, TA - 1:TA], u[:, TA - 1:TA], em[:, 1:2], None, ALU.mult)
            P.act(V, u[:, 1:TALL + 1], AF.Identity, bias=cb[:, hc:hc + 1], scale=cw[:, hc, 1:2])
            P.stt(V, u[:, 0:TALL], cw[:, hc, 0:1], V, ALU.mult, ALU.add)
            P.stt(vb, u[:, 2:TALL + 2], cw[:, hc, 2:3], V, ALU.mult, ALU.add)
            P.dma(hyT[hc * 128:(hc + 1) * 128, :], vb[:, 0:TOWN])
            P.dma(hycT[hc * 128:(hc + 1) * 128, :], vb[:, TA:TA + NCTX])

    gemm(P, hT, KC, w_in, list(range(12, 36)) if hchunks is None else hchunks, hy_tiles, ep_hy)
    return P
def attention(P, consts, qT_d, kT_d, v_d, oT_d, nheads, kv_of, nmaps, Tq, S, scale, finish, vw=128):
    m0 = P.mark()
    QW = min(512 // nmaps, Tq)
    dk = 128 // nmaps
    nkt = S // 128
    nqs = QW // 128
    KTs = [P.alloc([128, S], BF16) for _ in range(2)]
    VAs = [P.alloc([128, nkt, 129], BF16) for _ in range(2)]
    for va in VAs:
        P.memset(va[:, :, 128:129], 1.0)
    QTs = [P.alloc([128, QW], BF16) for _ in range(2)]
    PTs = [P.alloc([128, 512], BF16) for _ in range(3)]
    obf = [P.alloc([128, 128], BF16) for _ in range(2)]
    oTs = [P.alloc([128, QW], BF16) for _ in range(2)]
    vv = v_d.rearrange("(kt p) e -> p kt e", p=128)
    cur_kv = None
    kvi = 0
    qi = 0
    pi = 0
    oi = 0
    for h in range(nheads):
        kv = kv_of(h)
        if kv != cur_kv:
            KT = KTs[kvi % 2]
            VA = VAs[kvi % 2]
            kvi += 1
            cur_kv = kv
            P.dma(KT, kT_d[kv * 128:(kv + 1) * 128, :])
            P.dma(VA[:, :, 0:128], vv[:, :, kv * vw:kv * vw + 128])
        for qt in range(Tq // QW):
            QT = QTs[qi % 2]
            obase = 4 + 2 * (qi % 2)
            qi += 1
            P.dma(QT, qT_d[h * 128:(h + 1) * 128, qt * QW:(qt + 1) * QW])
            def Oacc(m, qs):
                idx = m * nqs + qs
                return P.bank(obase + idx // 2)[:, (idx % 2) * 256:(idx % 2) * 256 + 129]
            def sbank(kt, m):
                return P.bank(kt % 3) if nmaps == 1 else P.bank(2 * m + kt % 2)

            def issue_qk(kt):
                for m in range(nmaps):
                    P.mm(sbank(kt, m)[:, 0:QW], KT[m * dk:(m + 1) * dk, kt * 128:(kt + 1) * 128],
                         QT[m * dk:(m + 1) * dk, :])

            issue_qk(0)
            for kt in range(nkt):
                if kt + 1 < nkt:
                    issue_qk(kt + 1)
                PT = PTs[pi % 3]
                pi += 1
                for m in range(nmaps):
                    P.act(PT[:, m * QW:(m + 1) * QW], sbank(kt, m)[:, 0:QW], AF.Exp, scale=scale)
                for m in range(nmaps):
                    for qs in range(nqs):
                        P.mm(Oacc(m, qs), PT[:, m * QW + qs * 128:m * QW + (qs + 1) * 128], VA[:, kt, :],
                             start=(kt == 0 and (m * nqs + qs) % 2 == 0), stop=(kt == nkt - 1), skip=True)
            oT = oTs[oi % 2]
            oi += 1
            tb = P.bank(3, 1024, BF16)
            for qs in range(nqs):
                ob = obf[qs % 2]
                finish([Oacc(m, qs) for m in range(nmaps)], ob)
                P.transpose(tb[:, qs * 128:(qs + 1) * 128], ob, consts["ident"])
            P.copy(oT, tb[:, 0:QW])
            P.dma(oT_d[h * 128:(h + 1) * 128, qt * QW:(qt + 1) * QW], oT)
    P.release(m0)


def make_finish_gqa(P):
    r = P.alloc([128, 1], F32)

    def finish(Os, ob):
        O = Os[0]
        P.recip(r, O[:, 128:129])
        P.ts(ob, O[:, 0:128], r, None, ALU.mult)
    return finish


def make_finish_diff(P, consts, lam_bc_d, sg_bc_d, lambda_init):
    lv = P.alloc([128, 4, 64], F32)
    P.dma(lv, lam_bc_d)
    pr = P.alloc([128, 2, 64], F32)
    lvv = lv.rearrange("p (a b) d -> p a b d", b=2)
    P.tt(pr, lvv[:, :, 0, :], lvv[:, :, 1, :], ALU.mult)
    s2 = P.alloc([128, 2], F32)
    P.S.add(DVE, lambda e: e.reduce_sum(out=s2, in_=pr, axis=AX.X), reads=[pr], writes=[s2])
    e2 = P.alloc([128, 2], F32)
    P.act(e2, s2, AF.Exp)
    lam = P.alloc([128, 1], F32)
    P.tt(lam, e2[:, 0:1], e2[:, 1:2], ALU.subtract)
    P.ts(lam, lam, float(lambda_init), None, ALU.add)
    sg = P.alloc([128, 128], F32)
    P.dma(sg, sg_bc_d)
    P.ts(sg, sg, float(1.0 - lambda_init), None, ALU.mult)
    r0 = P.alloc([128, 1], F32)
    r1 = P.alloc([128, 1], F32)
    ss = P.alloc([128, 1], F32)
    t = P.alloc([128, 128], F32)
    o = P.alloc([128, 128], F32)
    junk = P.alloc([128, 128], F32)

    def finish(Os, ob):
        O0, O1 = Os
        P.recip(r0, O0[:, 128:129])
        P.recip(r1, O1[:, 128:129])
        P.tt(r1, r1, lam, ALU.mult)
        P.ts(t, O1[:, 0:128], r1, None, ALU.mult)
        P.stt(o, O0[:, 0:128], r0, t, ALU.mult, ALU.subtract)
        P.act(junk, o, AF.Square, accum_out=ss)
        P.act(ss, ss, AF.Sqrt, bias=consts["eps"], scale=1.0 / 128.0)
        P.recip(ss, ss)
        P.stt(ob, o, ss, sg, ALU.mult, ALU.mult)
    return finish
HC = 32
MAGIC = 12582912.0


def hy_host_tables(S, SD):
    N = 128 * S
    n2 = np.arange(S)[:, None]
    k2 = np.arange(S)[None, :]
    a = 2 * np.pi * (n2 * k2 % S) / S
    FA = np.concatenate([np.cos(a), -np.sin(a)], axis=1)
    n1 = np.arange(128)[:, None, None]
    kk2 = np.arange(S)[None, :, None]
    k1 = np.arange(128)[None, None, :]
    ph = 2 * np.pi * ((n1 * (S * k1 + kk2)) % N) / N
    GT = np.stack([np.cos(ph), -np.sin(ph), np.sin(ph)], axis=2)
    kk1 = np.arange(128)[:, None]
    nn1 = np.arange(128)[None, :]
    th = 2 * np.pi * ((kk1 * nn1) % 128) / 128
    CI = np.stack([np.concatenate([np.cos(th), np.sin(th)], 1),
                   np.concatenate([-np.sin(th), np.cos(th)], 1)], axis=1)
    ek2 = np.arange(S)[:, None, None]
    en1 = np.arange(128)[None, :, None]
    en2 = np.arange(SD)[None, None, :]
    ps = 2 * np.pi * ((ek2 * (en1 + 128 * en2)) % N) / N
    ET = np.stack([np.cos(ps), -np.sin(ps)], axis=2)
    return {"FA": FA.astype(NPBF), "GT": GT.astype(NPBF), "CI": CI.astype(NPBF), "ET": ET.astype(NPBF)}


def hy_host_filter_consts(n_tok, ch0, nch):
    HY_BANDS, HY_CH = 16, 1024
    f32 = np.float32
    t = np.arange(n_tok, dtype=f32)
    t_norm = (t / f32(max(n_tok - 1, 1))).astype(f32)
    w = (f32(2.0 * np.pi / n_tok) * t).astype(f32)
    bands = np.linspace(1e-4, HY_BANDS - 1, HY_BANDS, dtype=f32)
    z = (w[:, None] * bands).astype(f32)
    feats = np.concatenate([t_norm[:, None], np.cos(z), -np.sin(z)], axis=-1).astype(f32)
    deltas = np.abs(np.linspace(np.log(1e-2) / 0.3, np.log(1e-2) / 1.5, HY_CH, dtype=f32)).astype(f32)
    decay = np.exp(-t_norm[:, None] * deltas[None, ch0:ch0 + nch]).astype(f32)
    N = 2 * n_tok
    idx = np.zeros(N, np.int64)
    idx[:n_tok] = np.arange(n_tok)
    idx[n_tok + 1:] = N - np.arange(n_tok + 1, N)
    featsT = np.ascontiguousarray(feats[idx].T)
    decF = np.zeros((N, nch), f32)
    decB = np.zeros((N, nch), f32)
    decF[:n_tok] = decay
    decB[n_tok + 1:] = decay[idx[n_tok + 1:]]
    S = N // 128
    ng = nch // HC

    def lay(d):
        return np.ascontiguousarray(d.reshape(S, 128, ng, HC).transpose(2, 0, 1, 3))
    return featsT.astype(NPBF), lay(decF), lay(decB)


def hy_load_tables(P, td, S, SD):
    T = {"S": S, "SD": SD}
    T["FA"] = P.alloc([S, 2 * S], BF16)
    P.dma(T["FA"], td["FA"])
    T["CI"] = P.alloc([128, 2, 256], BF16)
    P.dma(T["CI"], td["CI"])
    T["ET"] = P.alloc([S, 128, 2, SD], BF16)
    P.dma(T["ET"], td["ET"])
    T["GTd"] = td["GT"]
    return T


def hy_mlp(P, consts, featsT_d, N, w1_d, b1_d, w2_d, b2_d, fr_d, hid2T):
    m0 = P.mark()
    w1 = P.alloc([33, 64], BF16)
    P.dma(w1, w1_d, eng=POOL)
    w2 = P.alloc([64, 64], BF16)
    P.dma(w2, w2_d, eng=POOL)
    bb = P.alloc([64, 2], F32)
    P.dma(bb[:, 0:1], b1_d)
    P.dma(bb[:, 1:2], b2_d)
    fr = P.alloc([64, 2], F32)
    P.dma(fr, fr_d)
    sc = P.alloc([64, 2], F32)
    of = P.alloc([64, 2], F32)
    P.ts(sc, fr, float(1.0 / (2 * np.pi)), None, ALU.mult)
    P.tt(of, bb, sc, ALU.mult)
    ft = P.alloc([33, N], BF16)
    P.dma(ft, featsT_d)
    h1 = P.alloc([64, N], BF16)
    y = P.alloc([64, 512], F32)
    mm_ = P.alloc([64, 512], F32)
    for layer in range(2):
        src, wt, dst = (ft, w1, h1) if layer == 0 else (h1, w2, hid2T)
        for t0 in range(0, N, 512):
            w = min(512, N - t0)
            bk = P.bank(6 + (t0 // 512) % 2)[0:64, 0:w]
            P.mm(bk, wt, src[:, t0:t0 + w])
            P.ts(y[:, 0:w], bk, sc[:, layer:layer + 1], of[:, layer:layer + 1], ALU.mult, ALU.add)
            P.ts(mm_[:, 0:w], y[:, 0:w], MAGIC, None, ALU.add)
            P.ts(mm_[:, 0:w], mm_[:, 0:w], MAGIC, None, ALU.subtract)
            P.tt(y[:, 0:w], y[:, 0:w], mm_[:, 0:w], ALU.subtract)
            P.act(dst[:, t0:t0 + w], y[:, 0:w], AF.Sin, scale=float(2 * np.pi))
    P.release(m0)


def hy_fwd(P, T, u_sb, nblk, A_sb, gti, epilogue):
    S = T["S"]
    C = HC
    for c in range(C):
        bk = P.bank(c % 2)[:, 0:2 * S]
        P.mm(bk, u_sb[0:nblk, c, :], T["FA"][0:nblk, :])
        P.copy(A_sb[:, :, :, c], bk.rearrange("p (j k) -> p k j", j=2), eng=(DVE if c % 2 else ACT))
    KB = min(256 // C, S)
    GCH = min(16, S)
    gts = T["gts"]
    for k0 in range(0, S, KB):
        if k0 % GCH == 0:
            gt = gts[gti[0] % 2]
            gti[0] += 1
            P.dma(gt[:, 0:GCH], T["GTd"][:, k0:k0 + GCH])
        bk = P.bank(2 + (k0 // KB) % 2)[:, 0:KB * 2 * C].rearrange("p (k j c) -> p k j c", k=KB, j=2)
        for kb in range(KB):
            k2 = k0 + kb
            g = gt[:, k2 % GCH]
            P.mm(bk[:, kb, 0, :], g[:, 0, :], A_sb[:, k2, 0, :], start=True, stop=False)
            P.mm(bk[:, kb, 0, :], g[:, 2, :], A_sb[:, k2, 1, :], start=False, stop=True)
            P.mm(bk[:, kb, 1, :], g[:, 1, :], A_sb[:, k2, 0, :], start=True, stop=False)
            P.mm(bk[:, kb, 1, :], g[:, 0, :], A_sb[:, k2, 1, :], start=False, stop=True)
        epilogue(k0, KB, bk)


def hy_conv(P, T, u_sb, Hd, gate_sb, skip_bc, z_sb, bufs, gti):
    S, SD = T["S"], T["SD"]
    C = HC
    AB, Y_sb, us, hch, tmps, tz = bufs
    A_sb = AB[:, 0:S * 2 * C].rearrange("p (k j c) -> p k j c", k=S, j=2)
    KB = min(256 // C, S)

    def ep(k0, KB_, bk):
        h = hch[(k0 // KB_) % 2]
        P.dma(h[:, 0:KB_], Hd[:, k0:k0 + KB_])
        t1, t2, t3, t4 = tmps
        P.tt(t1[:, 0:KB_], bk[:, :, 0, :], h[:, 0:KB_, 0, :], ALU.mult)
        P.tt(t2[:, 0:KB_], bk[:, :, 1, :], h[:, 0:KB_, 1, :], ALU.mult)
        P.tt(t3[:, 0:KB_], bk[:, :, 0, :], h[:, 0:KB_, 1, :], ALU.mult)
        P.tt(t4[:, 0:KB_], bk[:, :, 1, :], h[:, 0:KB_, 0, :], ALU.mult)
        P.tt(Y_sb[:, 0, :, k0:k0 + KB_].rearrange("p c k -> p k c"), t1[:, 0:KB_], t2[:, 0:KB_], ALU.subtract,
             eng=POOL)
        P.tt(Y_sb[:, 1, :, k0:k0 + KB_].rearrange("p c k -> p k c"), t3[:, 0:KB_], t4[:, 0:KB_], ALU.add,
             eng=POOL)

    hy_fwd(P, T, u_sb, SD, A_sb, gti, ep)
    P.tt(us, u_sb, bcast_last(skip_bc[0:SD, :], 128), ALU.mult, eng=POOL)
    P_sb = AB[0:S, :].rearrange("p (n j c) -> p n j c", n=128, j=2)
    for c in range(C):
        bk = P.bank(c % 2)[0:S, 0:256]
        P.mm(bk, Y_sb[:, 0, c, :], T["CI"][:, 0, :], start=True, stop=False)
        P.mm(bk, Y_sb[:, 1, c, :], T["CI"][:, 1, :], start=False, stop=True)
        P.copy(P_sb[:, :, :, c], bk.rearrange("p (j n) -> p n j", j=2), eng=(DVE if c % 2 else ACT))
    NB = 512 // C
    for n0 in range(0, 128, NB):
        bk = P.bank(4 + (n0 // NB) % 2)[0:SD, 0:NB * C].rearrange("p (n c) -> p n c", n=NB)
        for nb in range(NB):
            n1 = n0 + nb
            P.mm(bk[:, nb, :], T["ET"][:, n1, 0, :], P_sb[:, n1, 0, :], start=True, stop=False)
            P.mm(bk[:, nb, :], T["ET"][:, n1, 1, :], P_sb[:, n1, 1, :], start=False, stop=True)
        P.tt(tz[0:SD], bk.rearrange("p n c -> p c n"), us[:, :, n0:n0 + NB], ALU.add)
        P.tt(z_sb[:, :, n0:n0 + NB], tz[0:SD], gate_sb[:, :, n0:n0 + NB], ALU.mult)


def hy_filter(P, consts, T, hid2T, w3sb, o, decF_d, decB_d, Hd, bufs, gti, N):
    S = T["S"]
    C = HC
    AB, filt, tmpf, dch, stage, red, rl, ab_full = bufs
    A_sb = AB[:, 0:S * 2 * C].rearrange("p (k j c) -> p k j c", k=S, j=2)
    NB = 512 // C
    hv = hid2T.rearrange("p (n2 n1) -> p n1 n2", n1=128)
    for n0 in range(0, 128, NB):
        bF = P.bank(6)[0:S, 0:NB * C].rearrange("p (n c) -> p n c", n=NB)
        bB = P.bank(7)[0:S, 0:NB * C].rearrange("p (n c) -> p n c", n=NB)
        dF, dB = dch[(n0 // NB) % 2]
        P.dma(dF, decF_d[:, n0:n0 + NB, :])
        P.dma(dB, decB_d[:, n0:n0 + NB, :])
        for nb in range(NB):
            P.mm(bF[:, nb, :], hv[:, n0 + nb, :], w3sb[:, 0, :])
        for nb in range(NB):
            P.mm(bB[:, nb, :], hv[:, n0 + nb, :], w3sb[:, 1, :])
        t1 = tmpf[0][0:S]
        t2 = tmpf[1][0:S]
        P.tt(t1, bF, dF, ALU.mult)
        P.tt(t2, bB, dB, ALU.mult)
        P.tt(filt[:, :, n0:n0 + NB].rearrange("p c n -> p n c"), t1, t2, ALU.add, eng=POOL)
    ab = ab_full[0:S]
    P.act(ab, filt, AF.Abs)
    P.S.add(DVE, lambda e: e.reduce_sum(out=red[0:S, 0:C], in_=ab, axis=AX.X), reads=[ab], writes=[red[0:S, 0:C]])
    rh = red[0:S, C:2 * C].bitcast(BF16)[:, 0:C]
    rlo = red[0:S, 2 * C:3 * C].bitcast(BF16)[:, 0:C]
    rt = red[0:S, 3 * C:4 * C]
    P.copy(rh, red[0:S, 0:C])
    P.tt(rt, red[0:S, 0:C], rh, ALU.subtract)
    P.copy(rlo, rt)
    bk = P.bank(5)[:, 0:C]
    P.mm(bk, consts["ones"][0:S, :], rh, start=True, stop=False)
    P.mm(bk, consts["ones"][0:S, :], rlo, start=False, stop=True)
    P.ts(rl, bk, float(N), None, ALU.mult)
    P.recip(rl, rl)

    def ep(k0, KB_, bk2):
        st = stage[:, (k0 // KB_) % 2 * KB_:(k0 // KB_) % 2 * KB_ + KB_]
        P.tt(st.rearrange("p k j c -> p (k j) c"), bk2.rearrange("p k j c -> p (k j) c"),
             bcast_mid(rl, KB_ * 2), ALU.mult)
        P.dma(Hd[:, k0:k0 + KB_], st)

    hy_fwd(P, T, filt, S, A_sb, gti, ep)


def hyena_phase(P, consts, td, S, SD, hyv_d, featsT_d, decF_d, decB_d, wd, skipbc_d, w3_d, out_d, NCH, tag):
    N = 128 * S
    C = HC
    NG = NCH // C
    KB = min(256 // C, S)
    NB = 512 // C
    mT = P.mark()
    T = hy_load_tables(P, td, S, SD)
    T["gts"] = [P.alloc([128, min(16, S), 3, 128], BF16) for _ in range(2)]
    gti = [0]
    AB = P.alloc([128, 128 * 2 * C], BF16)
    us = P.alloc([max(S, SD), C, 128], F32)
    w3 = P.alloc([64, 2, 2, NCH], BF16)
    P.dma(w3, w3_d, eng=POOL)
    skb = P.alloc([128, 2, NCH], F32)
    P.dma(skb, skipbc_d)
    Hd = P.dscr("Hd_" + tag, [2, NG, 128, S, 2, C])
    m1 = P.mark()
    hid2T = P.alloc([64, N], BF16)
    hy_mlp(P, consts, featsT_d, N, wd["w1"], wd["b1"], wd["w2"], wd["b2"], wd["fr"], hid2T)
    filt = P.alloc([S, C, 128], BF16)
    tmpf = [P.alloc([S, NB, C], F32) for _ in range(2)]
    dch = [(P.alloc([S, NB, C], F32), P.alloc([S, NB, C], F32)) for _ in range(2)]
    stage = P.alloc([128, 2 * KB, 2, C], F32)
    red = P.alloc([128, 4 * C], F32)
    rl = P.alloc([128, C], F32)
    for o in range(2):
        for g in range(NG):
            w3g = w3[:, o, :, g * C:(g + 1) * C]
            hy_filter(P, consts, T, hid2T, w3g, o, decF_d[g], decB_d[g], Hd[o, g],
                      (AB, filt, tmpf, dch, stage, red, rl, us), gti, N)
    P.release(m1)
    Y_sb = P.alloc([128, 2, C, S], BF16)
    hch = [P.alloc([128, KB, 2, C], F32) for _ in range(2)]
    tmps = [P.alloc([128, KB, C], F32) for _ in range(4)]
    tz = P.alloc([max(SD, 1), C, NB], F32)
    xs = [[P.alloc([SD, C, 128], BF16) for _ in range(3)] for _ in range(2)]
    z1 = P.alloc([SD, C, 128], BF16)
    oo = [P.alloc([SD, C, 128], BF16) for _ in range(2)]
    bufs = (AB, Y_sb, us[0:SD], hch, tmps, tz)
    for g in range(NG):
        xv = xs[g % 2]
        for s3 in range(3):
            P.dma(xv[s3], hyv_d[s3, g * C:(g + 1) * C, :].rearrange("c (a b) -> a c b", b=128))
        hy_conv(P, T, xv[0], Hd[0, g], xv[1], skb[:, 0, g * C:(g + 1) * C], z1, bufs, gti)
        ob = oo[g % 2]
        hy_conv(P, T, z1, Hd[1, g], xv[2], skb[:, 1, g * C:(g + 1) * C], ob, bufs, gti)
        P.dma(out_d[g * C:(g + 1) * C, :].rearrange("c (a b) -> a c b", b=128), ob)
    P.release(mT)
SEQ = 8192
SKV = SEQ + NCTX
HYCH = 256


def build_B():
    P = Prog()
    qT = P.din("qT", [1024, TOWN], BF16)
    kT = P.din("kT", [256, SKV], BF16)
    v = P.din("v", [SKV, 256], BF16)
    qcT = P.din("qcT", [1024, NCTX], BF16)
    kcT = P.din("kcT", [256, NCTX], BF16)
    vc = P.din("vc", [NCTX, 256], BF16)
    ones_d = P.din("ones", [128, 128], BF16)
    ident_d = P.din("ident", [128, 128], BF16)
    oT = P.dout("oT", [1024, TOWN], BF16)
    ocT = P.dout("ocT", [1024, NCTX], BF16)
    consts = load_consts(P, ones_d, None, ident_d)
    fin = make_finish_gqa(P)
    attention(P, consts, qT, kT, v, oT, 8, lambda h: h // 4, 1, TOWN, SKV, 128 ** -0.5, fin)
    attention(P, consts, qcT, kcT, vc, ocT, 8, lambda h: h // 4, 1, NCTX, NCTX, 128 ** -0.5, fin)
    wd = {"w1": P.din("hy_w1", [33, 64]), "b1": P.din("hy_b1", [64, 1]), "w2": P.din("hy_w2", [64, 64]),
          "b2": P.din("hy_b2", [64, 1]), "fr": P.din("hy_fr", [64, 2])}
    sk_d = P.din("hy_skipbc", [128, 2, HYCH])
    w3_d = P.din("hy_w3", [64, 2, 2, HYCH])
    for tag, n_tok in (("l", SEQ), ("c", NCTX)):
        S = 2 * n_tok // 128
        SD = n_tok // 128
        td = {"FA": P.din("FA" + tag, [S, 2 * S], BF16), "GT": P.din("GT" + tag, [128, S, 3, 128], BF16),
              "CI": P.din("CI" + tag, [128, 2, 256], BF16), "ET": P.din("ET" + tag, [S, 128, 2, SD], BF16)}
        hyv = P.din("hyv" + tag, [3, HYCH, n_tok], BF16)
        ft = P.din("featsT" + tag, [33, 2 * n_tok], BF16)
        dF = P.din("decF" + tag, [HYCH // HC, S, 128, HC])
        dB = P.din("decB" + tag, [HYCH // HC, S, 128, HC])
        out = P.dout("ohy" + tag, [HYCH, n_tok], BF16)
        hyena_phase(P, consts, td, S, SD, hyv, ft, dF, dB, wd, sk_d, w3_d, out, HYCH, tag)
    return P


def build_D():
    P = Prog()
    qT = P.din("qT", [D, TOWN], BF16)
    kT = P.din("kT", [D, SKV], BF16)
    v = P.din("v", [SKV, D], BF16)
    lam_d = P.din("lam", [128, 4, 64])
    sg_d = P.din("sg", [128, 128])
    ones_d = P.din("ones", [128, 128], BF16)
    ident_d = P.din("ident", [128, 128], BF16)
    oT = P.dout("oT", [D, TOWN], BF16)
    consts = load_consts(P, ones_d, None, ident_d)
    lambda_init = 0.8 - 0.6 * float(np.exp(-0.3 * 1))
    fin = make_finish_diff(P, consts, lam_d, sg_d, lambda_init)
    attention(P, consts, qT, kT, v, oT, 16, lambda h: h, 2, TOWN, SKV, 64 ** -0.5, fin)
    return P
DFF = 5632
FC = DFF // 128


def ffn_up(P, consts, h2T, Tl, with_ctx, w_up_d, cw, cb, em, actT_d):
    m0 = P.mark()
    tall = Tl + (NCTX if with_ctx else 0)
    UW = tall + 2
    Ua = [P.alloc([128, UW], F32) for _ in range(2)]
    Ug = [P.alloc([128, UW], F32) for _ in range(2)]
    Va = P.alloc([128, tall], F32)
    Vg = P.alloc([128, tall], F32)
    ab = [P.alloc([128, tall], BF16) for _ in range(2)]
    sg = P.alloc([128, tall], F32)
    for u in Ua + Ug:
        P.memset(u[:, Tl:Tl + 1], 0.0)
        P.memset(u[:, UW - 1:UW], 0.0)
    tiles = [(512 * i, 512 * (i + 1)) for i in range((Tl - 2) // 512)] + [(Tl - 2, Tl)]
    if with_ctx:
        tiles.append((Tl, tall))
    nt = len(tiles)
    order = []
    for j in range(FC):
        order += [j, j + FC]

    def conv(u, hc, V):
        P.ts(u[:, 0:1], u[:, 0:1], em[:, 0:1], None, ALU.mult)
        P.ts(u[:, Tl - 1:Tl], u[:, Tl - 1:Tl], em[:, 1:2], None, ALU.mult)
        P.act(V, u[:, 1:tall + 1], AF.Identity, bias=cb[:, hc:hc + 1], scale=cw[:, hc, 1:2])
        P.stt(V, u[:, 0:tall], cw[:, hc, 0:1], V, ALU.mult, ALU.add)
        P.stt(V, u[:, 2:tall + 2], cw[:, hc, 2:3], V, ALU.mult, ALU.add)

    def ep(n, ti, a, b, bk):
        j = n % FC
        u = (Ua if n < FC else Ug)[j % 2]
        if with_ctx and ti == nt - 1:
            P.act(u[:, Tl + 1:tall + 1], bk, AF.Copy)
        else:
            P.act(u[:, a:b], bk, AF.Copy)
        if ti == nt - 1:
            if n < FC:
                conv(u, n, Va)
            else:
                conv(u, n, Vg)
                P.act(sg, Vg, AF.Silu)
                o = ab[j % 2]
                P.tt(o, Va, sg, ALU.mult)
                P.dma(actT_d[j * 128:(j + 1) * 128, 0:Tl - 2], o[:, 0:Tl - 2])
                if with_ctx:
                    P.dma(actT_d[j * 128:(j + 1) * 128, Tl - 2:Tl - 2 + NCTX], o[:, Tl:Tl + NCTX])

    gemm(P, h2T, KC, w_up_d, order, tiles, ep)
    P.release(m0)


def ffn_down(P, consts, actT_d, ntok, w_down_d, yT_d):
    m0 = P.mark()
    BLK = 768
    av = actT_d.rearrange("(kc p) t -> p kc t", p=128)
    act = P.alloc([128, FC, BLK], BF16)
    wsl = [P.alloc([128, FC, 128], BF16) for _ in range(3)]
    st = [P.alloc([128, 512], F32) for _ in range(3)]
    si = [0]
    for b0 in range(0, ntok, BLK):
        bw = min(BLK, ntok - b0)
        P.dma(act[:, :, 0:bw], av[:, :, b0:b0 + bw])
        tiles = [(a, min(a + 512, bw)) for a in range(0, bw, 512)]

        def ep(n, ti, a, b, bk):
            s = st[si[0] % 3]
            si[0] += 1
            P.copy(s[:, 0:b - a], bk, eng=(ACT if si[0] % 2 else DVE))
            P.dma(yT_d[n * 128:(n + 1) * 128, b0 + a:b0 + b], s[:, 0:b - a])

        gemm(P, act, FC, w_down_d, list(range(KC)), tiles, ep, wslots=wsl)
    P.release(m0)


def build_post(layer):
    with_ctx = (layer == 0)
    P = Prog()
    Tl = TA
    tall = Tl + (NCTX if with_ctx else 0)
    ntok = TOWN + (NCTX if with_ctx else 0)
    oT = P.din("oT", [D, Tl], BF16)
    xT = P.din("xT", [D, Tl])
    modT = P.din("modT", [128, 192])
    ngs = P.din("ngs", [3, 128, KC])
    w_out = P.din("w_out", [D, D])
    w_up = P.din("w_up", [D, 2 * DFF])
    w_down = P.din("w_down", [DFF, D])
    fcw = P.din("fcw", [128, 2 * FC, 3])
    fcb = P.din("fcb", [128, 2 * FC])
    em_d = P.din("edge", [128, 2])
    ones_d = P.din("ones", [128, 128], BF16)
    rot_d = P.din("rot", [128, 128], BF16)
    if with_ctx:
        ocT = P.din("ocT", [D, NCTX], BF16)
        ctxT = P.din("ctxT", [D, NCTX])
    yT = P.dscr("yT", [D, tall])
    xmT = P.dscr("xmT", [D, tall])
    actT = P.dscr("actT", [DFF, ntok], BF16)
    y2T = P.dscr("y2T", [D, ntok])
    xoT = P.dout("xoT", [D, TOWN])
    if with_ctx:
        cxoT = P.dout("cxoT", [D, NCTX])

    consts = load_consts(P, ones_d, rot_d)
    modsb = P.alloc([128, 96, 2], F32)
    P.dma(modsb.rearrange("p k j -> p (k j)"), modT)
    gam = P.alloc([128, 3, KC], F32)
    P.dma(gam, ngs.rearrange("a p c -> p a c"))
    em = load_vec(P, em_d, [128, 2])
    cw = load_vec(P, fcw, [128, 2 * FC, 3])
    cb = load_vec(P, fcb, [128, 2 * FC])

    m1 = P.mark()
    osb = P.alloc([128, KC, tall], BF16)
    P.dma(osb[:, :, 0:Tl], oT.rearrange("(c p) t -> p c t", p=128))
    if with_ctx:
        P.dma(osb[:, :, Tl:tall], ocT.rearrange("(c p) t -> p c t", p=128))
    st = [P.alloc([128, 512], F32) for _ in range(3)]
    si = [0]
    tiles = [(512 * i, 512 * (i + 1)) for i in range(4)] + [(2048, Tl)] + ([(Tl, tall)] if with_ctx else [])

    def ep_out(n, ti, a, b, bk):
        s = st[si[0] % 3]
        si[0] += 1
        P.copy(s[:, 0:b - a], bk, eng=(ACT if si[0] % 2 else DVE))
        P.dma(yT[n * 128:(n + 1) * 128, a:b], s[:, 0:b - a])

    gemm(P, osb, KC, w_out, list(range(KC)), tiles, ep_out)
    P.release(m1)

    segs = [(xT, yT[:, 0:Tl], xmT[:, 0:Tl], Tl, 0)]
    if with_ctx:
        segs.append((ctxT, yT[:, Tl:tall], xmT[:, Tl:tall], NCTX, 1))
    resid(P, consts, segs, gam[:, 0, :], modsb, 2)

    m3 = P.mark()
    h2T = P.alloc([128, KC, tall], BF16)
    msegs = [(xmT[:, 0:Tl], 0, Tl, 0)]
    if with_ctx:
        msegs.append((xmT[:, Tl:tall], Tl, NCTX, 1))
    modnorm(P, consts, msegs, gam[:, 1, :], modsb, 4, 3, h2T)
    ffn_up(P, consts, h2T, Tl, with_ctx, w_up, cw, cb, em, actT)
    P.release(m3)
    ffn_down(P, consts, actT, ntok, w_down, y2T)
    segs = [(xmT[:, 1:1 + TOWN], y2T[:, 0:TOWN], xoT, TOWN, 0)]
    if with_ctx:
        segs.append((xmT[:, Tl:tall], y2T[:, TOWN:ntok], cxoT, NCTX, 1))
    resid(P, consts, segs, gam[:, 2, :], modsb, 5)
    if layer == 1:
        return P

    mod1 = P.din("modT1", [128, 192])
    ng10 = P.din("ng10", [128, KC])
    w_in1 = P.din("dc_w_in", [D, 6144])
    cosd = P.din("cos64", [128, TOWN])
    sind = P.din("sin64", [128, TOWN])
    q1T = P.dout("q1T", [D, TOWN], BF16)
    k1T = P.dout("k1T", [D, TOWN], BF16)
    v1T = P.dout("v1T", [D, TOWN], BF16)
    kc1T = P.dout("kc1T", [D, NCTX], BF16)
    vc1T = P.dout("vc1T", [D, NCTX], BF16)
    modsb1 = P.alloc([128, 96, 2], F32)
    P.dma(modsb1.rearrange("p k j -> p (k j)"), mod1)
    gam1 = load_vec(P, ng10, [128, KC])
    cos = load_vec(P, cosd, [128, TOWN])
    sin = load_vec(P, sind, [128, TOWN])
    t1 = TOWN + NCTX
    hT = P.alloc([128, KC, t1], BF16)
    modnorm(P, consts, [(xoT, 0, TOWN, 0), (cxoT, TOWN, NCTX, 1)], gam1, modsb1, 1, 0, hT)
    tmp = (P.alloc([128, 512], BF16), P.alloc([128, 512], BF16), P.alloc([128, 512], F32),
           P.alloc([128, 512], F32), P.alloc([128, 512], F32))
    osts = [P.alloc([128, 512], BF16) for _ in range(3)]
    oi = [0]
    lat_tiles = [(512 * i, 512 * (i + 1)) for i in range(4)]

    def ep1(n, ti, a, b, bk):
        w = b - a
        ost = osts[oi[0] % 3]
        oi[0] += 1
        is_ctx = (ti == 4)
        if n < 16:
            dst = q1T[n * 128:(n + 1) * 128]
        elif n < 32:
            dst = (kc1T if is_ctx else k1T)[(n - 16) * 128:(n - 15) * 128]
        else:
            dst = (vc1T if is_ctx else v1T)[(n - 32) * 128:(n - 31) * 128]
        if n >= 32 or is_ctx:
            P.act(ost[:, 0:w], bk, AF.Copy)
        else:
            qk_epilogue(P, consts, bk, w, None, cos[:, a:b], sin[:, a:b], ost[:, 0:w], tmp, 3 + ti % 2, 5 + ti % 2)
        if is_ctx:
            P.dma(dst[:, 0:w], ost[:, 0:w])
        else:
            P.dma(dst[:, a:b], ost[:, 0:w])

    gemm(P, hT, KC, w_in1, list(range(16)), lat_tiles, ep1)
    gemm(P, hT, KC, w_in1, list(range(16, 48)), lat_tiles + [(TOWN, t1)], ep1)
    return P
GRID_W = 64
ROPE_THETA = 10000.0


def pc(v, k=None):
    v = np.asarray(v, np.float32).reshape(-1, 128)
    return np.ascontiguousarray(v.T)


def rope_tables(hd, t0, n):
    t = np.arange(t0, t0 + n)
    row = (t // GRID_W).astype(np.float32)
    col = (t % GRID_W).astype(np.float32)
    axis_dim = hd // 2
    inv_freq = (ROPE_THETA ** (-np.arange(0, axis_dim, 2, dtype=np.float32) / np.float32(axis_dim))).astype(np.float32)
    ang = np.concatenate([row[:, None] * inv_freq[None, :], col[:, None] * inv_freq[None, :]], axis=-1).astype(np.float32)
    pair = (np.arange(128) % hd) // 2
    a = ang[:, pair].T
    return np.ascontiguousarray(np.cos(a).astype(np.float32)), np.ascontiguousarray(np.sin(a).astype(np.float32))


def rot_matrix():
    R = np.zeros((128, 128), np.float32)
    for i in range(64):
        R[2 * i + 1, 2 * i] = -1.0
        R[2 * i, 2 * i + 1] = 1.0
    return R.astype(NPBF)


def prep_A(inp, core):
    b, qtr = core // 4, core % 4
    t0 = qtr * TOWN
    x = inp["x"][b]
    xe = np.zeros((TA, D), np.float32)
    lo, hi = max(t0 - 1, 0), min(t0 + TOWN + 1, x.shape[0])
    xe[lo - (t0 - 1):hi - (t0 - 1)] = x[lo:hi]
    cT = np.stack([pc(inp["c"][b]), pc(inp["c_ctx"])], axis=-1)
    cos, sin = rope_tables(128, t0, TOWN)
    m = {
        "xT": np.ascontiguousarray(xe.T),
        "ctxT": np.ascontiguousarray(inp["ctx"][b].T),
        "cT": np.ascontiguousarray(cT),
        "ada_w": inp["ada_w"],
        "adab": np.stack([pc(inp["ada_b"][i]) for i in range(2)]),
        "ng0": pc(inp["norm_g"][0, 0]),
        "ab_w_in": inp["ab_w_in"][0],
        "qkg": np.ascontiguousarray(inp["ab_qk_g"][0].T),
        "cos128": cos, "sin128": sin,
        "ones": np.ones((128, 128), NPBF),
        "rot": rot_matrix(),
        "hy_cw": np.ascontiguousarray(inp["hy_conv_w"][0].reshape(3, 24, 128).transpose(2, 1, 0)),
        "hy_cb": pc(inp["hy_conv_b"][0]),
        "edge": np.tile(np.array([[1.0 if t0 > 0 else 0.0, 1.0 if t0 + TOWN < x.shape[0] else 0.0]], np.float32), (128, 1)),
    }
    return m
NCORES = 8
_PROGS = {}


def _prog(name, builder):
    if name not in _PROGS:
        _PROGS[name] = builder().finish()
    return _PROGS[name]


def _run(name, builder, maps):
    nc = _prog(name, builder)
    res = run_bass_kernel_spmd(nc, maps, core_ids=list(range(NCORES)))
    return [{k: np.asarray(v) for k, v in r.items()} for r in res.results]


def _pad_cols(a):
    z = np.zeros((a.shape[0], 1), a.dtype)
    return np.concatenate([z, a, z], axis=1)


def prep_B(inp, A, core, cache):
    b, qtr = core // 4, core % 4
    g = [A[4 * b + q] for q in range(4)]
    kT = np.concatenate([x["kT"] for x in g] + [g[0]["kcT"]], axis=1)
    vT = np.concatenate([x["vT"] for x in g] + [g[0]["vcT"]], axis=1)
    ch = slice(qtr * HYCH, (qtr + 1) * HYCH)
    hyl = np.stack([np.concatenate([x["hyT"][s * 1024:(s + 1) * 1024][ch] for x in g], axis=1) for s in range(3)])
    hyc = np.stack([g[0]["hycT"][s * 1024:(s + 1) * 1024][ch] for s in range(3)])
    m = {
        "qT": A[core]["qT"], "kT": np.ascontiguousarray(kT), "v": np.ascontiguousarray(vT.T),
        "qcT": g[0]["qcT"], "kcT": g[0]["kcT"], "vc": np.ascontiguousarray(g[0]["vcT"].T),
        "ones": np.ones((128, 128), NPBF), "ident": np.eye(128, dtype=np.float32).astype(NPBF),
        "hy_w1": inp["hy_w1"][0], "hy_b1": inp["hy_b1"][0][:, None], "hy_w2": inp["hy_w2"][0],
        "hy_b2": inp["hy_b2"][0][:, None], "hy_fr": np.ascontiguousarray(inp["hy_freq"][0].T),
        "hy_skipbc": np.ascontiguousarray(np.tile(inp["hy_skip"][0][None, :, ch], (128, 1, 1))),
        "hy_w3": np.ascontiguousarray(inp["hy_w3"][0].reshape(64, 2, 2, 1024)[:, :, :, ch]),
        "hyvl": np.ascontiguousarray(hyl), "hyvc": np.ascontiguousarray(hyc),
    }
    for tag, n_tok in (("l", SEQ), ("c", NCTX)):
        key = ("tab", tag)
        if key not in cache:
            cache[key] = hy_host_tables(2 * n_tok // 128, n_tok // 128)
        for k, v in cache[key].items():
            m[k + tag] = v
        key = ("filt", tag, qtr)
        if key not in cache:
            cache[key] = hy_host_filter_consts(n_tok, qtr * HYCH, HYCH)
        m["featsT" + tag], m["decF" + tag], m["decB" + tag] = cache[key]
    return m


def _post_common(inp, layer, core, o_full, x_full, modT):
    b, qtr = core // 4, core % 4
    t0 = qtr * TOWN
    return {
        "oT": np.ascontiguousarray(_pad_cols(o_full)[:, t0:t0 + TA]),
        "xT": np.ascontiguousarray(_pad_cols(x_full)[:, t0:t0 + TA]),
        "modT": modT,
        "ngs": np.stack([pc(inp["norm_g"][layer, i]) for i in (1, 2, 3)]),
        "w_out": (inp["ab_w_out"] if layer == 0 else inp["dc_w_out"])[0],
        "w_up": inp["ffn_w_up"][layer], "w_down": inp["ffn_w_down"][layer],
        "fcw": np.ascontiguousarray(inp["ffn_conv_w"][layer].reshape(3, 2 * FC, 128).transpose(2, 1, 0)),
        "fcb": pc(inp["ffn_conv_b"][layer]),
        "edge": np.tile(np.array([[1.0 if t0 > 0 else 0.0, 1.0 if t0 + TOWN < SEQ else 0.0]], np.float32), (128, 1)),
        "ones": np.ones((128, 128), NPBF), "rot": rot_matrix(),
    }


def kernel(**inp):
    inp = {k: np.asarray(v) for k, v in inp.items()}
    cache = {}
    A = _run("A", build_A, [prep_A(inp, c) for c in range(NCORES)])
    B = _run("B", build_B, [prep_B(inp, A, c, cache) for c in range(NCORES)])
    del cache
    mapsC = []
    for b in range(2):
        g = [B[4 * b + q] for q in range(4)]
        o_full = np.concatenate([np.concatenate([x["oT"] for x in g], axis=1),
                                 np.concatenate([x["ohyl"] for x in g], axis=0)], axis=0)
        oc = np.concatenate([g[0]["ocT"], np.concatenate([x["ohyc"] for x in g], axis=0)], axis=0)
        x_full = np.ascontiguousarray(inp["x"][b].T)
        for q in range(4):
            c = 4 * b + q
            m = _post_common(inp, 0, c, o_full, x_full, A[c]["modT"][0])
            cos, sin = rope_tables(64, q * TOWN, TOWN)
            m.update({"ocT": np.ascontiguousarray(oc), "ctxT": np.ascontiguousarray(inp["ctx"][b].T),
                      "modT1": A[c]["modT"][1], "ng10": pc(inp["norm_g"][1, 0]), "dc_w_in": inp["dc_w_in"][0],
                      "cos64": cos, "sin64": sin})
            mapsC.append(m)
    C = _run("C", lambda: build_post(0), mapsC)
    del mapsC, B
    mapsD = []
    for b in range(2):
        g = [C[4 * b + q] for q in range(4)]
        kT = np.ascontiguousarray(np.concatenate([x["k1T"] for x in g] + [g[0]["kc1T"]], axis=1))
        v = np.ascontiguousarray(np.concatenate([x["v1T"] for x in g] + [g[0]["vc1T"]], axis=1).T)
        for q in range(4):
            mapsD.append({"qT": g[q]["q1T"], "kT": kT, "v": v,
                          "lam": np.ascontiguousarray(np.tile(inp["dc_lambda"][0][None], (128, 1, 1))),
                          "sg": np.ascontiguousarray(np.tile(inp["dc_subln_g"][0][None], (128, 1))),
                          "ones": np.ones((128, 128), NPBF), "ident": np.eye(128, dtype=np.float32).astype(NPBF)})
    Dm = _run("D", build_D, mapsD)
    del mapsD
    mapsE = []
    for b in range(2):
        o_full = np.concatenate([Dm[4 * b + q]["oT"] for q in range(4)], axis=1)
        x_full = np.concatenate([C[4 * b + q]["xoT"] for q in range(4)], axis=1)
        for q in range(4):
            c = 4 * b + q
            mapsE.append(_post_common(inp, 1, c, o_full, x_full, A[c]["modT"][1]))
    E = _run("E", lambda: build_post(1), mapsE)
    out = np.empty((2, SEQ, D), np.float32)
    for c in range(NCORES):
        b, q = c // 4, c % 4
        out[b, q * TOWN:(q + 1) * TOWN, :] = E[c]["xoT"].T
    return out
```

```python
import contextlib
import numpy as np
import ml_dtypes
import concourse.bass as bass
import concourse.mybir as mybir
from concourse.bass_utils import run_bass_kernel_spmd

F32 = mybir.dt.float32
BF16 = mybir.dt.bfloat16
U8 = mybir.dt.uint8
AF = mybir.ActivationFunctionType
ALU = mybir.AluOpType
AX = mybir.AxisListType
NPBF = ml_dtypes.bfloat16

PE, ACT, DVE, POOL, SP = "pe", "act", "dve", "pool", "sp"
N_DMA_SEMS = 24
ARENA = 204800


def _esize(dt):
    return int(mybir.dt.size(dt))


def _box(ap):
    t = ap.tensor
    es = _esize(ap.dtype)
    C = 1
    for s in list(t.shape)[1:]:
        C *= int(s)
    off = int(ap.offset)
    r0 = off // C
    c0 = off % C
    rext = 0
    cext = 0
    for (step, cnt) in ap.ap:
        step = int(step)
        cnt = int(cnt)
        if cnt <= 1 or step == 0:
            continue
        if step % C == 0:
            rext += (step // C) * (cnt - 1)
        else:
            cext += step * (cnt - 1)
    if c0 + cext >= C:
        rext += (c0 + cext) // C
        return (t.name, r0, r0 + rext + 1, 0, C * es)
    return (t.name, r0, r0 + rext + 1, c0 * es, (c0 + cext + 1) * es)


def _ovl(a, b):
    return a[1] < b[2] and b[1] < a[2] and a[3] < b[4] and b[3] < a[4]


def _covers(a, b):
    return a[1] <= b[1] and a[2] >= b[2] and a[3] <= b[3] and a[4] >= b[4]


class Op:
    __slots__ = ("eng", "fn", "waits", "sig", "sem", "val", "is_dma", "idx")

    def __init__(self, eng, fn, is_dma=False):
        self.eng = eng
        self.fn = fn
        self.waits = {}
        self.sig = False
        self.sem = None
        self.val = None
        self.is_dma = is_dma


class Sched:
    def __init__(self, nc):
        self.nc = nc
        self.ops = {e: [] for e in (PE, ACT, DVE, POOL, SP)}
        self.recs = {}
        self.readonly = set()
        self.dma_rr = 0
        self.dma_rr_pool = 0
        self.dma_last = [None] * N_DMA_SEMS
        self.dma_cnt = [0] * N_DMA_SEMS
        self.n_ops = 0

    def _dep(self, op, src):
        if src is op:
            return
        if src.eng == PE and op.eng == PE and not src.is_dma and not op.is_dma:
            return
        src.sig = True
        key = id(src) if src.is_dma else src.eng
        cur = op.waits.get(key)
        if cur is None or cur.idx < src.idx:
            op.waits[key] = src

    def _track(self, op, reads, writes):
        for ap in reads:
            b = _box(ap)
            if b[0] in self.readonly:
                continue
            ps = (b[0] == "psum")
            if ps:
                b = (b[0], 0, 128, b[3] // 2048 * 2048, (b[4] + 2047) // 2048 * 2048)
            lst = self.recs.setdefault(b[0], [])
            merged = False
            for rec in lst:
                if rec[1] or (ps and rec[2].eng != op.eng):
                    if _ovl(rec[0], b):
                        self._dep(op, rec[2])
                if (not rec[1]) and (not op.is_dma) and (not merged) and rec[2].eng == op.eng \
                        and (not rec[2].is_dma) and rec[0] == b:
                    rec[2] = op
                    merged = True
            if not merged:
                lst.append([b, False, op])
        for ap in writes:
            b = _box(ap)
            if b[0] == "psum":
                b = (b[0], 0, 128, b[3] // 2048 * 2048, (b[4] + 2047) // 2048 * 2048)
            lst = self.recs.setdefault(b[0], [])
            keep = []
            for rec in lst:
                if _ovl(rec[0], b):
                    if rec[2] is not op:
                        self._dep(op, rec[2])
                        if _covers(b, rec[0]):
                            continue
                keep.append(rec)
            keep.append([b, True, op])
            self.recs[b[0]] = keep

    def add(self, eng, fn, reads=(), writes=()):
        op = Op(eng, fn)
        op.idx = self.n_ops
        self.n_ops += 1
        self._track(op, reads, writes)
        self.ops[eng].append(op)
        return op

    def dma(self, out, in_, eng=SP, **kw):
        op = Op(eng, None, is_dma=True)
        op.idx = self.n_ops
        self.n_ops += 1
        half = N_DMA_SEMS // 2
        if eng == POOL:
            slot = half + self.dma_rr_pool
            self.dma_rr_pool = (self.dma_rr_pool + 1) % half
        else:
            slot = self.dma_rr
            self.dma_rr = (self.dma_rr + 1) % half
        prev = self.dma_last[slot]
        if prev is not None:
            op.waits[id(prev)] = prev
        self.dma_cnt[slot] += 16
        op.sem = slot
        op.val = self.dma_cnt[slot]
        op.sig = True
        self.dma_last[slot] = op
        op.fn = (out, in_, kw)
        self._track(op, [in_], [out])
        self.ops[eng].append(op)
        return op

    def collective(self, kind, groups, in_ap, out_ap):
        op = self.dma(out_ap, in_ap, eng=POOL)
        op.fn = (out_ap, in_ap, {"_cc": (kind, groups)})
        return op

    def emit(self):
        nc = self.nc
        with contextlib.ExitStack() as st:
            esem = {e: st.enter_context(nc.semaphore("s_" + e)) for e in (PE, ACT, DVE, POOL)}
            dsem = [st.enter_context(nc.semaphore("d%d" % i)) for i in range(N_DMA_SEMS)]
            for e in (PE, ACT, DVE, POOL, SP):
                n = 0
                for op in self.ops[e]:
                    if op.is_dma:
                        continue
                    if op.sig:
                        n += 1
                        op.sem = e
                        op.val = n
            block = st.enter_context(nc.Block())

            def run(e, eng):
                waited = {}
                for op in self.ops[e]:
                    for src in op.waits.values():
                        if src.is_dma:
                            sem = dsem[src.sem]
                            k = ("d", src.sem)
                        else:
                            sem = esem[src.sem]
                            k = src.sem
                        if waited.get(k, 0) >= src.val:
                            continue
                        waited[k] = src.val
                        eng.wait_ge(sem, src.val)
                    if op.is_dma:
                        out, in_, kw = op.fn
                        if "_cc" in kw:
                            kind, groups = kw["_cc"]
                            eng.collective_compute(kind, mybir.AluOpType.bypass, replica_groups=groups,
                                                   ins=[in_], outs=[out]).then_inc(dsem[op.sem], 16)
                        else:
                            eng.dma_start(out=out, in_=in_, **kw).then_inc(dsem[op.sem], 16)
                    else:
                        ins = op.fn(eng)
                        if op.sig:
                            ins.then_inc(esem[e], 1)
                for slot in range(N_DMA_SEMS):
                    last = self.dma_last[slot]
                    if last is not None and last.eng == e:
                        eng.wait_ge(dsem[slot], last.val)

            @block.tensor
            def _(eng):
                run(PE, eng)

            @block.scalar
            def _(eng):
                run(ACT, eng)

            @block.vector
            def _(eng):
                run(DVE, eng)

            @block.gpsimd
            def _(eng):
                run(POOL, eng)

            @block.sync
            def _(eng):
                run(SP, eng)


class Prog:
    def __init__(self):
        self.nc = bass.Bass("TRN2", target_bir_lowering=False)
        self.st = contextlib.ExitStack()
        self.arena = self.st.enter_context(self.nc.sbuf_tensor("arena", [128, ARENA], U8))
        self.psum = self.st.enter_context(self.nc.psum_tensor("psum", [128, 4096], F32))
        self.S = Sched(self.nc)
        self.top = 0
        self.inputs = {}
        self.outputs = []

    def din(self, name, shape, dtype=F32):
        self.S.readonly.add(name)
        return self.nc.dram_tensor(name, list(shape), dtype, kind="ExternalInput").ap()

    def dout(self, name, shape, dtype=F32):
        self.outputs.append(name)
        return self.nc.dram_tensor(name, list(shape), dtype, kind="ExternalOutput").ap()

    def dscr(self, name, shape, dtype=F32):
        return self.nc.dram_tensor(name, list(shape), dtype, kind="Internal").ap()

    def alloc(self, shape, dtype):
        n = 1
        for s in shape[1:]:
            n *= int(s)
        nbytes = n * _esize(dtype)
        nbytes = (nbytes + 63) // 64 * 64
        assert self.top + nbytes <= ARENA, ("arena overflow", self.top, nbytes, shape)
        v = self.arena[:, self.top:self.top + nbytes].bitcast(dtype)[:, 0:n]
        self.top += nbytes
        if len(shape) > 2:
            names = " ".join("d%d" % i for i in range(1, len(shape)))
            kw = {"d%d" % i: int(shape[i]) for i in range(1, len(shape))}
            v = v.rearrange("p (%s) -> p %s" % (names, names), **kw)
        if int(shape[0]) < 128:
            v = v[0:int(shape[0])]
        return v

    def mark(self):
        return self.top

    def release(self, m):
        self.top = m

    def bank(self, i, n=512, dtype=F32):
        if dtype == F32:
            return self.psum[:, i * 512:i * 512 + n]
        v = self.psum[:, i * 512:(i + 1) * 512].bitcast(dtype)
        return v[:, 0:n]

    def dma(self, out, in_, eng=SP, **kw):
        return self.S.dma(out, in_, eng=eng, **kw)

    def mm(self, out, lhsT, rhs, start=True, stop=True, skip=False):
        if skip:
            return self.S.add(PE, lambda e: e.matmul(out, lhsT, rhs, start=start, stop=stop,
                                                     skip_group_check=True),
                              reads=[lhsT, rhs], writes=[out])
        return self.S.add(PE, lambda e: e.matmul(out, lhsT, rhs, start=start, stop=stop),
                          reads=[lhsT, rhs], writes=[out])

    def transpose(self, out, in_, ident):
        return self.S.add(PE, lambda e: e.transpose(out, in_, ident),
                          reads=[in_, ident], writes=[out])

    def act(self, out, in_, func, bias=None, scale=None, accum_out=None, eng=ACT):
        kw = {}
        reads = [in_]
        writes = [out]
        if bias is not None:
            kw["bias"] = bias
            if not isinstance(bias, (int, float)):
                reads.append(bias)
        if scale is not None:
            kw["scale"] = scale
            if not isinstance(scale, (int, float)):
                reads.append(scale)
        if accum_out is not None:
            kw["accum_out"] = accum_out
            writes.append(accum_out)
        return self.S.add(eng, lambda e: e.activation(out=out, in_=in_, func=func, **kw),
                          reads=reads, writes=writes)

    def tt(self, out, in0, in1, op, eng=DVE):
        return self.S.add(eng, lambda e: e.tensor_tensor(out=out, in0=in0, in1=in1, op=op),
                          reads=[in0, in1], writes=[out])

    def ts(self, out, in0, s1, s2, op0, op1=None, eng=DVE, accum_out=None):
        reads = [in0]
        writes = [out]
        for s in (s1, s2):
            if s is not None and not isinstance(s, (int, float)):
                reads.append(s)
        kw = {}
        if op1 is not None:
            kw["op1"] = op1
        if accum_out is not None:
            kw["accum_out"] = accum_out
            writes.append(accum_out)
        return self.S.add(eng, lambda e: e.tensor_scalar(out=out, in0=in0, scalar1=s1, scalar2=s2,
                                                         op0=op0, **kw),
                          reads=reads, writes=writes)

    def stt(self, out, in0, scalar, in1, op0, op1, eng=DVE):
        reads = [in0, in1]
        if not isinstance(scalar, (int, float)):
            reads.append(scalar)
        return self.S.add(eng, lambda e: e.scalar_tensor_tensor(out=out, in0=in0, scalar=scalar, in1=in1,
                                                                op0=op0, op1=op1),
                          reads=reads, writes=[out])

    def copy(self, out, in_, eng=DVE):
        if eng == ACT:
            return self.act(out, in_, AF.Copy)
        return self.S.add(eng, lambda e: e.tensor_copy(out=out, in_=in_), reads=[in_], writes=[out])

    def memset(self, out, val, eng=DVE):
        return self.S.add(eng, lambda e: e.memset(out, val), writes=[out])

    def recip(self, out, in_):
        return self.S.add(DVE, lambda e: e.reciprocal(out=out, in_=in_), reads=[in_], writes=[out])

    def finish(self):
        self.S.emit()
        return self.nc
D = 2048
KC = 16
EPS = 1e-6


def bcast_mid(ap2, n):
    a = ap2.ap
    return bass.AP(ap2.tensor, ap2.offset, [list(a[0]), [0, n]] + [list(x) for x in a[1:]])


def bcast_last(ap2, n):
    a = ap2.ap
    return bass.AP(ap2.tensor, ap2.offset, [list(x) for x in a] + [[0, n]])


def load_consts(P, ones_d, rot_d=None, ident_d=None):
    c = {}
    c["ones"] = P.alloc([128, 128], BF16)
    P.dma(c["ones"], ones_d)
    if rot_d is not None:
        c["rot"] = P.alloc([128, 128], BF16)
        P.dma(c["rot"], rot_d)
    if ident_d is not None:
        c["ident"] = P.alloc([128, 128], BF16)
        P.dma(c["ident"], ident_d)
    c["eps"] = P.alloc([128, 1], F32)
    P.memset(c["eps"], EPS)
    return c


def sumsq_rstd(P, consts, src_chunks, w, nfeat, bank_i, rstd_out, sq_tmp):
    n = len(src_chunks)
    bk = P.bank(bank_i)[:, 0:w]
    for i, s in enumerate(src_chunks):
        P.act(sq_tmp[:, i, 0:w], s, AF.Square)
    for i in range(n):
        P.mm(bk, consts["ones"], sq_tmp[:, i, 0:w], start=(i == 0), stop=(i == n - 1))
    P.act(rstd_out, bk, AF.Sqrt, bias=consts["eps"], scale=1.0 / nfeat)
    P.recip(rstd_out, rstd_out)


def modnorm(P, consts, segs, gam, modsb, sc_idx, sh_idx, hT):
    m0 = P.mark()
    ab = {}
    for (_, _, _, j) in segs:
        if j in ab:
            continue
        a = P.alloc([128, KC], F32)
        b = P.alloc([128, KC], F32)
        P.stt(a, modsb[:, sc_idx * KC:(sc_idx + 1) * KC, j], 1.0, gam, ALU.add, ALU.mult)
        P.copy(b, modsb[:, sh_idx * KC:(sh_idx + 1) * KC, j])
        ab[j] = (a, b)
    xts = [P.alloc([128, KC, 512], F32) for _ in range(2)]
    sq = P.alloc([128, KC, 512], BF16)
    rstd = P.alloc([128, 512], F32)
    ti = 0
    for (xd, col0, n, j) in segs:
        xv = xd.rearrange("(c p) t -> p c t", p=128)
        a, b = ab[j]
        for t0 in range(0, n, 512):
            w = min(512, n - t0)
            xt = xts[ti % 2]
            ti += 1
            P.dma(xt[:, :, 0:w], xv[:, :, t0:t0 + w])
            sumsq_rstd(P, consts, [xt[:, c, 0:w] for c in range(KC)], w, D, 7, rstd[:, 0:w], sq)
            P.tt(xt[:, :, 0:w], xt[:, :, 0:w], bcast_mid(rstd[:, 0:w], KC), ALU.mult)
            for c in range(KC):
                P.act(hT[:, c, col0 + t0:col0 + t0 + w], xt[:, c, 0:w], AF.Identity,
                      bias=b[:, c:c + 1], scale=a[:, c:c + 1])
    P.release(m0)


def gemm(P, hT, kc_n, Wd, chunks, tiles, epilogue, nbanks=3, bank0=0, wslots=None):
    Wv = Wd.rearrange("(kc p) n -> p kc n", p=128)
    own = wslots is None
    if own:
        wslots = [P.alloc([128, kc_n, 128], BF16) for _ in range(5)]
    bi = 0
    for ci, n in enumerate(chunks):
        wt = wslots[ci % len(wslots)]
        P.dma(wt, Wv[:, :, n * 128:(n + 1) * 128], eng=POOL)
        for ti, (a, b) in enumerate(tiles):
            bk = P.bank(bank0 + bi % nbanks)[:, 0:b - a]
            bi += 1
            for k in range(kc_n):
                P.mm(bk, wt[:, k, :], hT[:, k, a:b], start=(k == 0), stop=(k == kc_n - 1))
            epilogue(n, ti, a, b, bk)


def qk_epilogue(P, consts, bk, w, g_col, cos, sin, out_bf, tmp, bank_ss, bank_pq):
    sq, qg, t1, t2, rstd = tmp
    if g_col is not None:
        P.act(sq[:, 0:w], bk, AF.Square)
        P.ts(qg[:, 0:w], bk, g_col, None, ALU.mult)
        ss = P.bank(bank_ss)[:, 0:w]
        P.mm(ss, consts["ones"], sq[:, 0:w])
        P.act(rstd[:, 0:w], ss, AF.Sqrt, bias=consts["eps"], scale=1.0 / 128.0)
        P.recip(rstd[:, 0:w], rstd[:, 0:w])
    else:
        P.copy(qg[:, 0:w], bk)
    if cos is not None:
        pq = P.bank(bank_pq)[:, 0:w]
        P.mm(pq, consts["rot"], qg[:, 0:w])
        P.tt(t1[:, 0:w], qg[:, 0:w], cos, ALU.mult)
        P.tt(t2[:, 0:w], pq, sin, ALU.mult)
        if g_col is not None:
            P.tt(t1[:, 0:w], t1[:, 0:w], t2[:, 0:w], ALU.add)
            P.tt(out_bf, t1[:, 0:w], rstd[:, 0:w], ALU.mult)
        else:
            P.tt(out_bf, t1[:, 0:w], t2[:, 0:w], ALU.add)
    else:
        if g_col is not None:
            P.tt(out_bf, qg[:, 0:w], rstd[:, 0:w], ALU.mult)
        else:
            P.copy(out_bf, qg[:, 0:w])


def resid(P, consts, segs, gam, modsb, g_idx):
    m0 = P.mark()
    gg = {}
    for seg in segs:
        j = seg[4]
        if j not in gg:
            g = P.alloc([128, KC], F32)
            P.tt(g, modsb[:, g_idx * KC:(g_idx + 1) * KC, j], gam, ALU.mult)
            gg[j] = g
    xts = [P.alloc([128, KC, 512], F32) for _ in range(2)]
    yts = [P.alloc([128, KC, 512], F32) for _ in range(2)]
    sq = P.alloc([128, KC, 512], BF16)
    rstd = P.alloc([128, 512], F32)
    ti = 0
    for (xin, yd, xout, n, j) in segs:
        xv = xin.rearrange("(c p) t -> p c t", p=128)
        yv = yd.rearrange("(c p) t -> p c t", p=128)
        ov = xout.rearrange("(c p) t -> p c t", p=128)
        g = gg[j]
        for t0 in range(0, n, 512):
            w = min(512, n - t0)
            xt = xts[ti % 2]
            yt = yts[ti % 2]
            ti += 1
            P.dma(xt[:, :, 0:w], xv[:, :, t0:t0 + w])
            P.dma(yt[:, :, 0:w], yv[:, :, t0:t0 + w])
            sumsq_rstd(P, consts, [yt[:, c, 0:w] for c in range(KC)], w, D, 7, rstd[:, 0:w], sq)
            P.tt(yt[:, :, 0:w], yt[:, :, 0:w], bcast_mid(rstd[:, 0:w], KC), ALU.mult)
            for c in range(KC):
                P.stt(xt[:, c, 0:w], yt[:, c, 0:w], g[:, c:c + 1], xt[:, c, 0:w], ALU.mult, ALU.add)
            P.dma(ov[:, :, t0:t0 + w], xt[:, :, 0:w])
    P.release(m0)
TOWN = 2048
NCTX = 256
TA = TOWN + 2
TALL = TA + NCTX


def load_vec(P, d, shape, dtype=F32):
    t = P.alloc(shape, dtype)
    P.dma(t, d)
    return t


def mod_phase(P, consts, cT_d, ada_w_d, adab_d, modT_out):
    m0 = P.mark()
    cs = P.alloc([128, KC, 2], F32)
    P.dma(cs, cT_d)
    sc = P.alloc([128, KC, 2], BF16)
    P.act(sc, cs, AF.Silu)
    wslots = [P.alloc([128, KC, 128], BF16) for _ in range(4)]
    for i in range(2):
        Wv = ada_w_d[i].rearrange("(kc p) n -> p kc n", p=128)
        adab = P.alloc([128, 96], F32)
        P.dma(adab, adab_d[i])
        bk = P.bank(6)
        for k in range(96):
            wt = wslots[k % 4]
            P.dma(wt, Wv[:, :, k * 128:(k + 1) * 128], eng=POOL)
            for c in range(KC):
                P.mm(bk[:, 2 * k:2 * k + 2], wt[:, c, :], sc[:, c, :], start=(c == 0), stop=(c == KC - 1))
        msb = P.alloc([128, 96, 2], F32)
        bv = bk[:, 0:192].rearrange("p (k j) -> p k j", j=2)
        for j in range(2):
            P.tt(msb[:, :, j], bv[:, :, j], adab, ALU.add)
        P.dma(modT_out[i], msb.rearrange("p k j -> p (k j)"))
    P.release(m0)


def build_A(stage=9, qchunks=None, hchunks=None):
    P = Prog()
    xT = P.din("xT", [D, TA])
    ctxT = P.din("ctxT", [D, NCTX])
    cT = P.din("cT", [128, KC, 2])
    ada_w = P.din("ada_w", [2, D, 6 * D])
    adab = P.din("adab", [2, 128, 96])
    ng = P.din("ng0", [128, KC])
    w_in = P.din("ab_w_in", [D, 4608])
    qkg = P.din("qkg", [128, 2])
    cosd = P.din("cos128", [128, TOWN])
    sind = P.din("sin128", [128, TOWN])
    ones_d = P.din("ones", [128, 128], BF16)
    rot_d = P.din("rot", [128, 128], BF16)
    cw_d = P.din("hy_cw", [128, 24, 3])
    cb_d = P.din("hy_cb", [128, 24])
    em_d = P.din("edge", [128, 2])
    modT = P.dout("modT", [2, 128, 192])
    qT = P.dout("qT", [1024, TOWN], BF16)
    kT = P.dout("kT", [256, TOWN], BF16)
    vT = P.dout("vT", [256, TOWN], BF16)
    hyT = P.dout("hyT", [3072, TOWN], BF16)
    qcT = P.dout("qcT", [1024, NCTX], BF16)
    kcT = P.dout("kcT", [256, NCTX], BF16)
    vcT = P.dout("vcT", [256, NCTX], BF16)
    hycT = P.dout("hycT", [3072, NCTX], BF16)

    consts = load_consts(P, ones_d, rot_d)
    mod_phase(P, consts, cT, ada_w, adab, modT)
    if stage <= 1:
        return P
    modsb = P.alloc([128, 96, 2], F32)
    P.dma(modsb.rearrange("p k j -> p (k j)"), modT[0])
    gam = load_vec(P, ng, [128, KC])
    hT = P.alloc([128, KC, TALL], BF16)
    modnorm(P, consts, [(xT, 0, TA, 0), (ctxT, TA, NCTX, 1)], gam, modsb, 1, 0, hT)
    if stage <= 2:
        return P

    g2 = load_vec(P, qkg, [128, 2])
    cos = load_vec(P, cosd, [128, TOWN])
    sin = load_vec(P, sind, [128, TOWN])
    cw = load_vec(P, cw_d, [128, 24, 3])
    cb = load_vec(P, cb_d, [128, 24])
    em = load_vec(P, em_d, [128, 2])
    tmp = (P.alloc([128, 512], BF16), P.alloc([128, 512], BF16), P.alloc([128, 512], F32),
           P.alloc([128, 512], F32), P.alloc([128, 512], F32))
    osts = [P.alloc([128, 512], BF16) for _ in range(3)]
    oi = [0]
    lat_tiles = [(1 + 512 * i, 1 + 512 * (i + 1)) for i in range(4)]
    ctx_tile = (TA, TALL)

    def ep_qkv(n, ti, a, b, bk):
        w = b - a
        ost = osts[oi[0] % 3]
        oi[0] += 1
        is_ctx = (ti == 4)
        if n < 8:
            dst = (qcT if is_ctx else qT)[n * 128:(n + 1) * 128]
            gc = g2[:, 0:1]
        elif n < 10:
            dst = (kcT if is_ctx else kT)[(n - 8) * 128:(n - 7) * 128]
            gc = g2[:, 1:2]
        else:
            dst = (vcT if is_ctx else vT)[(n - 10) * 128:(n - 9) * 128]
            gc = None
        if gc is None:
            P.act(ost[:, 0:w], bk, AF.Copy)
        elif is_ctx:
            qk_epilogue(P, consts, bk, w, gc, None, None, ost[:, 0:w], tmp, 3, 5)
        else:
            qk_epilogue(P, consts, bk, w, gc, cos[:, a - 1:b - 1], sin[:, a - 1:b - 1], ost[:, 0:w], tmp,
                        3 + ti % 2, 5 + ti % 2)
        if is_ctx:
            P.dma(dst[:, 0:w], ost[:, 0:w])
        else:
            P.dma(dst[:, a - 1:b - 1], ost[:, 0:w])

    gemm(P, hT, KC, w_in, list(range(12)) if qchunks is None else qchunks, lat_tiles + [ctx_tile], ep_qkv)
    if stage <= 3:
        return P

    U = [P.alloc([128, TALL + 2], F32) for _ in range(2)]
    V = P.alloc([128, TALL], F32)
    Vb = [P.alloc([128, TALL], BF16) for _ in range(2)]
    for u in U:
        P.memset(u[:, TA:TA + 1], 0.0)
        P.memset(u[:, TALL + 1:TALL + 2], 0.0)
    hy_tiles = [(512 * i, 512 * (i + 1)) for i in range(4)] + [(2048, TA), (TA, TALL)]

    def ep_hy(n, ti, a, b, bk):
        hc = n - 12
        u = U[hc % 2]
        if ti < 5:
            P.act(u[:, a:b], bk, AF.Copy)
        else:
            P.act(u[:, TA + 1:TALL + 1], bk, AF.Copy)
            vb = Vb[hc % 2]
            P.ts(u[:, 0:1], u[:, 0:1], em[:, 0:1], None, ALU.mult)
            P.ts(u[:, TA - 1:TA], u[:, TA - 1:TA], em[:, 1:2], None, ALU.mult)
            P.act(V, u[:, 1:TALL + 1], AF.Identity, bias=cb[:, hc:hc + 1], scale=cw[:, hc, 1:2])
            P.stt(V, u[:, 0:TALL], cw[:, hc, 0:1], V, ALU.mult, ALU.add)
            P.stt(vb, u[:, 2:TALL + 2], cw[:, hc, 2:3], V, ALU.mult, ALU.add)
            P.dma(hyT[hc * 128:(hc + 1) * 128, :], vb[:, 0:TOWN])
            P.dma(hycT[hc * 128:(hc + 1) * 128, :], vb[:, TA:TA + NCTX])

    gemm(P, hT, KC, w_in, list(range(12, 36)) if hchunks is None else hchunks, hy_tiles, ep_hy)
    return P
def attention(P, consts, qT_d, kT_d, v_d, oT_d, nheads, kv_of, nmaps, Tq, S, scale, finish, vw=128):
    m0 = P.mark()
    QW = min(512 // nmaps, Tq)
    dk = 128 // nmaps
    nkt = S // 128
    nqs = QW // 128
    KTs = [P.alloc([128, S], BF16) for _ in range(2)]
    VAs = [P.alloc([128, nkt, 129], BF16) for _ in range(2)]
    for va in VAs:
        P.memset(va[:, :, 128:129], 1.0)
    QTs = [P.alloc([128, QW], BF16) for _ in range(2)]
    PTs = [P.alloc([128, 512], BF16) for _ in range(3)]
    obf = [P.alloc([128, 128], BF16) for _ in range(2)]
    oTs = [P.alloc([128, QW], BF16) for _ in range(2)]
    vv = v_d.rearrange("(kt p) e -> p kt e", p=128)
    cur_kv = None
    kvi = 0
    qi = 0
    pi = 0
    oi = 0
    for h in range(nheads):
        kv = kv_of(h)
        if kv != cur_kv:
            KT = KTs[kvi % 2]
            VA = VAs[kvi % 2]
            kvi += 1
            cur_kv = kv
            P.dma(KT, kT_d[kv * 128:(kv + 1) * 128, :])
            P.dma(VA[:, :, 0:128], vv[:, :, kv * vw:kv * vw + 128])
        for qt in range(Tq // QW):
            QT = QTs[qi % 2]
            obase = 4 + 2 * (qi % 2)
            qi += 1
            P.dma(QT, qT_d[h * 128:(h + 1) * 128, qt * QW:(qt + 1) * QW])
            def Oacc(m, qs):
                idx = m * nqs + qs
                return P.bank(obase + idx // 2)[:, (idx % 2) * 256:(idx % 2) * 256 + 129]
            def sbank(kt, m):
                return P.bank(kt % 3) if nmaps == 1 else P.bank(2 * m + kt % 2)

            def issue_qk(kt):
                for m in range(nmaps):
                    P.mm(sbank(kt, m)[:, 0:QW], KT[m * dk:(m + 1) * dk, kt * 128:(kt + 1) * 128],
                         QT[m * dk:(m + 1) * dk, :])

            issue_qk(0)
            for kt in range(nkt):
                if kt + 1 < nkt:
                    issue_qk(kt + 1)
                PT = PTs[pi % 3]
                pi += 1
                for m in range(nmaps):
                    P.act(PT[:, m * QW:(m + 1) * QW], sbank(kt, m)[:, 0:QW], AF.Exp, scale=scale)
                for m in range(nmaps):
                    for qs in range(nqs):
                        P.mm(Oacc(m, qs), PT[:, m * QW + qs * 128:m * QW + (qs + 1) * 128], VA[:, kt, :],
                             start=(kt == 0 and (m * nqs + qs) % 2 == 0), stop=(kt == nkt - 1), skip=True)
            oT = oTs[oi % 2]
            oi += 1
            tb = P.bank(3, 1024, BF16)
            for qs in range(nqs):
                ob = obf[qs % 2]
                finish([Oacc(m, qs) for m in range(nmaps)], ob)
                P.transpose(tb[:, qs * 128:(qs + 1) * 128], ob, consts["ident"])
            P.copy(oT, tb[:, 0:QW])
            P.dma(oT_d[h * 128:(h + 1) * 128, qt * QW:(qt + 1) * QW], oT)
    P.release(m0)


def make_finish_gqa(P):
    r = P.alloc([128, 1], F32)

    def finish(Os, ob):
        O = Os[0]
        P.recip(r, O[:, 128:129])
        P.ts(ob, O[:, 0:128], r, None, ALU.mult)
    return finish


def make_finish_diff(P, consts, lam_bc_d, sg_bc_d, lambda_init):
    lv = P.alloc([128, 4, 64], F32)
    P.dma(lv, lam_bc_d)
    pr = P.alloc([128, 2, 64], F32)
    lvv = lv.rearrange("p (a b) d -> p a b d", b=2)
    P.tt(pr, lvv[:, :, 0, :], lvv[:, :, 1, :], ALU.mult)
    s2 = P.alloc([128, 2], F32)
    P.S.add(DVE, lambda e: e.reduce_sum(out=s2, in_=pr, axis=AX.X), reads=[pr], writes=[s2])
    e2 = P.alloc([128, 2], F32)
    P.act(e2, s2, AF.Exp)
    lam = P.alloc([128, 1], F32)
    P.tt(lam, e2[:, 0:1], e2[:, 1:2], ALU.subtract)
    P.ts(lam, lam, float(lambda_init), None, ALU.add)
    sg = P.alloc([128, 128], F32)
    P.dma(sg, sg_bc_d)
    P.ts(sg, sg, float(1.0 - lambda_init), None, ALU.mult)
    r0 = P.alloc([128, 1], F32)
    r1 = P.alloc([128, 1], F32)
    ss = P.alloc([128, 1], F32)
    t = P.alloc([128, 128], F32)
    o = P.alloc([128, 128], F32)
    junk = P.alloc([128, 128], F32)

    def finish(Os, ob):
        O0, O1 = Os
        P.recip(r0, O0[:, 128:129])
        P.recip(r1, O1[:, 128:129])
        P.tt(r1, r1, lam, ALU.mult)
        P.ts(t, O1[:, 0:128], r1, None, ALU.mult)
        P.stt(o, O0[:, 0:128], r0, t, ALU.mult, ALU.subtract)
        P.act(junk, o, AF.Square, accum_out=ss)
        P.act(ss, ss, AF.Sqrt, bias=consts["eps"], scale=1.0 / 128.0)
        P.recip(ss, ss)
        P.stt(ob, o, ss, sg, ALU.mult, ALU.mult)
    return finish
HC = 32
MAGIC = 12582912.0


def hy_host_tables(S, SD):
    N = 128 * S
    n2 = np.arange(S)[:, None]
    k2 = np.arange(S)[None, :]
    a = 2 * np.pi * (n2 * k2 % S) / S
    FA = np.concatenate([np.cos(a), -np.sin(a)], axis=1)
    n1 = np.arange(128)[:, None, None]
    kk2 = np.arange(S)[None, :, None]
    k1 = np.arange(128)[None, None, :]
    ph = 2 * np.pi * ((n1 * (S * k1 + kk2)) % N) / N
    GT = np.stack([np.cos(ph), -np.sin(ph), np.sin(ph)], axis=2)
    kk1 = np.arange(128)[:, None]
    nn1 = np.arange(128)[None, :]
    th = 2 * np.pi * ((kk1 * nn1) % 128) / 128
    CI = np.stack([np.concatenate([np.cos(th), np.sin(th)], 1),
                   np.concatenate([-np.sin(th), np.cos(th)], 1)], axis=1)
    ek2 = np.arange(S)[:, None, None]
    en1 = np.arange(128)[None, :, None]
    en2 = np.arange(SD)[None, None, :]
    ps = 2 * np.pi * ((ek2 * (en1 + 128 * en2)) % N) / N
    ET = np.stack([np.cos(ps), -np.sin(ps)], axis=2)
    return {"FA": FA.astype(NPBF), "GT": GT.astype(NPBF), "CI": CI.astype(NPBF), "ET": ET.astype(NPBF)}


def hy_host_filter_consts(n_tok, ch0, nch):
    HY_BANDS, HY_CH = 16, 1024
    f32 = np.float32
    t = np.arange(n_tok, dtype=f32)
    t_norm = (t / f32(max(n_tok - 1, 1))).astype(f32)
    w = (f32(2.0 * np.pi / n_tok) * t).astype(f32)
    bands = np.linspace(1e-4, HY_BANDS - 1, HY_BANDS, dtype=f32)
    z = (w[:, None] * bands).astype(f32)
    feats = np.concatenate([t_norm[:, None], np.cos(z), -np.sin(z)], axis=-1).astype(f32)
    deltas = np.abs(np.linspace(np.log(1e-2) / 0.3, np.log(1e-2) / 1.5, HY_CH, dtype=f32)).astype(f32)
    decay = np.exp(-t_norm[:, None] * deltas[None, ch0:ch0 + nch]).astype(f32)
    N = 2 * n_tok
    idx = np.zeros(N, np.int64)
    idx[:n_tok] = np.arange(n_tok)
    idx[n_tok + 1:] = N - np.arange(n_tok + 1, N)
    featsT = np.ascontiguousarray(feats[idx].T)
    decF = np.zeros((N, nch), f32)
    decB = np.zeros((N, nch), f32)
    decF[:n_tok] = decay
    decB[n_tok + 1:] = decay[idx[n_tok + 1:]]
    S = N // 128
    ng = nch // HC

    def lay(d):
        return np.ascontiguousarray(d.reshape(S, 128, ng, HC).transpose(2, 0, 1, 3))
    return featsT.astype(NPBF), lay(decF), lay(decB)


def hy_load_tables(P, td, S, SD):
    T = {"S": S, "SD": SD}
    T["FA"] = P.alloc([S, 2 * S], BF16)
    P.dma(T["FA"], td["FA"])
    T["CI"] = P.alloc([128, 2, 256], BF16)
    P.dma(T["CI"], td["CI"])
    T["ET"] = P.alloc([S, 128, 2, SD], BF16)
    P.dma(T["ET"], td["ET"])
    T["GTd"] = td["GT"]
    return T


def hy_mlp(P, consts, featsT_d, N, w1_d, b1_d, w2_d, b2_d, fr_d, hid2T):
    m0 = P.mark()
    w1 = P.alloc([33, 64], BF16)
    P.dma(w1, w1_d, eng=POOL)
    w2 = P.alloc([64, 64], BF16)
    P.dma(w2, w2_d, eng=POOL)
    bb = P.alloc([64, 2], F32)
    P.dma(bb[:, 0:1], b1_d)
    P.dma(bb[:, 1:2], b2_d)
    fr = P.alloc([64, 2], F32)
    P.dma(fr, fr_d)
    sc = P.alloc([64, 2], F32)
    of = P.alloc([64, 2], F32)
    P.ts(sc, fr, float(1.0 / (2 * np.pi)), None, ALU.mult)
    P.tt(of, bb, sc, ALU.mult)
    ft = P.alloc([33, N], BF16)
    P.dma(ft, featsT_d)
    h1 = P.alloc([64, N], BF16)
    y = P.alloc([64, 512], F32)
    mm_ = P.alloc([64, 512], F32)
    for layer in range(2):
        src, wt, dst = (ft, w1, h1) if layer == 0 else (h1, w2, hid2T)
        for t0 in range(0, N, 512):
            w = min(512, N - t0)
            bk = P.bank(6 + (t0 // 512) % 2)[0:64, 0:w]
            P.mm(bk, wt, src[:, t0:t0 + w])
            P.ts(y[:, 0:w], bk, sc[:, layer:layer + 1], of[:, layer:layer + 1], ALU.mult, ALU.add)
            P.ts(mm_[:, 0:w], y[:, 0:w], MAGIC, None, ALU.add)
            P.ts(mm_[:, 0:w], mm_[:, 0:w], MAGIC, None, ALU.subtract)
            P.tt(y[:, 0:w], y[:, 0:w], mm_[:, 0:w], ALU.subtract)
            P.act(dst[:, t0:t0 + w], y[:, 0:w], AF.Sin, scale=float(2 * np.pi))
    P.release(m0)


def hy_fwd(P, T, u_sb, nblk, A_sb, gti, epilogue):
    S = T["S"]
    C = HC
    for c in range(C):
        bk = P.bank(c % 2)[:, 0:2 * S]
        P.mm(bk, u_sb[0:nblk, c, :], T["FA"][0:nblk, :])
        P.copy(A_sb[:, :, :, c], bk.rearrange("p (j k) -> p k j", j=2), eng=(DVE if c % 2 else ACT))
    KB = min(256 // C, S)
    GCH = min(16, S)
    gts = T["gts"]
    for k0 in range(0, S, KB):
        if k0 % GCH == 0:
            gt = gts[gti[0] % 2]
            gti[0] += 1
            P.dma(gt[:, 0:GCH], T["GTd"][:, k0:k0 + GCH])
        bk = P.bank(2 + (k0 // KB) % 2)[:, 0:KB * 2 * C].rearrange("p (k j c) -> p k j c", k=KB, j=2)
        for kb in range(KB):
            k2 = k0 + kb
            g = gt[:, k2 % GCH]
            P.mm(bk[:, kb, 0, :], g[:, 0, :], A_sb[:, k2, 0, :], start=True, stop=False)
            P.mm(bk[:, kb, 0, :], g[:, 2, :], A_sb[:, k2, 1, :], start=False, stop=True)
            P.mm(bk[:, kb, 1, :], g[:, 1, :], A_sb[:, k2, 0, :], start=True, stop=False)
            P.mm(bk[:, kb, 1, :], g[:, 0, :], A_sb[:, k2, 1, :], start=False, stop=True)
        epilogue(k0, KB, bk)


def hy_conv(P, T, u_sb, Hd, gate_sb, skip_bc, z_sb, bufs, gti):
    S, SD = T["S"], T["SD"]
    C = HC
    AB, Y_sb, us, hch, tmps, tz = bufs
    A_sb = AB[:, 0:S * 2 * C].rearrange("p (k j c) -> p k j c", k=S, j=2)
    KB = min(256 // C, S)

    def ep(k0, KB_, bk):
        h = hch[(k0 // KB_) % 2]
        P.dma(h[:, 0:KB_], Hd[:, k0:k0 + KB_])
        t1, t2, t3, t4 = tmps
        P.tt(t1[:, 0:KB_], bk[:, :, 0, :], h[:, 0:KB_, 0, :], ALU.mult)
        P.tt(t2[:, 0:KB_], bk[:, :, 1, :], h[:, 0:KB_, 1, :], ALU.mult)
        P.tt(t3[:, 0:KB_], bk[:, :, 0, :], h[:, 0:KB_, 1, :], ALU.mult)
        P.tt(t4[:, 0:KB_], bk[:, :, 1, :], h[:, 0:KB_, 0, :], ALU.mult)
        P.tt(Y_sb[:, 0, :, k0:k0 + KB_].rearrange("p c k -> p k c"), t1[:, 0:KB_], t2[:, 0:KB_], ALU.subtract,
             eng=POOL)
        P.tt(Y_sb[:, 1, :, k0:k0 + KB_].rearrange("p c k -> p k c"), t3[:, 0:KB_], t4[:, 0:KB_], ALU.add,
             eng=POOL)

    hy_fwd(P, T, u_sb, SD, A_sb, gti, ep)
    P.tt(us, u_sb, bcast_last(skip_bc[0:SD, :], 128), ALU.mult, eng=POOL)
    P_sb = AB[0:S, :].rearrange("p (n j c) -> p n j c", n=128, j=2)
    for c in range(C):
        bk = P.bank(c % 2)[0:S, 0:256]
        P.mm(bk, Y_sb[:, 0, c, :], T["CI"][:, 0, :], start=True, stop=False)
        P.mm(bk, Y_sb[:, 1, c, :], T["CI"][:, 1, :], start=False, stop=True)
        P.copy(P_sb[:, :, :, c], bk.rearrange("p (j n) -> p n j", j=2), eng=(DVE if c % 2 else ACT))
    NB = 512 // C
    for n0 in range(0, 128, NB):
        bk = P.bank(4 + (n0 // NB) % 2)[0:SD, 0:NB * C].rearrange("p (n c) -> p n c", n=NB)
        for nb in range(NB):
            n1 = n0 + nb
            P.mm(bk[:, nb, :], T["ET"][:, n1, 0, :], P_sb[:, n1, 0, :], start=True, stop=False)
            P.mm(bk[:, nb, :], T["ET"][:, n1, 1, :], P_sb[:, n1, 1, :], start=False, stop=True)
        P.tt(tz[0:SD], bk.rearrange("p n c -> p c n"), us[:, :, n0:n0 + NB], ALU.add)
        P.tt(z_sb[:, :, n0:n0 + NB], tz[0:SD], gate_sb[:, :, n0:n0 + NB], ALU.mult)


def hy_filter(P, consts, T, hid2T, w3sb, o, decF_d, decB_d, Hd, bufs, gti, N):
    S = T["S"]
    C = HC
    AB, filt, tmpf, dch, stage, red, rl, ab_full = bufs
    A_sb = AB[:, 0:S * 2 * C].rearrange("p (k j c) -> p k j c", k=S, j=2)
    NB = 512 // C
    hv = hid2T.rearrange("p (n2 n1) -> p n1 n2", n1=128)
    for n0 in range(0, 128, NB):
        bF = P.bank(6)[0:S, 0:NB * C].rearrange("p (n c) -> p n c", n=NB)
        bB = P.bank(7)[0:S, 0:NB * C].rearrange("p (n c) -> p n c", n=NB)
        dF, dB = dch[(n0 // NB) % 2]
        P.dma(dF, decF_d[:, n0:n0 + NB, :])
        P.dma(dB, decB_d[:, n0:n0 + NB, :])
        for nb in range(NB):
            P.mm(bF[:, nb, :], hv[:, n0 + nb, :], w3sb[:, 0, :])
        for nb in range(NB):
            P.mm(bB[:, nb, :], hv[:, n0 + nb, :], w3sb[:, 1, :])
        t1 = tmpf[0][0:S]
        t2 = tmpf[1][0:S]
        P.tt(t1, bF, dF, ALU.mult)
        P.tt(t2, bB, dB, ALU.mult)
        P.tt(filt[:, :, n0:n0 + NB].rearrange("p c n -> p n c"), t1, t2, ALU.add, eng=POOL)
    ab = ab_full[0:S]
    P.act(ab, filt, AF.Abs)
    P.S.add(DVE, lambda e: e.reduce_sum(out=red[0:S, 0:C], in_=ab, axis=AX.X), reads=[ab], writes=[red[0:S, 0:C]])
    rh = red[0:S, C:2 * C].bitcast(BF16)[:, 0:C]
    rlo = red[0:S, 2 * C:3 * C].bitcast(BF16)[:, 0:C]
    rt = red[0:S, 3 * C:4 * C]
    P.copy(rh, red[0:S, 0:C])
    P.tt(rt, red[0:S, 0:C], rh, ALU.subtract)
    P.copy(rlo, rt)
    bk = P.bank(5)[:, 0:C]
    P.mm(bk, consts["ones"][0:S, :], rh, start=True, stop=False)
    P.mm(bk, consts["ones"][0:S, :], rlo, start=False, stop=True)
    P.ts(rl, bk, float(N), None, ALU.mult)
    P.recip(rl, rl)

    def ep(k0, KB_, bk2):
        st = stage[:, (k0 // KB_) % 2 * KB_:(k0 // KB_) % 2 * KB_ + KB_]
        P.tt(st.rearrange("p k j c -> p (k j) c"), bk2.rearrange("p k j c -> p (k j) c"),
             bcast_mid(rl, KB_ * 2), ALU.mult)
        P.dma(Hd[:, k0:k0 + KB_], st)

    hy_fwd(P, T, filt, S, A_sb, gti, ep)


def hyena_phase(P, consts, td, S, SD, hyv_d, featsT_d, decF_d, decB_d, wd, skipbc_d, w3_d, out_d, NCH, tag):
    N = 128 * S
    C = HC
    NG = NCH // C
    KB = min(256 // C, S)
    NB = 512 // C
    mT = P.mark()
    T = hy_load_tables(P, td, S, SD)
    T["gts"] = [P.alloc([128, min(16, S), 3, 128], BF16) for _ in range(2)]
    gti = [0]
    AB = P.alloc([128, 128 * 2 * C], BF16)
    us = P.alloc([max(S, SD), C, 128], F32)
    w3 = P.alloc([64, 2, 2, NCH], BF16)
    P.dma(w3, w3_d, eng=POOL)
    skb = P.alloc([128, 2, NCH], F32)
    P.dma(skb, skipbc_d)
    Hd = P.dscr("Hd_" + tag, [2, NG, 128, S, 2, C])
    m1 = P.mark()
    hid2T = P.alloc([64, N], BF16)
    hy_mlp(P, consts, featsT_d, N, wd["w1"], wd["b1"], wd["w2"], wd["b2"], wd["fr"], hid2T)
    filt = P.alloc([S, C, 128], BF16)
    tmpf = [P.alloc([S, NB, C], F32) for _ in range(2)]
    dch = [(P.alloc([S, NB, C], F32), P.alloc([S, NB, C], F32)) for _ in range(2)]
    stage = P.alloc([128, 2 * KB, 2, C], F32)
    red = P.alloc([128, 4 * C], F32)
    rl = P.alloc([128, C], F32)
    for o in range(2):
        for g in range(NG):
            w3g = w3[:, o, :, g * C:(g + 1) * C]
            hy_filter(P, consts, T, hid2T, w3g, o, decF_d[g], decB_d[g], Hd[o, g],
                      (AB, filt, tmpf, dch, stage, red, rl, us), gti, N)
    P.release(m1)
    Y_sb = P.alloc([128, 2, C, S], BF16)
    hch = [P.alloc([128, KB, 2, C], F32) for _ in range(2)]
    tmps = [P.alloc([128, KB, C], F32) for _ in range(4)]
    tz = P.alloc([max(SD, 1), C, NB], F32)
    xs = [[P.alloc([SD, C, 128], BF16) for _ in range(3)] for _ in range(2)]
    z1 = P.alloc([SD, C, 128], BF16)
    oo = [P.alloc([SD, C, 128], BF16) for _ in range(2)]
    bufs = (AB, Y_sb, us[0:SD], hch, tmps, tz)
    for g in range(NG):
        xv = xs[g % 2]
        for s3 in range(3):
            P.dma(xv[s3], hyv_d[s3, g * C:(g + 1) * C, :].rearrange("c (a b) -> a c b", b=128))
        hy_conv(P, T, xv[0], Hd[0, g], xv[1], skb[:, 0, g * C:(g + 1) * C], z1, bufs, gti)
        ob = oo[g % 2]
        hy_conv(P, T, z1, Hd[1, g], xv[2], skb[:, 1, g * C:(g + 1) * C], ob, bufs, gti)
        P.dma(out_d[g * C:(g + 1) * C, :].rearrange("c (a b) -> a c b", b=128), ob)
    P.release(mT)
SEQ = 8192
SKV = SEQ + NCTX
HYCH = 256


def build_B():
    P = Prog()
    qT = P.din("qT", [1024, TOWN], BF16)
    kT = P.din("kT", [256, SKV], BF16)
    v = P.din("v", [SKV, 256], BF16)
    qcT = P.din("qcT", [1024, NCTX], BF16)
    kcT = P.din("kcT", [256, NCTX], BF16)
    vc = P.din("vc", [NCTX, 256], BF16)
    ones_d = P.din("ones", [128, 128], BF16)
    ident_d = P.din("ident", [128, 128], BF16)
    oT = P.dout("oT", [1024, TOWN], BF16)
    ocT = P.dout("ocT", [1024, NCTX], BF16)
    consts = load_consts(P, ones_d, None, ident_d)
    fin = make_finish_gqa(P)
    attention(P, consts, qT, kT, v, oT, 8, lambda h: h // 4, 1, TOWN, SKV, 128 ** -0.5, fin)
    attention(P, consts, qcT, kcT, vc, ocT, 8, lambda h: h // 4, 1, NCTX, NCTX, 128 ** -0.5, fin)
    wd = {"w1": P.din("hy_w1", [33, 64]), "b1": P.din("hy_b1", [64, 1]), "w2": P.din("hy_w2", [64, 64]),
          "b2": P.din("hy_b2", [64, 1]), "fr": P.din("hy_fr", [64, 2])}
    sk_d = P.din("hy_skipbc", [128, 2, HYCH])
    w3_d = P.din("hy_w3", [64, 2, 2, HYCH])
    for tag, n_tok in (("l", SEQ), ("c", NCTX)):
        S = 2 * n_tok // 128
        SD = n_tok // 128
        td = {"FA": P.din("FA" + tag, [S, 2 * S], BF16), "GT": P.din("GT" + tag, [128, S, 3, 128], BF16),
              "CI": P.din("CI" + tag, [128, 2, 256], BF16), "ET": P.din("ET" + tag, [S, 128, 2, SD], BF16)}
        hyv = P.din("hyv" + tag, [3, HYCH, n_tok], BF16)
        ft = P.din("featsT" + tag, [33, 2 * n_tok], BF16)
        dF = P.din("decF" + tag, [HYCH // HC, S, 128, HC])
        dB = P.din("decB" + tag, [HYCH // HC, S, 128, HC])
        out = P.dout("ohy" + tag, [HYCH, n_tok], BF16)
        hyena_phase(P, consts, td, S, SD, hyv, ft, dF, dB, wd, sk_d, w3_d, out, HYCH, tag)
    return P


def build_D():
    P = Prog()
    qT = P.din("qT", [D, TOWN], BF16)
    kT = P.din("kT", [D, SKV], BF16)
    v = P.din("v", [SKV, D], BF16)
    lam_d = P.din("lam", [128, 4, 64])
    sg_d = P.din("sg", [128, 128])
    ones_d = P.din("ones", [128, 128], BF16)
    ident_d = P.din("ident", [128, 128], BF16)
    oT = P.dout("oT", [D, TOWN], BF16)
    consts = load_consts(P, ones_d, None, ident_d)
    lambda_init = 0.8 - 0.6 * float(np.exp(-0.3 * 1))
    fin = make_finish_diff(P, consts, lam_d, sg_d, lambda_init)
    attention(P, consts, qT, kT, v, oT, 16, lambda h: h, 2, TOWN, SKV, 64 ** -0.5, fin)
    return P
DFF = 5632
FC = DFF // 128


def ffn_up(P, consts, h2T, Tl, with_ctx, w_up_d, cw, cb, em, actT_d):
    m0 = P.mark()
    tall = Tl + (NCTX if with_ctx else 0)
    UW = tall + 2
    Ua = [P.alloc([128, UW], F32) for _ in range(2)]
    Ug = [P.alloc([128, UW], F32) for _ in range(2)]
    Va = P.alloc([128, tall], F32)
    Vg = P.alloc([128, tall], F32)
    ab = [P.alloc([128, tall], BF16) for _ in range(2)]
    sg = P.alloc([128, tall], F32)
    for u in Ua + Ug:
        P.memset(u[:, Tl:Tl + 1], 0.0)
        P.memset(u[:, UW - 1:UW], 0.0)
    tiles = [(512 * i, 512 * (i + 1)) for i in range((Tl - 2) // 512)] + [(Tl - 2, Tl)]
    if with_ctx:
        tiles.append((Tl, tall))
    nt = len(tiles)
    order = []
    for j in range(FC):
        order += [j, j + FC]

    def conv(u, hc, V):
        P.ts(u[:, 0:1], u[:, 0:1], em[:, 0:1], None, ALU.mult)
        P.ts(u[:, Tl - 1:Tl], u[:, Tl - 1:Tl], em[:, 1:2], None, ALU.mult)
        P.act(V, u[:, 1:tall + 1], AF.Identity, bias=cb[:, hc:hc + 1], scale=cw[:, hc, 1:2])
        P.stt(V, u[:, 0:tall], cw[:, hc, 0:1], V, ALU.mult, ALU.add)
        P.stt(V, u[:, 2:tall + 2], cw[:, hc, 2:3], V, ALU.mult, ALU.add)

    def ep(n, ti, a, b, bk):
        j = n % FC
        u = (Ua if n < FC else Ug)[j % 2]
        if with_ctx and ti == nt - 1:
            P.act(u[:, Tl + 1:tall + 1], bk, AF.Copy)
        else:
            P.act(u[:, a:b], bk, AF.Copy)
        if ti == nt - 1:
            if n < FC:
                conv(u, n, Va)
            else:
                conv(u, n, Vg)
                P.act(sg, Vg, AF.Silu)
                o = ab[j % 2]
                P.tt(o, Va, sg, ALU.mult)
                P.dma(actT_d[j * 128:(j + 1) * 128, 0:Tl - 2], o[:, 0:Tl - 2])
                if with_ctx:
                    P.dma(actT_d[j * 128:(j + 1) * 128, Tl - 2:Tl - 2 + NCTX], o[:, Tl:Tl + NCTX])

    gemm(P, h2T, KC, w_up_d, order, tiles, ep)
    P.release(m0)


def ffn_down(P, consts, actT_d, ntok, w_down_d, yT_d):
    m0 = P.mark()
    BLK = 768
    av = actT_d.rearrange("(kc p) t -> p kc t", p=128)
    act = P.alloc([128, FC, BLK], BF16)
    wsl = [P.alloc([128, FC, 128], BF16) for _ in range(4)]
    st = [P.alloc([128, 512], F32) for _ in range(3)]
    si = [0]
    for b0 in range(0, ntok, BLK):
        bw = min(BLK, ntok - b0)
        P.dma(act[:, :, 0:bw], av[:, :, b0:b0 + bw])
        tiles = [(a, min(a + 512, bw)) for a in range(0, bw, 512)]

        def ep(n, ti, a, b, bk):
            s = st[si[0] % 3]
            si[0] += 1
            P.copy(s[:, 0:b - a], bk, eng=(ACT if si[0] % 2 else DVE))
            P.dma(yT_d[n * 128:(n + 1) * 128, b0 + a:b0 + b], s[:, 0:b - a])

        gemm(P, act, FC, w_down_d, list(range(KC)), tiles, ep, wslots=wsl)
    P.release(m0)


def build_post(layer):
    with_ctx = (layer == 0)
    P = Prog()
    Tl = TA
    tall = Tl + (NCTX if with_ctx else 0)
    ntok = TOWN + (NCTX if with_ctx else 0)
    oT = P.din("oT", [D, Tl], BF16)
    xT = P.din("xT", [D, Tl])
    modT = P.din("modT", [128, 192])
    ngs = P.din("ngs", [3, 128, KC])
    w_out = P.din("w_out", [D, D])
    w_up = P.din("w_up", [D, 2 * DFF])
    w_down = P.din("w_down", [DFF, D])
    fcw = P.din("fcw", [128, 2 * FC, 3])
    fcb = P.din("fcb", [128, 2 * FC])
    em_d = P.din("edge", [128, 2])
    ones_d = P.din("ones", [128, 128], BF16)
    rot_d = P.din("rot", [128, 128], BF16)
    if with_ctx:
        ocT = P.din("ocT", [D, NCTX], BF16)
        ctxT = P.din("ctxT", [D, NCTX])
    yT = P.dscr("yT", [D, tall])
    xmT = P.dscr("xmT", [D, tall])
    actT = P.dscr("actT", [DFF, ntok], BF16)
    y2T = P.dscr("y2T", [D, ntok])
    xoT = P.dout("xoT", [D, TOWN])
    if with_ctx:
        cxoT = P.dout("cxoT", [D, NCTX])

    consts = load_consts(P, ones_d, rot_d)
    modsb = P.alloc([128, 96, 2], F32)
    P.dma(modsb.rearrange("p k j -> p (k j)"), modT)
    gam = P.alloc([128, 3, KC], F32)
    P.dma(gam, ngs.rearrange("a p c -> p a c"))
    em = load_vec(P, em_d, [128, 2])
    cw = load_vec(P, fcw, [128, 2 * FC, 3])
    cb = load_vec(P, fcb, [128, 2 * FC])

    m1 = P.mark()
    osb = P.alloc([128, KC, tall], BF16)
    P.dma(osb[:, :, 0:Tl], oT.rearrange("(c p) t -> p c t", p=128))
    if with_ctx:
        P.dma(osb[:, :, Tl:tall], ocT.rearrange("(c p) t -> p c t", p=128))
    st = [P.alloc([128, 512], F32) for _ in range(3)]
    si = [0]
    tiles = [(512 * i, 512 * (i + 1)) for i in range(4)] + [(2048, Tl)] + ([(Tl, tall)] if with_ctx else [])

    def ep_out(n, ti, a, b, bk):
        s = st[si[0] % 3]
        si[0] += 1
        P.copy(s[:, 0:b - a], bk, eng=(ACT if si[0] % 2 else DVE))
        P.dma(yT[n * 128:(n + 1) * 128, a:b], s[:, 0:b - a])

    gemm(P, osb, KC, w_out, list(range(KC)), tiles, ep_out)
    P.release(m1)

    segs = [(xT, yT[:, 0:Tl], xmT[:, 0:Tl], Tl, 0)]
    if with_ctx:
        segs.append((ctxT, yT[:, Tl:tall], xmT[:, Tl:tall], NCTX, 1))
    resid(P, consts, segs, gam[:, 0, :], modsb, 2)

    m3 = P.mark()
    h2T = P.alloc([128, KC, tall], BF16)
    msegs = [(xmT[:, 0:Tl], 0, Tl, 0)]
    if with_ctx:
        msegs.append((xmT[:, Tl:tall], Tl, NCTX, 1))
    modnorm(P, consts, msegs, gam[:, 1, :], modsb, 4, 3, h2T)
    ffn_up(P, consts, h2T, Tl, with_ctx, w_up, cw, cb, em, actT)
    P.release(m3)
    ffn_down(P, consts, actT, ntok, w_down, y2T)
    segs = [(xmT[:, 1:1 + TOWN], y2T[:, 0:TOWN], xoT, TOWN, 0)]
    if with_ctx:
        segs.append((xmT[:, Tl:tall], y2T[:, TOWN:ntok], cxoT, NCTX, 1))
    resid(P, consts, segs, gam[:, 2, :], modsb, 5)
    if layer == 1:
        return P

    mod1 = P.din("modT1", [128, 192])
    ng10 = P.din("ng10", [128, KC])
    w_in1 = P.din("dc_w_in", [D, 6144])
    cosd = P.din("cos64", [128, TOWN])
    sind = P.din("sin64", [128, TOWN])
    q1T = P.dout("q1T", [D, TOWN], BF16)
    k1T = P.dout("k1T", [D, TOWN], BF16)
    v1T = P.dout("v1T", [D, TOWN], BF16)
    kc1T = P.dout("kc1T", [D, NCTX], BF16)
    vc1T = P.dout("vc1T", [D, NCTX], BF16)
    modsb1 = P.alloc([128, 96, 2], F32)
    P.dma(modsb1.rearrange("p k j -> p (k j)"), mod1)
    gam1 = load_vec(P, ng10, [128, KC])
    cos = load_vec(P, cosd, [128, TOWN])
    sin = load_vec(P, sind, [128, TOWN])
    t1 = TOWN + NCTX
    hT = P.alloc([128, KC, t1], BF16)
    modnorm(P, consts, [(xoT, 0, TOWN, 0), (cxoT, TOWN, NCTX, 1)], gam1, modsb1, 1, 0, hT)
    tmp = (P.alloc([128, 512], BF16), P.alloc([128, 512], BF16), P.alloc([128, 512], F32),
           P.alloc([128, 512], F32), P.alloc([128, 512], F32))
    osts = [P.alloc([128, 512], BF16) for _ in range(3)]
    oi = [0]
    lat_tiles = [(512 * i, 512 * (i + 1)) for i in range(4)]

    def ep1(n, ti, a, b, bk):
        w = b - a
        ost = osts[oi[0] % 3]
        oi[0] += 1
        is_ctx = (ti == 4)
        if n < 16:
            dst = q1T[n * 128:(n + 1) * 128]
        elif n < 32:
            dst = (kc1T if is_ctx else k1T)[(n - 16) * 128:(n - 15) * 128]
        else:
            dst = (vc1T if is_ctx else v1T)[(n - 32) * 128:(n - 31) * 128]
        if n >= 32 or is_ctx:
            P.act(ost[:, 0:w], bk, AF.Copy)
        else:
            qk_epilogue(P, consts, bk, w, None, cos[:, a:b], sin[:, a:b], ost[:, 0:w], tmp, 3 + ti % 2, 5 + ti % 2)
        if is_ctx:
            P.dma(dst[:, 0:w], ost[:, 0:w])
        else:
            P.dma(dst[:, a:b], ost[:, 0:w])

    gemm(P, hT, KC, w_in1, list(range(16)), lat_tiles, ep1)
    gemm(P, hT, KC, w_in1, list(range(16, 48)), lat_tiles + [(TOWN, t1)], ep1)
    return P
GRID_W = 64
ROPE_THETA = 10000.0


def pc(v, k=None):
    v = np.asarray(v, np.float32).reshape(-1, 128)
    return np.ascontiguousarray(v.T)


def rope_tables(hd, t0, n):
    t = np.arange(t0, t0 + n)
    row = (t // GRID_W).astype(np.float32)
    col = (t % GRID_W).astype(np.float32)
    axis_dim = hd // 2
    inv_freq = (ROPE_THETA ** (-np.arange(0, axis_dim, 2, dtype=np.float32) / np.float32(axis_dim))).astype(np.float32)
    ang = np.concatenate([row[:, None] * inv_freq[None, :], col[:, None] * inv_freq[None, :]], axis=-1).astype(np.float32)
    pair = (np.arange(128) % hd) // 2
    a = ang[:, pair].T
    return np.ascontiguousarray(np.cos(a).astype(np.float32)), np.ascontiguousarray(np.sin(a).astype(np.float32))


def rot_matrix():
    R = np.zeros((128, 128), np.float32)
    for i in range(64):
        R[2 * i + 1, 2 * i] = -1.0
        R[2 * i, 2 * i + 1] = 1.0
    return R.astype(NPBF)


def prep_A(inp, core):
    b, qtr = core // 4, core % 4
    t0 = qtr * TOWN
    x = inp["x"][b]
    xe = np.zeros((TA, D), np.float32)
    lo, hi = max(t0 - 1, 0), min(t0 + TOWN + 1, x.shape[0])
    xe[lo - (t0 - 1):hi - (t0 - 1)] = x[lo:hi]
    cT = np.stack([pc(inp["c"][b]), pc(inp["c_ctx"])], axis=-1)
    cos, sin = rope_tables(128, t0, TOWN)
    m = {
        "xT": np.ascontiguousarray(xe.T),
        "ctxT": np.ascontiguousarray(inp["ctx"][b].T),
        "cT": np.ascontiguousarray(cT),
        "ada_w": inp["ada_w"],
        "adab": np.stack([pc(inp["ada_b"][i]) for i in range(2)]),
        "ng0": pc(inp["norm_g"][0, 0]),
        "ab_w_in": inp["ab_w_in"][0],
        "qkg": np.ascontiguousarray(inp["ab_qk_g"][0].T),
        "cos128": cos, "sin128": sin,
        "ones": np.ones((128, 128), NPBF),
        "rot": rot_matrix(),
        "hy_cw": np.ascontiguousarray(inp["hy_conv_w"][0].reshape(3, 24, 128).transpose(2, 1, 0)),
        "hy_cb": pc(inp["hy_conv_b"][0]),
        "edge": np.tile(np.array([[1.0 if t0 > 0 else 0.0, 1.0 if t0 + TOWN < x.shape[0] else 0.0]], np.float32), (128, 1)),
    }
    return m
NCORES = 8
_PROGS = {}


def _prog(name, builder):
    if name not in _PROGS:
        _PROGS[name] = builder().finish()
    return _PROGS[name]


def _run(name, builder, maps):
    nc = _prog(name, builder)
    res = run_bass_kernel_spmd(nc, maps, core_ids=list(range(NCORES)))
    return [{k: np.asarray(v) for k, v in r.items()} for r in res.results]


def _pad_cols(a):
    z = np.zeros((a.shape[0], 1), a.dtype)
    return np.concatenate([z, a, z], axis=1)


def prep_B(inp, A, core, cache):
    b, qtr = core // 4, core % 4
    g = [A[4 * b + q] for q in range(4)]
    kT = np.concatenate([x["kT"] for x in g] + [g[0]["kcT"]], axis=1)
    vT = np.concatenate([x["vT"] for x in g] + [g[0]["vcT"]], axis=1)
    ch = slice(qtr * HYCH, (qtr + 1) * HYCH)
    hyl = np.stack([np.concatenate([x["hyT"][s * 1024:(s + 1) * 1024][ch] for x in g], axis=1) for s in range(3)])
    hyc = np.stack([g[0]["hycT"][s * 1024:(s + 1) * 1024][ch] for s in range(3)])
    m = {
        "qT": A[core]["qT"], "kT": np.ascontiguousarray(kT), "v": np.ascontiguousarray(vT.T),
        "qcT": g[0]["qcT"], "kcT": g[0]["kcT"], "vc": np.ascontiguousarray(g[0]["vcT"].T),
        "ones": np.ones((128, 128), NPBF), "ident": np.eye(128, dtype=np.float32).astype(NPBF),
        "hy_w1": inp["hy_w1"][0], "hy_b1": inp["hy_b1"][0][:, None], "hy_w2": inp["hy_w2"][0],
        "hy_b2": inp["hy_b2"][0][:, None], "hy_fr": np.ascontiguousarray(inp["hy_freq"][0].T),
        "hy_skipbc": np.ascontiguousarray(np.tile(inp["hy_skip"][0][None, :, ch], (128, 1, 1))),
        "hy_w3": np.ascontiguousarray(inp["hy_w3"][0].reshape(64, 2, 2, 1024)[:, :, :, ch]),
        "hyvl": np.ascontiguousarray(hyl), "hyvc": np.ascontiguousarray(hyc),
    }
    for tag, n_tok in (("l", SEQ), ("c", NCTX)):
        key = ("tab", tag)
        if key not in cache:
            cache[key] = hy_host_tables(2 * n_tok // 128, n_tok // 128)
        for k, v in cache[key].items():
            m[k + tag] = v
        key = ("filt", tag, qtr)
        if key not in cache:
            cache[key] = hy_host_filter_consts(n_tok, qtr * HYCH, HYCH)
        m["featsT" + tag], m["decF" + tag], m["decB" + tag] = cache[key]
    return m


def _post_common(inp, layer, core, o_full, x_full, modT):
    b, qtr = core // 4, core % 4
    t0 = qtr * TOWN
    return {
        "oT": np.ascontiguousarray(_pad_cols(o_full)[:, t0:t0 + TA]),
        "xT": np.ascontiguousarray(_pad_cols(x_full)[:, t0:t0 + TA]),
        "modT": modT,
        "ngs": np.stack([pc(inp["norm_g"][layer, i]) for i in (1, 2, 3)]),
        "w_out": (inp["ab_w_out"] if layer == 0 else inp["dc_w_out"])[0],
        "w_up": inp["ffn_w_up"][layer], "w_down": inp["ffn_w_down"][layer],
        "fcw": np.ascontiguousarray(inp["ffn_conv_w"][layer].reshape(3, 2 * FC, 128).transpose(2, 1, 0)),
        "fcb": pc(inp["ffn_conv_b"][layer]),
        "edge": np.tile(np.array([[1.0 if t0 > 0 else 0.0, 1.0 if t0 + TOWN < SEQ else 0.0]], np.float32), (128, 1)),
        "ones": np.ones((128, 128), NPBF), "rot": rot_matrix(),
    }


def kernel(**inp):
    inp = {k: np.asarray(v) for k, v in inp.items()}
    cache = {}
    A = _run("A", build_A, [prep_A(inp, c) for c in range(NCORES)])
    B = _run("B", build_B, [prep_B(inp, A, c, cache) for c in range(NCORES)])
    del cache
    mapsC = []
    for b in range(2):
        g = [B[4 * b + q] for q in range(4)]
        o_full = np.concatenate([np.concatenate([x["oT"] for x in g], axis=1),
                                 np.concatenate([x["ohyl"] for x in g], axis=0)], axis=0)
        oc = np.concatenate([g[0]["ocT"], np.concatenate([x["ohyc"] for x in g], axis=0)], axis=0)
        x_full = np.ascontiguousarray(inp["x"][b].T)
        for q in range(4):
            c = 4 * b + q
            m = _post_common(inp, 0, c, o_full, x_full, A[c]["modT"][0])
            cos, sin = rope_tables(64, q * TOWN, TOWN)
            m.update({"ocT": np.ascontiguousarray(oc), "ctxT": np.ascontiguousarray(inp["ctx"][b].T),
                      "modT1": A[c]["modT"][1], "ng10": pc(inp["norm_g"][1, 0]), "dc_w_in": inp["dc_w_in"][0],
                      "cos64": cos, "sin64": sin})
            mapsC.append(m)
    C = _run("C", lambda: build_post(0), mapsC)
    del mapsC, B
    mapsD = []
    for b in range(2):
        g = [C[4 * b + q] for q in range(4)]
        kT = np.ascontiguousarray(np.concatenate([x["k1T"] for x in g] + [g[0]["kc1T"]], axis=1))
        v = np.ascontiguousarray(np.concatenate([x["v1T"] for x in g] + [g[0]["vc1T"]], axis=1).T)
        for q in range(4):
            mapsD.append({"qT": g[q]["q1T"], "kT": kT, "v": v,
                          "lam": np.ascontiguousarray(np.tile(inp["dc_lambda"][0][None], (128, 1, 1))),
                          "sg": np.ascontiguousarray(np.tile(inp["dc_subln_g"][0][None], (128, 1))),
                          "ones": np.ones((128, 128), NPBF), "ident": np.eye(128, dtype=np.float32).astype(NPBF)})
    Dm = _run("D", build_D, mapsD)
    del mapsD
    mapsE = []
    for b in range(2):
        o_full = np.concatenate([Dm[4 * b + q]["oT"] for q in range(4)], axis=1)
        x_full = np.concatenate([C[4 * b + q]["xoT"] for q in range(4)], axis=1)
        for q in range(4):
            c = 4 * b + q
            mapsE.append(_post_common(inp, 1, c, o_full, x_full, A[c]["modT"][1]))
    E = _run("E", lambda: build_post(1), mapsE)
    out = np.empty((2, SEQ, D), np.float32)
    for c in range(NCORES):
        b, q = c // 4, c % 4
        out[b, q * TOWN:(q + 1) * TOWN, :] = E[c]["xoT"].T
    return out
```
